# Optimizing a Trainium2 kernel written in Bass

```python
import math
import jax, jax.numpy as jnp
from jax import lax
import numpy as np

D_MODEL = 1024
BATCH = 4
SEQ = 8192
DEPTH = 2

GRID_W = 64
CTX_LEN = 256
HEAD_DIM = 64
N_HEADS = D_MODEL // HEAD_DIM
NA_HEADS = N_HEADS // 2
GQA_Q_HEADS = N_HEADS - NA_HEADS
GQA_KV_HEADS = GQA_Q_HEADS // 4
WIN_R = 8
WIN_C = 16
DIFF_HEADS = N_HEADS // 2
DIFF_V_DIM = 2 * HEAD_DIM
D_FF = -(-8 * D_MODEL // (3 * 256)) * 256
Q_BLOCK = 128
ROPE_THETA = 10000.0
EPS = 1e-6
N_MOD = 6

NA_WIDTH = NA_HEADS * HEAD_DIM
GQA_Q_WIDTH = GQA_Q_HEADS * HEAD_DIM
GQA_KV_WIDTH = GQA_KV_HEADS * HEAD_DIM
PAR_IN = 3 * NA_WIDTH + GQA_Q_WIDTH + 2 * GQA_KV_WIDTH
PAR_SPLITS = (NA_WIDTH, 2 * NA_WIDTH, 3 * NA_WIDTH, 3 * NA_WIDTH + GQA_Q_WIDTH,
              3 * NA_WIDTH + GQA_Q_WIDTH + GQA_KV_WIDTH)
PAR_OUT = (NA_HEADS + GQA_Q_HEADS) * HEAD_DIM
DIFF_IN = 3 * DIFF_HEADS * 2 * HEAD_DIM
DIFF_OUT = DIFF_HEADS * DIFF_V_DIM

kernel_name = "hybrid_natten_gqa_diffattn_dit"


def rms_norm(x, gain=None):
    xf = x.astype(jnp.float32)
    y = xf * lax.rsqrt(jnp.mean(xf * xf, axis=-1, keepdims=True) + EPS)
    if gain is not None:
        y = y * gain.astype(jnp.float32)
    return y.astype(x.dtype)


def ada_params(cond, w, b):
    m = jax.nn.silu(cond) @ w + b
    return jnp.split(m[..., None, :], N_MOD, axis=-1)


def modulate(h, shift, scale):
    return h * (1.0 + scale) + shift


def axial_angles(n_tokens):
    t = jnp.arange(n_tokens)
    row = (t // GRID_W).astype(jnp.float32)
    col = (t % GRID_W).astype(jnp.float32)
    n_freq = HEAD_DIM // 4
    inv = ROPE_THETA ** (-jnp.arange(n_freq, dtype=jnp.float32) / n_freq)
    ang = jnp.concatenate([row[:, None] * inv, col[:, None] * inv], axis=-1)
    return jnp.cos(ang), jnp.sin(ang)


def apply_rope(x, cos, sin):
    xf = x.astype(jnp.float32).reshape(x.shape[:-1] + (HEAD_DIM // 2, 2))
    x1, x2 = xf[..., 0], xf[..., 1]
    out = jnp.stack([x1 * cos - x2 * sin, x1 * sin + x2 * cos], axis=-1)
    return out.reshape(x.shape).astype(x.dtype)


def to_heads(t, n_heads):
    b, l, _ = t.shape
    return t.reshape(b, l, n_heads, -1).transpose(0, 2, 1, 3)


def merge_heads(o):
    b, h, l, d = o.shape
    return o.transpose(0, 2, 1, 3).reshape(b, l, h * d)


def grouped_attention(q, k, v):
    b, hq, lq, dh = q.shape
    hkv = k.shape[1]
    qg = q.reshape(b, hkv, hq // hkv, lq, dh)
    s = jnp.einsum('bhgqd,bhkd->bhgqk', qg, k).astype(jnp.float32) * (dh ** -0.5)
    p = jax.nn.softmax(s, axis=-1).astype(v.dtype)
    return jnp.einsum('bhgqk,bhkd->bhgqd', p, v).reshape(b, hq, lq, v.shape[-1])


def sweep_queries(attend, *qs):
    b, h, l, _ = qs[0].shape
    nb = l // Q_BLOCK
    blocks = tuple(q.reshape(b, h, nb, Q_BLOCK, q.shape[-1]).transpose(2, 0, 1, 3, 4) for q in qs)
    out = lax.map(lambda blk: attend(*blk), blocks)
    return out.transpose(1, 2, 0, 3, 4).reshape(b, h, l, out.shape[-1])


def neighbourhood_attention(q, k, v, kc, vc, rpb):
    b, h, l, dh = q.shape
    rows = l // GRID_W
    kr = min(WIN_R, rows)
    scale = dh ** -0.5
    qg = q.reshape(b, h, rows, GRID_W, dh)
    kg = k.reshape(b, h, rows, GRID_W, dh)
    vg = v.reshape(b, h, rows, GRID_W, dh)
    cols = jnp.arange(GRID_W)
    col_start = jnp.clip(cols - WIN_C // 2, 0, GRID_W - WIN_C)
    col_idx = col_start[:, None] + jnp.arange(WIN_C)[None, :]
    col_bias = rpb[:, :, col_idx - cols[:, None] + (WIN_C - 1)]
    n_win = kr * WIN_C

    def row_block(r):
        rs = jnp.clip(r - kr // 2, 0, rows - kr)
        q_r = lax.dynamic_index_in_dim(qg, r, axis=2, keepdims=False)
        k_rows = lax.dynamic_slice_in_dim(kg, rs, kr, axis=2)
        v_rows = lax.dynamic_slice_in_dim(vg, rs, kr, axis=2)
        k_win = k_rows[:, :, :, col_idx]
        v_win = v_rows[:, :, :, col_idx]
        row_off = rs + jnp.arange(kr) - r + (WIN_R - 1)
        bias = jnp.take(col_bias, row_off, axis=1).transpose(0, 2, 1, 3)
        s_win = jnp.einsum('bhqd,bhrqjd->bhqrj', q_r, k_win).astype(jnp.float32) * scale + bias[None].astype(jnp.float32)
        s_ctx = jnp.einsum('bhqd,bhkd->bhqk', q_r, kc).astype(jnp.float32) * scale
        p = jax.nn.softmax(jnp.concatenate([s_win.reshape(b, h, GRID_W, n_win), s_ctx], axis=-1), axis=-1)
        p = p.astype(v.dtype)
        p_win = p[..., :n_win].reshape(b, h, GRID_W, kr, WIN_C)
        p_ctx = p[..., n_win:]
        return (jnp.einsum('bhqrj,bhrqjd->bhqd', p_win, v_win)
                + jnp.einsum('bhqk,bhkd->bhqd', p_ctx, vc))

    out = lax.map(row_block, jnp.arange(rows))
    return out.transpose(1, 2, 0, 3, 4).reshape(b, h, l, dh)


def parallel_mixer(h, hc, w_in, w_out, rpb, q_gain, k_gain, cos, sin, need_ctx):
    def split(p):
        nq, nk, nv, gq, gk, gv = jnp.split(p, PAR_SPLITS, axis=-1)
        return (to_heads(nq, NA_HEADS), to_heads(nk, NA_HEADS), to_heads(nv, NA_HEADS),
                rms_norm(to_heads(gq, GQA_Q_HEADS), q_gain), rms_norm(to_heads(gk, GQA_KV_HEADS), k_gain),
                to_heads(gv, GQA_KV_HEADS))
    nq, nk, nv, gq, gk, gv = split(h @ w_in)
    cnq, cnk, cnv, cgq, cgk, cgv = split(hc @ w_in)
    gq = apply_rope(gq, cos, sin)
    gk = apply_rope(gk, cos, sin)
    out_na = neighbourhood_attention(nq, nk, nv, cnk, cnv, rpb)
    k_all = jnp.concatenate([cgk, gk], axis=2)
    v_all = jnp.concatenate([cgv, gv], axis=2)
    out_gqa = sweep_queries(lambda qb: grouped_attention(qb, k_all, v_all), gq)
    y = merge_heads(jnp.concatenate([out_na, out_gqa], axis=1)) @ w_out
    yc = None
    if need_ctx:
        yc_na = grouped_attention(cnq, cnk, cnv)
        yc_g = grouped_attention(cgq, cgk, cgv)
        yc = merge_heads(jnp.concatenate([yc_na, yc_g], axis=1)) @ w_out
    return y, yc


def diff_mixer(h, hc, w_in, w_out, lq1, lk1, lq2, lk2, subln_gain, lambda_init, cos, sin, need_ctx):
    def split(p):
        b, l, _ = p.shape
        q, k, v = jnp.split(p, 3, axis=-1)
        q = q.reshape(b, l, DIFF_HEADS, 2, HEAD_DIM).transpose(0, 2, 3, 1, 4)
        k = k.reshape(b, l, DIFF_HEADS, 2, HEAD_DIM).transpose(0, 2, 3, 1, 4)
        v = v.reshape(b, l, DIFF_HEADS, DIFF_V_DIM).transpose(0, 2, 1, 3)
        return q[:, :, 0], q[:, :, 1], k[:, :, 0], k[:, :, 1], v
    lam = (jnp.exp(jnp.sum(lq1.astype(jnp.float32) * lk1.astype(jnp.float32)))
           - jnp.exp(jnp.sum(lq2.astype(jnp.float32) * lk2.astype(jnp.float32))) + lambda_init)
    scale = HEAD_DIM ** -0.5

    def attend(a1, a2, k1, k2, v):
        s1 = jnp.einsum('bhqd,bhkd->bhqk', a1, k1).astype(jnp.float32) * scale
        s2 = jnp.einsum('bhqd,bhkd->bhqk', a2, k2).astype(jnp.float32) * scale
        p = jax.nn.softmax(s1, axis=-1) - lam * jax.nn.softmax(s2, axis=-1)
        return jnp.einsum('bhqk,bhkd->bhqd', p.astype(v.dtype), v)

    def finish(o):
        return merge_heads(rms_norm(o, subln_gain) * (1.0 - lambda_init)) @ w_out

    q1, q2, k1, k2, v = split(h @ w_in)
    cq1, cq2, ck1, ck2, cv = split(hc @ w_in)
    q1, q2, k1, k2 = (apply_rope(t, cos, sin) for t in (q1, q2, k1, k2))
    k1a = jnp.concatenate([ck1, k1], axis=2)
    k2a = jnp.concatenate([ck2, k2], axis=2)
    va = jnp.concatenate([cv, v], axis=2)
    y = finish(sweep_queries(lambda a1, a2: attend(a1, a2, k1a, k2a, va), q1, q2))
    yc = None
    if need_ctx:
        yc = finish(attend(cq1, cq2, ck1, ck2, cv))
    return y, yc


def swiglu(h, w_gate, w_up, w_down):
    return (jax.nn.silu(h @ w_gate) * (h @ w_up)) @ w_down


def setup_inputs(seed: int = 0) -> dict:
    key = jax.random.key(seed)
    ks = jax.random.split(key, 22)
    n_par = (DEPTH + 1) // 2
    n_diff = DEPTH // 2
    nrm = lambda k, shape: jax.random.normal(k, shape, jnp.float32)
    w = lambda k, shape, fan_in: nrm(k, shape) * fan_in ** -0.5
    gain = lambda k, shape: 1.0 + 0.01 * nrm(k, shape)
    return {
        "x": nrm(ks[0], (BATCH, SEQ, D_MODEL)),
        "c": nrm(ks[1], (BATCH, D_MODEL)),
        "ctx": nrm(ks[2], (BATCH, CTX_LEN, D_MODEL)),
        "c_ctx": nrm(ks[3], (D_MODEL,)),
        "ada_w": 0.5 * w(ks[4], (DEPTH, D_MODEL, N_MOD * D_MODEL), D_MODEL),
        "ada_b": 0.01 * nrm(ks[5], (DEPTH, N_MOD * D_MODEL)),
        "ffn_w_gate": w(ks[6], (DEPTH, D_MODEL, D_FF), D_MODEL),
        "ffn_w_up": w(ks[7], (DEPTH, D_MODEL, D_FF), D_MODEL),
        "ffn_w_down": w(ks[8], (DEPTH, D_FF, D_MODEL), D_FF),
        "par_w_in": w(ks[9], (n_par, D_MODEL, PAR_IN), D_MODEL),
        "par_w_out": w(ks[10], (n_par, PAR_OUT, D_MODEL), PAR_OUT),
        "na_rpb": 0.02 * nrm(ks[11], (n_par, NA_HEADS, 2 * WIN_R - 1, 2 * WIN_C - 1)),
        "gqa_q_gain": gain(ks[12], (n_par, HEAD_DIM)),
        "gqa_k_gain": gain(ks[13], (n_par, HEAD_DIM)),
        "diff_w_in": w(ks[14], (n_diff, D_MODEL, DIFF_IN), D_MODEL),
        "diff_w_out": w(ks[15], (n_diff, DIFF_OUT, D_MODEL), DIFF_OUT),
        "diff_lambda_q1": 0.1 * nrm(ks[16], (n_diff, HEAD_DIM)),
        "diff_lambda_k1": 0.1 * nrm(ks[17], (n_diff, HEAD_DIM)),
        "diff_lambda_q2": 0.1 * nrm(ks[18], (n_diff, HEAD_DIM)),
        "diff_lambda_k2": 0.1 * nrm(ks[19], (n_diff, HEAD_DIM)),
        "diff_subln_gain": gain(ks[20], (n_diff, DIFF_V_DIM)),
        "final_norm_gain": gain(ks[21], (D_MODEL,)),
    }


def reference(x, c, ctx, c_ctx, ada_w, ada_b, ffn_w_gate, ffn_w_up, ffn_w_down,
              par_w_in, par_w_out, na_rpb, gqa_q_gain, gqa_k_gain,
              diff_w_in, diff_w_out, diff_lambda_q1, diff_lambda_k1, diff_lambda_q2, diff_lambda_k2,
              diff_subln_gain, final_norm_gain):
    cos, sin = axial_angles(x.shape[1])
    xc = ctx
    for l in range(DEPTH):
        need_ctx = l < DEPTH - 1
        sh1, sc1, g1, sh2, sc2, g2 = ada_params(c, ada_w[l], ada_b[l])
        csh1, csc1, cg1, csh2, csc2, cg2 = ada_params(c_ctx, ada_w[l], ada_b[l])
        h = modulate(rms_norm(x), sh1, sc1)
        hc = modulate(rms_norm(xc), csh1, csc1)
        i = l // 2
        if l % 2 == 0:
            y, yc = parallel_mixer(h, hc, par_w_in[i], par_w_out[i], na_rpb[i], gqa_q_gain[i], gqa_k_gain[i],
                                   cos, sin, need_ctx)
        else:
            lambda_init = 0.8 - 0.6 * math.exp(-0.3 * l)
            y, yc = diff_mixer(h, hc, diff_w_in[i], diff_w_out[i], diff_lambda_q1[i], diff_lambda_k1[i],
                               diff_lambda_q2[i], diff_lambda_k2[i], diff_subln_gain[i], lambda_init,
                               cos, sin, need_ctx)
        x = x + g1 * y
        x = x + g2 * swiglu(modulate(rms_norm(x), sh2, sc2), ffn_w_gate[l], ffn_w_up[l], ffn_w_down[l])
        if need_ctx:
            xc = xc + cg1 * yc
            xc = xc + cg2 * swiglu(modulate(rms_norm(xc), csh2, csc2), ffn_w_gate[l], ffn_w_up[l], ffn_w_down[l])
    return rms_norm(x, final_norm_gain)
```

```python
import math
import numpy as np
import ml_dtypes
from contextlib import ExitStack
import concourse.bass as bass
import concourse.mybir as mybir
from concourse.bass_utils import run_bass_kernel_spmd

F32 = mybir.dt.float32
BF16 = mybir.dt.bfloat16
U8 = mybir.dt.uint8
AF = mybir.ActivationFunctionType
ALU = mybir.AluOpType
AX = mybir.AxisListType

ENGS = ("pe", "act", "dve", "pool", "sp")
EPOCH = 2000
GRID_W = 64
EPS = 1e-6
NEG = -30000.0

FULL_CFG = dict(D=1024, ROWS=128, CTX=256, NA=8, GQ=8, GKV=2, DH=8, FF=2816)


class Buf:
    __slots__ = ("name", "w", "r", "dsem")

    def __init__(self, name, dsem=None):
        self.name = name
        self.w = {}
        self.r = {}
        self.dsem = dsem


class DmaSem:
    __slots__ = ("h", "count")

    def __init__(self, h):
        self.h = h
        self.count = 0


class _Rec:
    def __getattr__(self, name):
        def f(*a, **kw):
            self.call = (name, a, kw)
            return self
        return f


class K:
    def __init__(self, nc, stack):
        self.nc = nc
        self.stack = stack
        self.ops = {e: [] for e in ENGS}
        self.instr = {e: [] for e in ENGS}
        self.known = {e: {} for e in ENGS}
        self.dsems = []
        self.dpool = {}
        self.nsem = 0

    def sem(self, name):
        h = self.stack.enter_context(self.nc.semaphore(name))
        self.nsem += 1
        return h

    def dsem(self, name):
        d = DmaSem(self.sem(name))
        self.dsems.append(d)
        return d

    def dbuf(self, name):
        if name not in self.dpool:
            self.dpool[name] = self.dsem("d_" + name)
        return Buf(name, dsem=self.dpool[name])

    def fence(self, buf):
        self._merge(buf.r, buf.w)

    @staticmethod
    def _merge(deps, d):
        for s, v in d.items():
            if deps.get(s, (None, 0))[1] < v[1]:
                deps[s] = v

    def _deps(self, reads, writes):
        deps = {}
        for b in reads:
            self._merge(deps, b.w)
        for b in writes:
            if b.r:
                self._merge(deps, b.r)
                self._merge(deps, b.w)
        return deps

    def _post(self, reads, writes, key, val):
        for b in writes:
            if b.r:
                b.w = {}
                b.r = {}
            if b.w.get(key, (None, 0))[1] < val[1]:
                b.w[key] = val
        for b in reads:
            if b in writes:
                continue
            if b.r.get(key, (None, 0))[1] < val[1]:
                b.r[key] = val

    def _waits(self, eng, deps):
        ws = []
        kn = self.known[eng]
        for key, (payload, v) in deps.items():
            if kn.get(key, 0) >= v:
                continue
            kn[key] = v
            ws.append((key, payload, v))
            if key[0] == "E":
                self.instr[payload][v - 1]["needed"] = True
        return ws

    @staticmethod
    def _bind(fn):
        rec = _Rec()
        fn(rec)
        return rec.call

    def op(self, eng, fn, reads=(), writes=()):
        deps = self._deps(reads, writes)
        ws = self._waits(eng, deps)
        r = {"call": self._bind(fn), "waits": ws, "needed": False, "dma": None}
        self.ops[eng].append(r)
        self.instr[eng].append(r)
        self._post(reads, writes, ("E", eng), (eng, len(self.instr[eng])))

    def dma(self, eng, out_ap, in_ap, reads=(), writes=()):
        ds = None
        for b in list(writes) + list(reads):
            if b.dsem is not None:
                ds = b.dsem
                break
        assert ds is not None
        deps = self._deps(reads, writes)
        ws = self._waits(eng, deps)
        ds.count += 16
        self.ops[eng].append({"call": ("dma_start", (), {"out": out_ap, "in_": in_ap}), "waits": ws, "needed": False, "dma": ds})
        self._post(reads, writes, ("D", id(ds)), (ds, ds.count))

    def barrier(self):
        evs = {}
        for e in ENGS:
            if self.instr[e]:
                evs[("E", e)] = (e, len(self.instr[e]))
        for d in self.dsems:
            if d.count > 0:
                evs[("D", id(d))] = (d, d.count)
        for e in ENGS:
            ws = self._waits(e, evs)
            if ws:
                self.ops[e].append({"call": None, "waits": ws, "needed": False, "dma": None})

    def emit(self):
        nc = self.nc
        esems = {}
        for e in ENGS:
            c = 0
            for r in self.instr[e]:
                if r["needed"]:
                    c += 1
                    r["count"] = c
            esems[e] = [self.sem(f"e_{e}_{j}") for j in range((c + EPOCH - 1) // EPOCH)]
        self.ncounts = {e: sum(1 for r in self.instr[e] if r["needed"]) for e in ENGS}

        def resolve(w):
            key, payload, v = w
            if key[0] == "D":
                return payload.h, v
            c = self.instr[payload][v - 1]["count"]
            return esems[payload][(c - 1) // EPOCH], (c - 1) % EPOCH + 1

        with nc.Block() as block:
            def run(e, eng):
                for r in self.ops[eng]:
                    for w in r["waits"]:
                        h, v = resolve(w)
                        e.wait_ge(h, v)
                    if r["call"] is None:
                        continue
                    name, a, kw = r["call"]
                    ins = getattr(e, name)(*a, **kw)
                    if r["dma"] is not None:
                        ins.then_inc(r["dma"].h, 16)
                    elif r["needed"]:
                        c = r["count"]
                        ins.then_inc(esems[eng][(c - 1) // EPOCH], 1)

            @block.tensor
            def _(e):
                run(e, "pe")

            @block.scalar
            def _(e):
                run(e, "act")

            @block.vector
            def _(e):
                run(e, "dve")

            @block.gpsimd
            def _(e):
                run(e, "pool")

            @block.sync
            def _(e):
                run(e, "sp")


class Arena:
    def __init__(self, ap, size):
        self.ap = ap
        self.size = size
        self.off = 0

    def alloc(self, shape, dt):
        esz = 4 if dt == F32 else 2
        n = int(np.prod(shape)) * esz
        n_al = (n + 63) // 64 * 64
        assert self.off + n_al <= self.size, f"arena overflow {self.off}+{n_al}>{self.size}"
        a = self.ap[:, self.off:self.off + n].bitcast(dt)
        self.off += n_al
        if len(shape) == 2:
            a = a.rearrange("p (a b) -> p a b", a=shape[0])
        elif len(shape) == 3:
            a = a.rearrange("p (a b c) -> p a b c", a=shape[0], b=shape[1])
        return a


def build_program(cfg, debug=False, split=True):
    D = cfg["D"]; KC = D // 128; ROWS = cfg["ROWS"]; T = ROWS * GRID_W; CTX = cfg["CTX"]
    NA = cfg["NA"]; GQ = cfg["GQ"]; GKV = cfg["GKV"]; DH = cfg["DH"]; FF = cfg["FF"]; FC = FF // 128
    NKEY = CTX + T; NCH = NKEY // 128; CT = CTX // 128; NT = T // 128
    NAP = NA // 2; GQP = GQ // 2
    W0 = (3 * NA + GQ + 2 * GKV) * 64
    W1 = 3 * DH * 128
    NCLS = 5
    lam_init = 0.8 - 0.6 * math.exp(-0.3 * 1)

    nc = bass.Bass("TRN2", target_bir_lowering=False)

    def din(name, shape, dt=F32):
        return nc.dram_tensor(name, list(shape), dt, kind="ExternalInput").ap()

    def dscr(name, shape, dt):
        return nc.dram_tensor(name, list(shape), dt, kind="ExternalOutput" if debug else "Internal").ap()

    x_in = din("x", [T, D]); ctx_in = din("ctx", [CTX, D])
    cvec = din("cvec", [128, 2 * KC])
    ada_w = din("ada_w", [2, D, 6 * D]); ada_b = din("ada_b", [2, 6 * D])
    w_gate = din("ffn_w_gate", [2, D, FF]); w_up = din("ffn_w_up", [2, D, FF]); w_down = din("ffn_w_down", [2, FF, D])
    w_in0 = din("par_w_in", [D, W0]); w_out0 = din("par_w_out", [D, D])
    w_in1 = din("diff_w_in", [D, W1]); w_out1 = din("diff_w_out", [D, D])
    qgain = din("gqa_q_gain", [1, 64]); kgain = din("gqa_k_gain", [1, 64])
    lamv = din("lamv", [1, 256]); subln = din("diff_subln_gain", [1, 128]); fgain = din("final_norm_gain", [1, D])
    cossin = din("cossin", [NKEY, 64])
    nabias = din("nabias", [NAP, 128, NCLS * 2 * 5 * 128])
    ident_in = din("ident", [128, 128], BF16)
    sel_in = din("sel", [128, 2])
    TO = T // 2 if split else T
    out_d = nc.dram_tensor("out", [TO, D], F32, kind="ExternalOutput").ap()

    XM = dscr("XM", [NKEY, D], F32); X1 = dscr("X1", [NKEY, D], F32)
    AO = dscr("AO", [NKEY, D], BF16)
    NAQT = dscr("NAQT", [NAP, 128, NKEY], BF16); NAKT = dscr("NAKT", [NAP, 128, NKEY], BF16)
    NAV = dscr("NAV", [NAP, 128, NCH * 130], BF16)
    GQT = dscr("GQT", [GQP, 128, NKEY], BF16); GKT = dscr("GKT", [GKV, 128, NKEY], BF16)
    GV = dscr("GV", [GKV, 128, NCH * 65], BF16)
    DQT = dscr("DQT", [DH, 128, NKEY], BF16); DKT = dscr("DKT", [DH, 128, NKEY], BF16)
    DV = dscr("DV", [DH, 128, NCH * 129], BF16)

    with ExitStack() as st:
        k = K(nc, st)
        ARENA = 204 * 1024
        arena_t = st.enter_context(nc.sbuf_tensor("arena", [128, ARENA], U8))
        ar = Arena(arena_t[:, :], ARENA)
        ps = st.enter_context(nc.psum_tensor("ps", [128, 4096], F32))

        def bank(i, n=1):
            return ps[:, i * 512:(i + n) * 512]

        PB = [Buf(f"psb{i}") for i in range(8)]

        ident = ar.alloc([128], BF16); B_ident = k.dbuf("ident")
        modrows = ar.alloc([6 * D], F32); B_mod = Buf("mod")
        csil = ar.alloc([2 * KC], F32); B_csil = k.dbuf("csil")
        ones_f = ar.alloc([128], F32); B_ones = Buf("ones")
        qg_r = ar.alloc([64], F32); kg_r = ar.alloc([64], F32); B_gains = k.dbuf("gains")
        lam_r = ar.alloc([256], F32); sub_r = ar.alloc([128], F32); fg_r = ar.alloc([D], F32)
        lam_s = ar.alloc([8], F32); B_lam = Buf("lam")
        sel = ar.alloc([2], F32)
        PERS = ar.off

        k.dma("sp", ident, ident_in, writes=[B_ident])
        k.dma("sp", csil, cvec, writes=[B_csil])
        k.dma("sp", qg_r, qgain.partition_broadcast(128), writes=[B_gains])
        k.dma("sp", kg_r, kgain.partition_broadcast(128), writes=[B_gains])
        k.dma("sp", lam_r, lamv.partition_broadcast(128), writes=[B_gains])
        k.dma("sp", sub_r, subln.partition_broadcast(128), writes=[B_gains])
        k.dma("sp", fg_r, fgain.partition_broadcast(128), writes=[B_gains])
        k.dma("sp", sel, sel_in, writes=[B_gains])
        k.op("dve", lambda e: e.memset(ones_f, 1.0), writes=[B_ones])
        k.op("act", lambda e: e.activation(out=csil, in_=csil, func=AF.Silu), reads=[B_csil], writes=[B_csil])
        lamtmp = ar.alloc([128], F32)
        PERS = ar.off
        k.op("dve", lambda e: e.tensor_tensor(out=lamtmp[:, 0:64], in0=lam_r[:, 0:64], in1=lam_r[:, 64:128], op=ALU.mult), reads=[B_gains], writes=[B_lam])
        k.op("dve", lambda e: e.tensor_tensor(out=lamtmp[:, 64:128], in0=lam_r[:, 128:192], in1=lam_r[:, 192:256], op=ALU.mult), reads=[B_gains], writes=[B_lam])
        k.op("dve", lambda e: e.tensor_reduce(out=lam_s[:, 0:2], in_=lamtmp.rearrange("p (a b) -> p a b", a=2), axis=AX.X, op=ALU.add), reads=[B_lam], writes=[B_lam])
        k.op("act", lambda e: e.activation(out=lam_s[:, 2:4], in_=lam_s[:, 0:2], func=AF.Exp), reads=[B_lam], writes=[B_lam])
        k.op("dve", lambda e: e.tensor_tensor(out=lam_s[:, 4:5], in0=lam_s[:, 3:4], in1=lam_s[:, 2:3], op=ALU.subtract), reads=[B_lam], writes=[B_lam])
        k.op("dve", lambda e: e.tensor_scalar(out=lam_s[:, 5:6], in0=lam_s[:, 4:5], scalar1=-lam_init, scalar2=None, op0=ALU.add), reads=[B_lam], writes=[B_lam])
        neglam = lam_s[:, 5:6]
        k.op("dve", lambda e: e.tensor_scalar(out=sub_r, in0=sub_r, scalar1=(1.0 - lam_init), scalar2=None, op0=ALU.mult), reads=[B_gains], writes=[B_gains])

        def cast_copy(i, out, in_, reads, writes):
            eng = ("pool", "dve", "act")[i % 3] if True else "dve"
            if eng == "act":
                k.op("act", lambda e: e.activation(out=out, in_=in_, func=AF.Copy), reads=reads, writes=writes)
            else:
                k.op(eng, lambda e: e.tensor_copy(out=out, in_=in_), reads=reads, writes=writes)

        def load_w(dst, src, kcs, n, stg, B_stg, B_dst):
            for kc in range(kcs):
                j = kc % 2
                k.dma("sp", stg[j][:, 0:n], src[kc * 128:(kc + 1) * 128, :], writes=[B_stg[j]])
                cast_copy(kc, dst[:, kc, :], stg[j][:, 0:n], [B_stg[j]], [B_dst])

        def ada_phase(l, cond, ncols):
            ar.off = PERS
            rep = ar.alloc([KC, 128], F32); B_rep = Buf("rep")
            wst = [ar.alloc([KC, 512], F32) for _ in range(2)]; B_wst = [k.dbuf(f"adaw{j}") for j in range(2)]
            bst = [ar.alloc([512], F32) for _ in range(2)]; B_bst = [k.dbuf(f"adab{j}") for j in range(2)]
            for kc in range(KC):
                k.op("dve", lambda e, kc=kc: e.tensor_scalar(out=rep[:, kc, :], in0=ones_f, scalar1=csil[:, cond * KC + kc:cond * KC + kc + 1], scalar2=None, op0=ALU.mult),
                     reads=[B_ones, B_csil], writes=[B_rep])
            nb = ncols // 512
            for n in range(nb):
                j = n % 2
                k.dma("sp", wst[j], ada_w[l, :, n * 512:(n + 1) * 512].rearrange("(a p) n -> p a n", p=128), writes=[B_wst[j]])
                k.dma("sp", bst[j], ada_b[l:l + 1, n * 512:(n + 1) * 512].partition_broadcast(128), writes=[B_bst[j]])
                pb = n % 2
                for kc in range(KC):
                    k.op("pe", lambda e, kc=kc, j=j, pb=pb: e.matmul(bank(pb), lhsT=rep[:, kc, :], rhs=wst[j][:, kc, :], start=(kc == 0), stop=(kc == KC - 1)),
                         reads=[B_rep, B_wst[j]], writes=[PB[pb]])
                k.op("dve", lambda e, n=n, j=j, pb=pb: e.tensor_tensor(out=modrows[:, n * 512:(n + 1) * 512], in0=bank(pb), in1=bst[j], op=ALU.add),
                     reads=[PB[pb], B_bst[j]], writes=[B_mod])
            for off in (D, 4 * D):
                if off < ncols:
                    k.op("dve", lambda e, off=off: e.tensor_scalar(out=modrows[:, off:off + D], in0=modrows[:, off:off + D], scalar1=1.0, scalar2=None, op0=ALU.add),
                         reads=[B_mod], writes=[B_mod])
            k.barrier()

        def rstd_chain(ss, n_inv, nrm_bufs):
            B = nrm_bufs
            w = ss.shape[1] // 3
            k.op("dve", lambda e: e.tensor_scalar(out=ss[:, w:2 * w], in0=ss[:, 0:w], scalar1=n_inv, scalar2=EPS, op0=ALU.mult, op1=ALU.add), reads=[B], writes=[B])
            k.op("act", lambda e: e.activation(out=ss[:, w:2 * w], in_=ss[:, w:2 * w], func=AF.Sqrt), reads=[B], writes=[B])
            k.op("dve", lambda e: e.reciprocal(out=ss[:, 2 * w:3 * w], in_=ss[:, w:2 * w]), reads=[B], writes=[B])

        def norm_mod_T(xt, B_x, sc_off, sh_off, junk, B_junk, ss, B_ss, tmp, B_tmp, hb, B_hb, hT_dst, B_hT, pbank):
            k.op("act", lambda e: e.activation(out=junk, in_=xt, func=AF.Square, accum_out=ss[:, 0:1]), reads=[B_x], writes=[B_junk, B_ss])
            rstd_chain(ss, 1.0 / D, B_ss)
            k.op("dve", lambda e: e.scalar_tensor_tensor(out=tmp, in0=xt, scalar=ss[:, 2:3], in1=modrows[:, sc_off:sc_off + D], op0=ALU.mult, op1=ALU.mult),
                 reads=[B_x, B_ss, B_mod], writes=[B_tmp])
            k.op("pool", lambda e: e.tensor_tensor(out=hb, in0=tmp, in1=modrows[:, sh_off:sh_off + D], op=ALU.add), reads=[B_tmp, B_mod], writes=[B_hb])
            pt = bank(pbank).bitcast(BF16)
            for kc in range(KC):
                k.op("pe", lambda e, kc=kc: e.transpose(out=pt[:, kc * 128:(kc + 1) * 128], in_=hb[:, kc * 128:(kc + 1) * 128], identity=ident),
                     reads=[B_hb, B_ident], writes=[PB[pbank]])
            k.op("act", lambda e: e.activation(out=hT_dst, in_=pt[:, 0:KC * 128].rearrange("p (a b) -> p a b", a=KC), func=AF.Copy), reads=[PB[pbank]], writes=[B_hT])

        def src_tile(l, i):
            if l == 0:
                return ctx_in[i * 128:(i + 1) * 128, :] if i < CT else x_in[(i - CT) * 128:(i - CT + 1) * 128, :]
            return X1[i * 128:(i + 1) * 128, :]

        def proj_phase(l, tiles):
            ar.off = PERS
            WW = W0 if l == 0 else W1
            w_src = w_in0 if l == 0 else w_in1
            wsb = ar.alloc([KC, WW], BF16); B_w = Buf("w_in")
            mark = ar.off
            stg = [ar.alloc([WW], F32) for _ in range(2)]; B_stg = [k.dbuf(f"wstg{j}") for j in range(2)]
            load_w(wsb, w_src, KC, WW, stg, B_stg, B_w)
            k.barrier()
            ar.off = mark
            xt = [ar.alloc([D], F32) for _ in range(2)]; B_x = [k.dbuf(f"px{j}") for j in range(2)]
            cst = [ar.alloc([64], F32) for _ in range(2)]; B_cs = [k.dbuf(f"pcs{j}") for j in range(2)]
            junk = ar.alloc([D], F32); B_junk = Buf("junk")
            ss = [ar.alloc([3], F32) for _ in range(2)]; B_ss = [Buf(f"ss{j}") for j in range(2)]
            tmp = ar.alloc([D], F32); B_tmp = Buf("tmp")
            hb = [ar.alloc([D], BF16) for _ in range(2)]; B_hb = [Buf(f"hb{j}") for j in range(2)]
            hT = [ar.alloc([KC, 128], BF16) for _ in range(2)]; B_hT = [Buf(f"hT{j}") for j in range(2)]
            sq = [ar.alloc([512], F32) for _ in range(2)]; B_sq = [Buf(f"sq{j}") for j in range(2)]
            qn = [ar.alloc([512], F32) for _ in range(2)]; B_qn = [Buf(f"qn{j}") for j in range(2)]
            ra = [ar.alloc([512], F32) for _ in range(2)]; B_ra = [Buf(f"ra{j}") for j in range(2)]
            rb = [ar.alloc([512], F32) for _ in range(2)]; B_rb = [Buf(f"rb{j}") for j in range(2)]
            nss = [ar.alloc([24], F32) for _ in range(2)]; B_nss = [Buf(f"nss{j}") for j in range(2)]
            tm = [ar.alloc([512], BF16) for _ in range(3)]; B_tm = [Buf(f"tm{j}") for j in range(3)]
            stT = [ar.alloc([4, 128], BF16) for _ in range(3)]; B_stT = [k.dbuf(f"stT{j}_{l}{int(tiles[0] < CT)}") for j in range(3)]
            vw = 65 if l == 0 else 129
            nvh = (NA + GKV) if l == 0 else DH
            vst = [ar.alloc([nvh, vw], BF16) for _ in range(2)]; B_vst = [k.dbuf(f"vst{j}_{l}{int(tiles[0] < CT)}") for j in range(2)]
            for j in range(2):
                k.op("pool", lambda e, j=j: e.memset(vst[j], 1.0), writes=[B_vst[j]])
                k.fence(B_vst[j])
            cnt = {"pj": 0, "tp": 0, "pp": 0, "tm": 0, "st": 0}

            def post_qk(pj, nm, s0, dsts, norm_gain, do_rope, csb, B_csb, dup=False):
                w = nm * 64
                src = bank(pj)[:, 0:w]
                srcB = PB[pj]
                pp = cnt["pp"] % 2; cnt["pp"] += 1
                if norm_gain is not None:
                    k.op("act", lambda e: e.activation(out=sq[pp][:, 0:w], in_=src, func=AF.Square), reads=[srcB], writes=[B_sq[pp]])
                    k.op("dve", lambda e: e.tensor_reduce(out=nss[pp][:, 0:nm], in_=sq[pp][:, 0:w].rearrange("p (a b) -> p a b", a=nm), axis=AX.X, op=ALU.add),
                         reads=[B_sq[pp]], writes=[B_nss[pp]])
                    k.op("dve", lambda e: e.tensor_scalar(out=nss[pp][:, 8:8 + nm], in0=nss[pp][:, 0:nm], scalar1=1.0 / 64, scalar2=EPS, op0=ALU.mult, op1=ALU.add), reads=[B_nss[pp]], writes=[B_nss[pp]])
                    k.op("act", lambda e: e.activation(out=nss[pp][:, 8:8 + nm], in_=nss[pp][:, 8:8 + nm], func=AF.Sqrt), reads=[B_nss[pp]], writes=[B_nss[pp]])
                    k.op("dve", lambda e: e.reciprocal(out=nss[pp][:, 16:16 + nm], in_=nss[pp][:, 8:8 + nm]), reads=[B_nss[pp]], writes=[B_nss[pp]])
                    k.op("dve", lambda e: e.tensor_tensor(out=qn[pp][:, 0:w].rearrange("p (a b) -> p a b", a=nm), in0=src.rearrange("p (a b) -> p a b", a=nm),
                                                          in1=nss[pp][:, 16:16 + nm][:, :, None].to_broadcast([128, nm, 64]), op=ALU.mult),
                         reads=[srcB, B_nss[pp]], writes=[B_qn[pp]])
                    k.op("pool", lambda e: e.tensor_tensor(out=qn[pp][:, 0:w].rearrange("p (a b) -> p a b", a=nm), in0=qn[pp][:, 0:w].rearrange("p (a b) -> p a b", a=nm),
                                                           in1=norm_gain[:, None, :].to_broadcast([128, nm, 64]), op=ALU.mult),
                         reads=[B_qn[pp], B_gains], writes=[B_qn[pp]])
                    src = qn[pp][:, 0:w]; srcB = B_qn[pp]
                ti = cnt["tm"] % 3; cnt["tm"] += 1
                if do_rope:
                    s4 = src.rearrange("p (a b c) -> p a b c", a=nm, c=2)
                    cosb = csb[:, 0:32][:, None, :, None].to_broadcast([128, nm, 32, 2])
                    sinb = csb[:, 32:64][:, None, :, None].to_broadcast([128, nm, 32, 2])
                    A = ra[pp][:, 0:w].rearrange("p (a b c) -> p a b c", a=nm, c=2)
                    Bm = rb[pp][:, 0:w].rearrange("p (a b c) -> p a b c", a=nm, c=2)
                    o4 = tm[ti][:, 0:w].rearrange("p (a b c) -> p a b c", a=nm, c=2)
                    k.op("dve", lambda e: e.tensor_tensor(out=A, in0=s4, in1=cosb, op=ALU.mult), reads=[srcB, B_csb], writes=[B_ra[pp]])
                    k.op("dve", lambda e: e.tensor_tensor(out=Bm, in0=s4, in1=sinb, op=ALU.mult), reads=[srcB, B_csb], writes=[B_rb[pp]])
                    k.op("pool", lambda e: e.tensor_tensor(out=o4[:, :, :, 0], in0=A[:, :, :, 0], in1=Bm[:, :, :, 1], op=ALU.subtract), reads=[B_ra[pp], B_rb[pp]], writes=[B_tm[ti]])
                    k.op("pool", lambda e: e.tensor_tensor(out=o4[:, :, :, 1], in0=Bm[:, :, :, 0], in1=A[:, :, :, 1], op=ALU.add), reads=[B_ra[pp], B_rb[pp]], writes=[B_tm[ti]])
                else:
                    k.op("act", lambda e: e.activation(out=tm[ti][:, 0:w], in_=src, func=AF.Copy), reads=[srcB], writes=[B_tm[ti]])
                if dup:
                    chunks = [(m * 64, 64) for m in range(nm)]
                else:
                    chunks = [(c * 128, 128) for c in range(w // 128)]
                tb = 4 + cnt["tp"] % 2; cnt["tp"] += 1
                ptb = bank(tb).bitcast(BF16)
                sti = cnt["st"] % 3; cnt["st"] += 1
                for ci, (c0, cw) in enumerate(chunks):
                    if dup:
                        for hlf in range(2):
                            k.op("pe", lambda e, ci=ci, c0=c0, hlf=hlf: e.transpose(out=ptb[hlf * 64:(hlf + 1) * 64, ci * 128:(ci + 1) * 128], in_=tm[ti][:, c0:c0 + 64], identity=ident),
                                 reads=[B_tm[ti], B_ident], writes=[PB[tb]])
                    else:
                        k.op("pe", lambda e, ci=ci, c0=c0: e.transpose(out=ptb[:, ci * 128:(ci + 1) * 128], in_=tm[ti][:, c0:c0 + 128], identity=ident),
                             reads=[B_tm[ti], B_ident], writes=[PB[tb]])
                nch_ = len(chunks)
                k.op("dve", lambda e: e.tensor_copy(out=stT[sti][:, 0:nch_, :], in_=ptb[:, 0:nch_ * 128].rearrange("p (a b) -> p a b", a=nch_)), reads=[PB[tb]], writes=[B_stT[sti]])
                for ci in range(nch_):
                    k.dma("sp", dsts[ci], stT[sti][:, ci, :], reads=[B_stT[sti]])

            for it, i in enumerate(tiles):
                j = it % 2
                s0 = i * 128
                k.dma("sp", xt[j], src_tile(l, i), writes=[B_x[j]])
                k.dma("sp", cst[j], cossin[s0:s0 + 128, :], writes=[B_cs[j]])
                norm_mod_T(xt[j], B_x[j], 1 * D, 0, junk, B_junk, ss[j], B_ss[j], tmp, B_tmp, hb[j], B_hb[j], hT[j], B_hT[j], 0)
                if l == 0:
                    blocks = []
                    c = 0
                    for nm_total, kind in ((NA, "naq"), (NA, "nak"), (NA, "nav"), (GQ, "gq"), (GKV, "gk"), (GKV, "gv")):
                        m0 = 0
                        while m0 < nm_total:
                            nm = min(8, nm_total - m0)
                            blocks.append((kind, m0, nm, c + m0 * 64))
                            m0 += nm
                        c += nm_total * 64
                else:
                    blocks = []
                    for kind, base in (("dq", 0), ("dk", DH * 128), ("dv", 2 * DH * 128)):
                        m0 = 0
                        while m0 < 2 * DH:
                            nm = min(8, 2 * DH - m0)
                            blocks.append((kind, m0, nm, base + m0 * 64))
                            m0 += nm
                vj = it % 2
                for (kind, m0, nm, c0) in blocks:
                    if kind == "dq" and i < CT:
                        continue
                    w = nm * 64
                    pj = 1 + cnt["pj"] % 3; cnt["pj"] += 1
                    for kc in range(KC):
                        k.op("pe", lambda e, kc=kc, pj=pj, c0=c0, w=w, j=j: e.matmul(bank(pj)[:, 0:w], lhsT=hT[j][:, kc, :], rhs=wsb[:, kc, c0:c0 + w], start=(kc == 0), stop=(kc == KC - 1)),
                             reads=[B_hT[j], B_w], writes=[PB[pj]])
                    if kind in ("naq", "nak"):
                        dst = NAQT if kind == "naq" else NAKT
                        post_qk(pj, nm, s0, [dst[(m0 // 2) + ci, :, s0:s0 + 128] for ci in range(nm // 2)], None, False, None, None)
                    elif kind == "gq":
                        post_qk(pj, nm, s0, [GQT[(m0 // 2) + ci, :, s0:s0 + 128] for ci in range(nm // 2)], qg_r, True, cst[j], B_cs[j])
                    elif kind == "gk":
                        post_qk(pj, nm, s0, [GKT[m0 + ci, :, s0:s0 + 128] for ci in range(nm)], kg_r, True, cst[j], B_cs[j], dup=True)
                    elif kind in ("dq", "dk"):
                        dst = DQT if kind == "dq" else DKT
                        post_qk(pj, nm, s0, [dst[(m0 // 2) + ci, :, s0:s0 + 128] for ci in range(nm // 2)], None, True, cst[j], B_cs[j])
                    elif kind in ("nav", "gv"):
                        h0 = m0 if kind == "nav" else NA + m0
                        k.op("act", lambda e, pj=pj, h0=h0, nm=nm, vj=vj: e.activation(out=vst[vj][:, h0:h0 + nm, 0:64], in_=bank(pj)[:, 0:nm * 64].rearrange("p (a b) -> p a b", a=nm), func=AF.Copy),
                             reads=[PB[pj]], writes=[B_vst[vj]])
                    elif kind == "dv":
                        h0 = m0 // 2
                        k.op("act", lambda e, pj=pj, h0=h0, nm=nm, vj=vj: e.activation(out=vst[vj][:, h0:h0 + nm // 2, 0:128], in_=bank(pj)[:, 0:nm * 64].rearrange("p (a b) -> p a b", a=nm // 2), func=AF.Copy),
                             reads=[PB[pj]], writes=[B_vst[vj]])
                if l == 0:
                    for p in range(NAP):
                        k.dma("sp", NAV[p, :, i * 130:(i + 1) * 130], vst[vj][:, 2 * p:2 * p + 2, :].rearrange("p a b -> p (a b)"), reads=[B_vst[vj]])
                    for g in range(GKV):
                        k.dma("sp", GV[g, :, i * 65:(i + 1) * 65], vst[vj][:, NA + g, :], reads=[B_vst[vj]])
                else:
                    for h in range(DH):
                        k.dma("sp", DV[h, :, i * 129:(i + 1) * 129], vst[vj][:, h, :], reads=[B_vst[vj]])
            k.barrier()

        def attn_phase(units, qtiles, chunks, finish_kind, qblend=None):
            ar.off = PERS
            vtot_max = max(u["vtot"] for u in units)
            KTs = [ar.alloc([NKEY], BF16) for _ in range(2)]; B_KT = [k.dbuf(f"aKT{j}") for j in range(2)]
            Vs = [ar.alloc([NCH * vtot_max], BF16) for _ in range(2)]; B_V = [k.dbuf(f"aV{j}") for j in range(2)]
            QTs = [ar.alloc([512], BF16) for _ in range(2)]; B_QT = [k.dbuf(f"aQT{j}") for j in range(2)]
            QBs = [ar.alloc([512], BF16) for _ in range(2)]; B_QB = [k.dbuf(f"aQB{j}") for j in range(2)]
            QMs = [ar.alloc([512], BF16) for _ in range(2)]; B_QM = [Buf(f"aQM{j}") for j in range(2)]
            Ps = [ar.alloc([2, 512], BF16) for _ in range(3)]; B_P = [Buf(f"aP{j}") for j in range(3)]
            rc = ar.alloc([16], F32); B_rc = Buf("rc")
            stg = [ar.alloc([4, 128], BF16) for _ in range(2)]; B_stg = [k.dbuf(f"aStg{j}") for j in range(2)]
            t1 = [ar.alloc([128], F32) for _ in range(2)]; B_t1 = [Buf(f"t1{j}") for j in range(2)]
            o1 = [ar.alloc([128], F32) for _ in range(2)]; B_o1 = [Buf(f"o1{j}") for j in range(2)]
            jnk = ar.alloc([128], F32); B_jnk = Buf("ajnk")
            ssd = [ar.alloc([3], F32) for _ in range(2)]; B_ssd = [Buf(f"ssd{j}") for j in range(2)]
            B_S = [Buf("S0"), Buf("S1")]
            B_O = Buf("O")
            cn = {"q": 0, "s": 0, "p": 0, "st": 0, "d": 0}
            for ui, u in enumerate(units):
                kj = ui % 2
                vt = u["vtot"]
                nck = max(chunks) + 1
                k.dma("sp", KTs[kj][:, 0:nck * 128], u["KT"][:, 0:nck * 128], writes=[B_KT[kj]])
                k.dma("sp", Vs[kj][:, 0:nck * vt], u["V"][:, 0:nck * vt], writes=[B_V[kj]])
                Vv = Vs[kj][:, 0:NCH * vt].rearrange("p (c w) -> p c w", w=vt)
                wmax = max(u["vsl"][0][1], u["vsl"][1][1])
                per_bank = 512 // wmax
                for (s0, nq) in qtiles:
                    nsub = nq // 128
                    qj = cn["q"] % 2; cn["q"] += 1
                    k.dma("sp", QTs[qj][:, 0:nq], u["QT"][:, s0:s0 + nq], writes=[B_QT[qj]])
                    if qblend is not None:
                        k.dma("sp", QBs[qj][:, 0:nq], u["QT"][:, s0 + qblend:s0 + qblend + nq], writes=[B_QB[qj]])
                        k.op("dve", lambda e: e.tensor_scalar(out=QMs[qj][:, 0:nq], in0=QTs[qj][:, 0:nq], scalar1=sel[:, 0:1], scalar2=None, op0=ALU.mult),
                             reads=[B_QT[qj], B_gains], writes=[B_QM[qj]])
                        k.op("dve", lambda e: e.scalar_tensor_tensor(out=QTs[qj][:, 0:nq], in0=QBs[qj][:, 0:nq], scalar=sel[:, 1:2], in1=QMs[qj][:, 0:nq], op0=ALU.mult, op1=ALU.add),
                             reads=[B_QB[qj], B_QM[qj], B_gains], writes=[B_QT[qj]])
                    accs = []
                    for a in range(nsub * 2):
                        b = 4 + a // per_bank
                        o = (a % per_bank) * wmax
                        accs.append((b, o))

                    def s_mm(c, sj):
                        for m in range(2):
                            k.op("pe", lambda e, c=c, m=m, sj=sj: e.matmul(bank(2 * sj + m)[:, 0:nq], lhsT=KTs[kj][m * 64:(m + 1) * 64, c * 128:(c + 1) * 128],
                                                                           rhs=QTs[qj][m * 64:(m + 1) * 64, 0:nq], start=True, stop=True),
                                 reads=[B_KT[kj], B_QT[qj]], writes=[B_S[sj]])

                    sidx = cn["s"]
                    s_mm(chunks[0], sidx % 2)
                    for ci, c in enumerate(chunks):
                        sj = (sidx + ci) % 2
                        if ci + 1 < len(chunks):
                            s_mm(chunks[ci + 1], (sidx + ci + 1) % 2)
                        pj = cn["p"] % 3; cn["p"] += 1
                        k.op("act", lambda e, sj=sj, pj=pj: e.activation(out=Ps[pj][:, :, 0:nq], in_=bank(2 * sj, 2).rearrange("p (a b) -> p a b", a=2)[:, :, 0:nq], func=AF.Exp, scale=0.125),
                             reads=[B_S[sj]], writes=[B_P[pj]])
                        seen = set()
                        for a in range(nsub * 2):
                            uu, m = a // 2, a % 2
                            b, o = accs[a]
                            off, w = u["vsl"][m]
                            first_in_bank = (ci == 0) and (b not in seen)
                            seen.add(b)
                            k.op("pe", lambda e, b=b, o=o, w=w, off=off, m=m, uu=uu, pj=pj, c=c, fib=first_in_bank, last=(ci == len(chunks) - 1):
                                 e.matmul(bank(b)[:, o:o + w], lhsT=Ps[pj][:, m, uu * 128:(uu + 1) * 128], rhs=Vv[:, c, off:off + w], start=fib, stop=last, skip_group_check=True),
                                 reads=[B_P[pj], B_V[kj]], writes=[B_O])
                    cn["s"] += len(chunks)
                    sti = cn["st"] % 2; cn["st"] += 1
                    for uu in range(nsub):
                        (b0, o0), (b1, o1_) = accs[2 * uu], accs[2 * uu + 1]
                        w0 = u["vsl"][0][1]; w1 = u["vsl"][1][1]
                        k.op("dve", lambda e, b0=b0, o0=o0, w0=w0, uu=uu: e.reciprocal(out=rc[:, 2 * uu:2 * uu + 1], in_=bank(b0)[:, o0 + w0 - 1:o0 + w0]), reads=[B_O], writes=[B_rc])
                        k.op("dve", lambda e, b1=b1, o1_=o1_, w1=w1, uu=uu: e.reciprocal(out=rc[:, 2 * uu + 1:2 * uu + 2], in_=bank(b1)[:, o1_ + w1 - 1:o1_ + w1]), reads=[B_O], writes=[B_rc])
                        if finish_kind == "gqa":
                            k.op("dve", lambda e, b0=b0, o0=o0, uu=uu, sti=sti: e.tensor_scalar(out=stg[sti][:, uu, 0:64], in0=bank(b0)[:, o0:o0 + 64], scalar1=rc[:, 2 * uu:2 * uu + 1], scalar2=None, op0=ALU.mult),
                                 reads=[B_O, B_rc], writes=[B_stg[sti]])
                            k.op("dve", lambda e, b1=b1, o1_=o1_, uu=uu, sti=sti: e.tensor_scalar(out=stg[sti][:, uu, 64:128], in0=bank(b1)[:, o1_:o1_ + 64], scalar1=rc[:, 2 * uu + 1:2 * uu + 2], scalar2=None, op0=ALU.mult),
                                 reads=[B_O, B_rc], writes=[B_stg[sti]])
                        else:
                            dj = cn["d"] % 2; cn["d"] += 1
                            k.op("dve", lambda e, uu=uu: e.tensor_tensor(out=rc[:, 8 + uu:9 + uu], in0=rc[:, 2 * uu + 1:2 * uu + 2], in1=neglam, op=ALU.mult), reads=[B_rc, B_lam], writes=[B_rc])
                            k.op("dve", lambda e, b0=b0, o0=o0, uu=uu, dj=dj: e.tensor_scalar(out=t1[dj], in0=bank(b0)[:, o0:o0 + 128], scalar1=rc[:, 2 * uu:2 * uu + 1], scalar2=None, op0=ALU.mult),
                                 reads=[B_O, B_rc], writes=[B_t1[dj]])
                            k.op("dve", lambda e, b1=b1, o1_=o1_, uu=uu, dj=dj: e.scalar_tensor_tensor(out=o1[dj], in0=bank(b1)[:, o1_:o1_ + 128], scalar=rc[:, 8 + uu:9 + uu], in1=t1[dj], op0=ALU.mult, op1=ALU.add),
                                 reads=[B_O, B_rc, B_t1[dj]], writes=[B_o1[dj]])
                            k.op("act", lambda e, dj=dj: e.activation(out=jnk, in_=o1[dj], func=AF.Square, accum_out=ssd[dj][:, 0:1]), reads=[B_o1[dj]], writes=[B_jnk, B_ssd[dj]])
                            rstd_chain(ssd[dj], 1.0 / 128, B_ssd[dj])
                            k.op("dve", lambda e, uu=uu, dj=dj, sti=sti: e.scalar_tensor_tensor(out=stg[sti][:, uu, :], in0=o1[dj], scalar=ssd[dj][:, 2:3], in1=sub_r, op0=ALU.mult, op1=ALU.mult),
                                 reads=[B_o1[dj], B_ssd[dj], B_gains], writes=[B_stg[sti]])
                    k.dma("sp", AO[s0:s0 + nq, u["col"]:u["col"] + 128].rearrange("(u p) c -> p u c", p=128), stg[sti][:, 0:nsub, :], reads=[B_stg[sti]])
            k.barrier()

        def na_phase():
            ar.off = PERS
            KTs = [ar.alloc([NKEY], BF16) for _ in range(2)]; B_KT = [k.dbuf(f"nKT{j}") for j in range(2)]
            Vs = [ar.alloc([NCH, 130], BF16) for _ in range(2)]; B_V = [k.dbuf(f"nV{j}") for j in range(2)]
            QTs = [ar.alloc([T], BF16) for _ in range(2)]; B_QT = [k.dbuf(f"nQT{j}") for j in range(2)]
            bst = ar.alloc([NCLS * 2 * 5 * 128], F32); B_bst = k.dbuf("nbst")
            Em = [ar.alloc([NCLS, 2, 640], BF16) for _ in range(2)]; B_E = [Buf(f"nE{j}") for j in range(2)]
            Ps = [ar.alloc([7 * 128], BF16) for _ in range(3)]; B_P = [Buf(f"nP{j}") for j in range(3)]
            rc = [ar.alloc([2], F32) for _ in range(2)]; B_rc = [Buf(f"nrc{j}") for j in range(2)]
            stg = [ar.alloc([4, 128], BF16) for _ in range(2)]; B_stg = [k.dbuf(f"nStg{j}") for j in range(2)]
            B_S = [Buf(f"nS{j}") for j in range(3)]
            B_O = [Buf(f"nO{j}") for j in range(2)]
            cn = {"s": 0, "p": 0, "o": 0}
            for p in range(NAP):
                kj = p % 2
                k.dma("sp", KTs[kj], NAKT[p], writes=[B_KT[kj]])
                k.dma("sp", Vs[kj].rearrange("p a b -> p (a b)"), NAV[p], writes=[B_V[kj]])
                k.dma("sp", QTs[kj], NAQT[p, :, CTX:NKEY], writes=[B_QT[kj]])
                k.dma("sp", bst, nabias[p], writes=[B_bst])
                k.op("act", lambda e, kj=kj: e.activation(out=Em[kj].rearrange("p a b c -> p (a b c)"), in_=bst, func=AF.Exp), reads=[B_bst], writes=[B_E[kj]])
                for u in range(NT):
                    r0 = 2 * u
                    csr = min(min(max(r0 - 4, 0), ROWS - 8), ROWS - 10)
                    cls = {0: 1, 2: 2, ROWS - 4: 3, ROWS - 2: 4}.get(r0, 0)
                    cidx = [CT + csr // 2 + j for j in range(5)] + list(range(CT))
                    sti = (u // 4) % 2
                    for m in range(2):
                        sj = cn["s"] % 3; cn["s"] += 1
                        pj = cn["p"] % 3; cn["p"] += 1
                        oj = cn["o"] % 2; cn["o"] += 1
                        Sb = bank(2 * sj, 2)
                        for j, c in enumerate(cidx):
                            k.op("pe", lambda e, j=j, c=c, m=m, Sb=Sb, u=u: e.matmul(Sb[:, j * 128:(j + 1) * 128], lhsT=KTs[kj][m * 64:(m + 1) * 64, c * 128:(c + 1) * 128],
                                                                               rhs=QTs[kj][m * 64:(m + 1) * 64, u * 128:(u + 1) * 128], start=True, stop=True),
                                 reads=[B_KT[kj], B_QT[kj]], writes=[B_S[sj]])
                        k.op("act", lambda e, Sb=Sb, pj=pj: e.activation(out=Ps[pj], in_=Sb[:, 0:7 * 128], func=AF.Exp, scale=0.125), reads=[B_S[sj]], writes=[B_P[pj]])
                        k.op("dve", lambda e, pj=pj, cls=cls, m=m: e.tensor_tensor(out=Ps[pj][:, 0:640], in0=Ps[pj][:, 0:640], in1=Em[kj][:, cls, m, :], op=ALU.mult),
                             reads=[B_P[pj], B_E[kj]], writes=[B_P[pj]])
                        Ob = bank(6 + oj)
                        for j, c in enumerate(cidx):
                            k.op("pe", lambda e, j=j, c=c, m=m, Ob=Ob, pj=pj: e.matmul(Ob[:, 0:65], lhsT=Ps[pj][:, j * 128:(j + 1) * 128], rhs=Vs[kj][:, c, m * 65:(m + 1) * 65], start=(j == 0), stop=(j == 6)),
                                 reads=[B_P[pj], B_V[kj]], writes=[B_O[oj]])
                        k.op("dve", lambda e, Ob=Ob, oj=oj: e.reciprocal(out=rc[oj][:, 0:1], in_=Ob[:, 64:65]), reads=[B_O[oj]], writes=[B_rc[oj]])
                        k.op("dve", lambda e, Ob=Ob, oj=oj, m=m, u=u, sti=sti: e.tensor_scalar(out=stg[sti][:, u % 4, m * 64:(m + 1) * 64], in0=Ob[:, 0:64], scalar1=rc[oj][:, 0:1], scalar2=None, op0=ALU.mult),
                             reads=[B_O[oj], B_rc[oj]], writes=[B_stg[sti]])
                    if u % 4 == 3:
                        s0 = CTX + (u - 3) * 128
                        k.dma("sp", AO[s0:s0 + 512, p * 128:(p + 1) * 128].rearrange("(u p) c -> p u c", p=128), stg[sti], reads=[B_stg[sti]])
            k.barrier()

        def wout_phase(l, tiles, dst, xblend=None):
            ar.off = PERS
            w_src = w_out0 if l == 0 else w_out1
            wsb = ar.alloc([KC, D], BF16); B_w = Buf("w_out")
            mark = ar.off
            stg = [ar.alloc([D], F32) for _ in range(2)]; B_stg = [k.dbuf(f"wostg{j}") for j in range(2)]
            load_w(wsb, w_src, KC, D, stg, B_stg, B_w)
            k.barrier()
            ar.off = mark
            ao = [ar.alloc([D], BF16) for _ in range(2)]; B_ao = [k.dbuf(f"ao{j}") for j in range(2)]
            aT = [ar.alloc([KC, 128], BF16) for _ in range(2)]; B_aT = [Buf(f"aT{j}") for j in range(2)]
            xt = [ar.alloc([D], F32) for _ in range(2)]; B_x = [k.dbuf(f"wx{j}") for j in range(2)]
            tmp = [ar.alloc([D], F32) for _ in range(2)]; B_tmp = [Buf(f"wtmp{j}") for j in range(2)]
            xb = [ar.alloc([D], F32) for _ in range(2)]; B_xb = [k.dbuf(f"wxb{j}") for j in range(2)]
            NB = (D + 511) // 512
            for it, i in enumerate(tiles):
                j = it % 2
                k.dma("sp", ao[j], AO[i * 128:(i + 1) * 128, :], writes=[B_ao[j]])
                k.dma("sp", xt[j], src_tile(l, i), writes=[B_x[j]])
                if xblend is not None:
                    k.dma("sp", xb[j], src_tile(l, i + xblend), writes=[B_xb[j]])
                    k.op("dve", lambda e: e.tensor_scalar(out=xt[j], in0=xt[j], scalar1=sel[:, 0:1], scalar2=None, op0=ALU.mult), reads=[B_x[j], B_gains], writes=[B_x[j]])
                    k.op("dve", lambda e: e.scalar_tensor_tensor(out=xt[j], in0=xb[j], scalar=sel[:, 1:2], in1=xt[j], op0=ALU.mult, op1=ALU.add), reads=[B_xb[j], B_x[j], B_gains], writes=[B_x[j]])
                tb = j
                ptb = bank(tb).bitcast(BF16)
                for kc in range(KC):
                    k.op("pe", lambda e, kc=kc, j=j, ptb=ptb: e.transpose(out=ptb[:, kc * 128:(kc + 1) * 128], in_=ao[j][:, kc * 128:(kc + 1) * 128], identity=ident),
                         reads=[B_ao[j], B_ident], writes=[PB[tb]])
                k.op("act", lambda e, j=j, ptb=ptb: e.activation(out=aT[j], in_=ptb[:, 0:KC * 128].rearrange("p (a b) -> p a b", a=KC), func=AF.Copy), reads=[PB[tb]], writes=[B_aT[j]])
                yb = 2 + 2 * j
                for nb in range(NB):
                    cw = min(512, D - nb * 512)
                    for kc in range(KC):
                        k.op("pe", lambda e, kc=kc, j=j, nb=nb, cw=cw, yb=yb: e.matmul(bank(yb + nb)[:, 0:cw], lhsT=aT[j][:, kc, :], rhs=wsb[:, kc, nb * 512:nb * 512 + cw], start=(kc == 0), stop=(kc == KC - 1)),
                             reads=[B_aT[j], B_w], writes=[PB[yb + nb]])
                    k.op("dve", lambda e, j=j, nb=nb, cw=cw, yb=yb: e.tensor_tensor(out=tmp[j][:, nb * 512:nb * 512 + cw], in0=bank(yb + nb)[:, 0:cw], in1=modrows[:, 2 * D + nb * 512:2 * D + nb * 512 + cw], op=ALU.mult),
                         reads=[PB[yb + nb], B_mod], writes=[B_tmp[j]])
                k.op("pool", lambda e, j=j: e.tensor_tensor(out=xt[j], in0=xt[j], in1=tmp[j], op=ALU.add), reads=[B_tmp[j], B_x[j]], writes=[B_x[j]])
                k.dma("sp", dst[i * 128:(i + 1) * 128, :], xt[j], reads=[B_x[j]])
            k.barrier()

        def ffn_phase(l, tiles, src, dst, final):
            ar.off = PERS
            wg = ar.alloc([KC, FF], BF16); wu = ar.alloc([KC, FF], BF16); wd = ar.alloc([FC, D], BF16)
            B_wg = Buf("wg"); B_wu = Buf("wu"); B_wd = Buf("wd")
            mark = ar.off
            stg = [ar.alloc([max(FF, D)], F32) for _ in range(2)]; B_stg = [k.dbuf(f"fstg{j}") for j in range(2)]
            load_w(wg, w_gate[l], KC, FF, stg, B_stg, B_wg)
            load_w(wu, w_up[l], KC, FF, stg, B_stg, B_wu)
            load_w(wd, w_down[l], FC, D, stg, B_stg, B_wd)
            k.barrier()
            ar.off = mark
            G = 2
            xg = ar.alloc([G, D], F32); B_xg = [k.dbuf(f"fx{j}") for j in range(G)]
            junk = ar.alloc([D], BF16); B_junk = Buf("fjunk")
            tmp = ar.alloc([D], F32); B_tmp = Buf("ftmp")
            hb = ar.alloc([D], BF16); B_hb = Buf("fhb")
            hT = ar.alloc([KC, G * 128], BF16); B_hT = Buf("fhT")
            AT = ar.alloc([FC, G * 128], BF16); B_AT = Buf("fAT")
            sg = [ar.alloc([G * 128], F32) for _ in range(2)]; B_sg = [Buf(f"sg{j}") for j in range(2)]
            ss = [ar.alloc([3], F32) for _ in range(G)]; B_ss = [Buf(f"fss{j}") for j in range(G)]
            fs = [ar.alloc([3], F32) for _ in range(G)]; B_fs = [Buf(f"ffs{j}") for j in range(G)]
            NB = (D + 511) // 512
            groups = [tiles[a:a + G] for a in range(0, len(tiles), G)]
            fi = 0
            for grp in groups:
                ng = len(grp)
                NQ = ng * 128
                for gi, i in enumerate(grp):
                    k.dma("sp", xg[:, gi, :], src[i * 128:(i + 1) * 128, :], writes=[B_xg[gi]])
                    norm_mod_T(xg[:, gi, :], B_xg[gi], 4 * D, 3 * D, junk, B_junk, ss[gi], B_ss[gi], tmp, B_tmp, hb, B_hb, hT[:, :, gi * 128:(gi + 1) * 128], B_hT, 0)
                for f in range(FC):
                    gb = 1 + fi % 2; ub = 3 + fi % 2; sj = fi % 2; fi += 1
                    for kc in range(KC):
                        k.op("pe", lambda e, kc=kc, f=f, gb=gb: e.matmul(bank(gb)[:, 0:NQ], lhsT=wg[:, kc, f * 128:(f + 1) * 128], rhs=hT[:, kc, 0:NQ], start=(kc == 0), stop=(kc == KC - 1)),
                             reads=[B_wg, B_hT], writes=[PB[gb]])
                    for kc in range(KC):
                        k.op("pe", lambda e, kc=kc, f=f, ub=ub: e.matmul(bank(ub)[:, 0:NQ], lhsT=wu[:, kc, f * 128:(f + 1) * 128], rhs=hT[:, kc, 0:NQ], start=(kc == 0), stop=(kc == KC - 1)),
                             reads=[B_wu, B_hT], writes=[PB[ub]])
                    k.op("act", lambda e, gb=gb, sj=sj: e.activation(out=sg[sj][:, 0:NQ], in_=bank(gb)[:, 0:NQ], func=AF.Silu), reads=[PB[gb]], writes=[B_sg[sj]])
                    k.op("dve", lambda e, ub=ub, sj=sj, f=f: e.tensor_tensor(out=AT[:, f, 0:NQ], in0=bank(ub)[:, 0:NQ], in1=sg[sj][:, 0:NQ], op=ALU.mult), reads=[PB[ub], B_sg[sj]], writes=[B_AT])
                for gi, i in enumerate(grp):
                    for nb in range(NB):
                        cw = min(512, D - nb * 512)
                        yb = 5 + nb
                        for f in range(FC):
                            k.op("pe", lambda e, f=f, gi=gi, nb=nb, cw=cw, yb=yb: e.matmul(bank(yb)[:, 0:cw], lhsT=AT[:, f, gi * 128:(gi + 1) * 128], rhs=wd[:, f, nb * 512:nb * 512 + cw], start=(f == 0), stop=(f == FC - 1)),
                                 reads=[B_AT, B_wd], writes=[PB[yb]])
                        k.op("dve", lambda e, nb=nb, cw=cw, yb=yb: e.tensor_tensor(out=tmp[:, nb * 512:nb * 512 + cw], in0=bank(yb)[:, 0:cw], in1=modrows[:, 5 * D + nb * 512:5 * D + nb * 512 + cw], op=ALU.mult),
                             reads=[PB[yb], B_mod], writes=[B_tmp])
                    xv = xg[:, gi, :]
                    k.op("pool", lambda e, xv=xv: e.tensor_tensor(out=xv, in0=xv, in1=tmp, op=ALU.add), reads=[B_tmp, B_xg[gi]], writes=[B_xg[gi]])
                    if final:
                        k.op("act", lambda e, xv=xv, gi=gi: e.activation(out=junk, in_=xv, func=AF.Square, accum_out=fs[gi][:, 0:1]), reads=[B_xg[gi]], writes=[B_junk, B_fs[gi]])
                        rstd_chain(fs[gi], 1.0 / D, B_fs[gi])
                        k.op("dve", lambda e, xv=xv, gi=gi: e.scalar_tensor_tensor(out=xv, in0=xv, scalar=fs[gi][:, 2:3], in1=fg_r, op0=ALU.mult, op1=ALU.mult),
                             reads=[B_xg[gi], B_fs[gi], B_gains], writes=[B_xg[gi]])
                        k.dma("sp", dst[(i - CT) * 128:(i - CT + 1) * 128, :], xv, reads=[B_xg[gi]])
                    else:
                        k.dma("sp", dst[i * 128:(i + 1) * 128, :], xv, reads=[B_xg[gi]])
            k.barrier()

        ctx_tiles = list(range(CT)); x_tiles = list(range(CT, NCH))
        qt_x = [(CTX + a * 512, 512) for a in range(T // 512)]
        all_chunks = list(range(NCH))
        ada_phase(0, 1, 6 * D)
        proj_phase(0, ctx_tiles)
        na_units = [dict(KT=NAKT[p], V=NAV[p], vtot=130, vsl=[(0, 65), (65, 65)], QT=NAQT[p], col=p * 128) for p in range(NAP)]
        g_units = [dict(KT=GKT[(2 * p) // (GQ // GKV)], V=GV[(2 * p) // (GQ // GKV)], vtot=65, vsl=[(0, 65), (0, 65)], QT=GQT[p], col=(NAP + p) * 128) for p in range(GQP)]
        attn_phase(na_units + g_units, [(0, CTX)], list(range(CT)), "gqa")
        wout_phase(0, ctx_tiles, XM)
        ffn_phase(0, ctx_tiles, XM, X1, False)
        ada_phase(0, 0, 6 * D)
        proj_phase(0, x_tiles)
        na_phase()
        attn_phase(g_units, qt_x, all_chunks, "gqa")
        wout_phase(0, x_tiles, XM)
        ffn_phase(0, x_tiles, XM, X1, False)
        ada_phase(1, 1, 2 * D)
        proj_phase(1, ctx_tiles)
        ada_phase(1, 0, 6 * D)
        proj_phase(1, x_tiles)
        d_units = [dict(KT=DKT[h], V=DV[h], vtot=129, vsl=[(0, 129), (0, 129)], QT=DQT[h], col=h * 128) for h in range(DH)]
        if split:
            own_tiles = list(range(CT, CT + NT // 2))
            qt_own = [(CTX + a * 512, 512) for a in range(T // 2 // 512)]
            attn_phase(d_units, qt_own, all_chunks, "diff", qblend=T // 2)
            wout_phase(1, own_tiles, XM, xblend=NT // 2)
            ffn_phase(1, own_tiles, XM, out_d, True)
        else:
            attn_phase(d_units, qt_x, all_chunks, "diff")
            wout_phase(1, x_tiles, XM)
            ffn_phase(1, x_tiles, XM, out_d, True)
        k.emit()
        print("instr counts", {e: len(k.ops[e]) for e in ENGS}, "signalled", k.ncounts, "sems", k.nsem, flush=True)
        print("max dma sem", sorted([(d.count, n) for n, d in k.dpool.items()])[-6:], flush=True)
    return nc


def _cossin_table(cfg):
    T = cfg["ROWS"] * GRID_W; CTX = cfg["CTX"]
    t = np.arange(T)
    row = (t // GRID_W).astype(np.float32); col = (t % GRID_W).astype(np.float32)
    nf = 16
    inv = (np.float32(10000.0) ** (-np.arange(nf, dtype=np.float32) / np.float32(nf))).astype(np.float32)
    ang = np.concatenate([row[:, None] * inv, col[:, None] * inv], axis=-1).astype(np.float32)
    tab = np.zeros((CTX + T, 64), np.float32)
    tab[:CTX, 0:32] = 1.0
    tab[CTX:, 0:32] = np.cos(ang)
    tab[CTX:, 32:64] = np.sin(ang)
    return tab


def _na_bias_table(cfg, rpb):
    ROWS = cfg["ROWS"]; NA = cfg["NA"]
    out = np.full((NA // 2, 128, 5, 2, 5, 128), NEG, np.float32)
    kk = np.arange(128); qq = np.arange(128)
    for cls, r0 in enumerate((4, 0, 2, ROWS - 4, ROWS - 2)):
        csr = min(min(max(r0 - 4, 0), ROWS - 8), ROWS - 10)
        qr = r0 + qq // 64; qc = qq % 64
        rs = np.clip(qr - 4, 0, ROWS - 8); cs = np.clip(qc - 8, 0, GRID_W - 16)
        for j in range(5):
            kr = csr + 2 * j + kk // 64; kc = kk % 64
            valid = ((kr[:, None] >= rs[None, :]) & (kr[:, None] < rs[None, :] + 8) &
                     (kc[:, None] >= cs[None, :]) & (kc[:, None] < cs[None, :] + 16))
            ri = np.clip(kr[:, None] - qr[None, :] + 7, 0, 14); ci = np.clip(kc[:, None] - qc[None, :] + 15, 0, 30)
            for h in range(NA):
                g = rpb[h][ri, ci]
                out[h // 2, :, cls, h % 2, j, :] = np.where(valid, g, np.float32(NEG))
    return out.reshape(NA // 2, 128, 5 * 2 * 5 * 128)


def make_core_inputs(cfg, inp, b):
    D = cfg["D"]; KC = D // 128
    f = lambda a: np.ascontiguousarray(np.asarray(a, dtype=np.float32))
    cvec = np.concatenate([f(inp["c"][b]).reshape(KC, 128).T, f(inp["c_ctx"]).reshape(KC, 128).T], axis=1)
    lamv = np.concatenate([f(inp["diff_lambda_q1"][0]), f(inp["diff_lambda_k1"][0]), f(inp["diff_lambda_q2"][0]), f(inp["diff_lambda_k2"][0])])[None]
    return {
        "x": f(inp["x"][b]), "ctx": f(inp["ctx"][b]), "cvec": f(cvec),
        "ada_w": f(inp["ada_w"]), "ada_b": f(inp["ada_b"]),
        "ffn_w_gate": f(inp["ffn_w_gate"]), "ffn_w_up": f(inp["ffn_w_up"]), "ffn_w_down": f(inp["ffn_w_down"]),
        "par_w_in": f(inp["par_w_in"][0]), "par_w_out": f(inp["par_w_out"][0]),
        "diff_w_in": f(inp["diff_w_in"][0]), "diff_w_out": f(inp["diff_w_out"][0]),
        "gqa_q_gain": f(inp["gqa_q_gain"]).reshape(1, 64), "gqa_k_gain": f(inp["gqa_k_gain"]).reshape(1, 64),
        "lamv": f(lamv), "diff_subln_gain": f(inp["diff_subln_gain"]).reshape(1, 128),
        "final_norm_gain": f(inp["final_norm_gain"]).reshape(1, D),
        "cossin": _cossin_table(cfg), "nabias": _na_bias_table(cfg, f(inp["na_rpb"][0])),
        "ident": np.eye(128, dtype=np.float32).astype(ml_dtypes.bfloat16),
        "sel": np.tile(np.array([[1.0, 0.0]], np.float32), (128, 1)),
    }


def kernel(**inputs):
    cfg = FULL_CFG
    B = inputs["x"].shape[0]
    nc = build_program(cfg)
    shared = None
    in_maps = []
    T = inputs["x"].shape[1]
    for core in range(8):
        b = core % B
        half = core // B
        m = make_core_inputs(cfg, inputs, b) if shared is None else dict(shared)
        if shared is None:
            shared = m
        else:
            f = lambda a: np.ascontiguousarray(np.asarray(a, dtype=np.float32))
            D = cfg["D"]; KC = D // 128
            m["x"] = f(inputs["x"][b]); m["ctx"] = f(inputs["ctx"][b])
            m["cvec"] = f(np.concatenate([f(inputs["c"][b]).reshape(KC, 128).T, f(inputs["c_ctx"]).reshape(KC, 128).T], axis=1))
        m["sel"] = np.tile(np.array([[1.0, 0.0]] if half == 0 else [[0.0, 1.0]], np.float32), (128, 1))
        in_maps.append(m)
    res = run_bass_kernel_spmd(nc, in_maps, core_ids=list(range(8)))
    out = np.empty((B, T, cfg["D"]), np.float32)
    for core in range(8):
        b = core % B
        half = core // B
        out[b, half * (T // 2):(half + 1) * (T // 2)] = np.asarray(res.results[core]["out"], dtype=np.float32)
    return out
```

```python
import math
import numpy as np
import ml_dtypes
from contextlib import ExitStack
import concourse.bass as bass
import concourse.mybir as mybir
from concourse.bass_utils import run_bass_kernel_spmd

F32 = mybir.dt.float32
BF16 = mybir.dt.bfloat16
U8 = mybir.dt.uint8
AF = mybir.ActivationFunctionType
ALU = mybir.AluOpType
AX = mybir.AxisListType

ENGS = ("pe", "act", "dve", "pool", "sp")
EPOCH = 2000
GRID_W = 64
EPS = 1e-6
NEG = -30000.0

FULL_CFG = dict(D=1024, ROWS=128, CTX=256, NA=8, GQ=8, GKV=2, DH=8, FF=2816)


class Buf:
    __slots__ = ("name", "w", "r", "dsem")

    def __init__(self, name, dsem=None):
        self.name = name
        self.w = {}
        self.r = {}
        self.dsem = dsem


class DmaSem:
    __slots__ = ("h", "count")

    def __init__(self, h):
        self.h = h
        self.count = 0


class _Rec:
    def __getattr__(self, name):
        def f(*a, **kw):
            self.call = (name, a, kw)
            return self
        return f


class K:
    def __init__(self, nc, stack):
        self.nc = nc
        self.stack = stack
        self.ops = {e: [] for e in ENGS}
        self.instr = {e: [] for e in ENGS}
        self.known = {e: {} for e in ENGS}
        self.dsems = []
        self.dpool = {}
        self.nsem = 0

    def sem(self, name):
        h = self.stack.enter_context(self.nc.semaphore(name))
        self.nsem += 1
        return h

    def dsem(self, name):
        d = DmaSem(self.sem(name))
        self.dsems.append(d)
        return d

    def dbuf(self, name):
        if name not in self.dpool:
            self.dpool[name] = self.dsem("d_" + name)
        return Buf(name, dsem=self.dpool[name])

    def fence(self, buf):
        self._merge(buf.r, buf.w)

    @staticmethod
    def _merge(deps, d):
        for s, v in d.items():
            if deps.get(s, (None, 0))[1] < v[1]:
                deps[s] = v

    def _deps(self, reads, writes):
        deps = {}
        for b in reads:
            self._merge(deps, b.w)
        for b in writes:
            if b.r:
                self._merge(deps, b.r)
                self._merge(deps, b.w)
        return deps

    def _post(self, reads, writes, key, val):
        for b in writes:
            if b.r:
                b.w = {}
                b.r = {}
            if b.w.get(key, (None, 0))[1] < val[1]:
                b.w[key] = val
        for b in reads:
            if b in writes:
                continue
            if b.r.get(key, (None, 0))[1] < val[1]:
                b.r[key] = val

    def _waits(self, eng, deps):
        ws = []
        kn = self.known[eng]
        for key, (payload, v) in deps.items():
            if kn.get(key, 0) >= v:
                continue
            kn[key] = v
            ws.append((key, payload, v))
            if key[0] == "E":
                self.instr[payload][v - 1]["needed"] = True
        return ws

    @staticmethod
    def _bind(fn):
        rec = _Rec()
        fn(rec)
        return rec.call

    def op(self, eng, fn, reads=(), writes=()):
        deps = self._deps(reads, writes)
        ws = self._waits(eng, deps)
        r = {"call": self._bind(fn), "waits": ws, "needed": False, "dma": None}
        self.ops[eng].append(r)
        self.instr[eng].append(r)
        self._post(reads, writes, ("E", eng), (eng, len(self.instr[eng])))

    def dma(self, eng, out_ap, in_ap, reads=(), writes=()):
        ds = None
        for b in list(writes) + list(reads):
            if b.dsem is not None:
                ds = b.dsem
                break
        assert ds is not None
        deps = self._deps(reads, writes)
        ws = self._waits(eng, deps)
        ds.count += 16
        self.ops[eng].append({"call": ("dma_start", (), {"out": out_ap, "in_": in_ap}), "waits": ws, "needed": False, "dma": ds})
        self._post(reads, writes, ("D", id(ds)), (ds, ds.count))

    def barrier(self):
        evs = {}
        for e in ENGS:
            if self.instr[e]:
                evs[("E", e)] = (e, len(self.instr[e]))
        for d in self.dsems:
            if d.count > 0:
                evs[("D", id(d))] = (d, d.count)
        for e in ENGS:
            ws = self._waits(e, evs)
            if ws:
                self.ops[e].append({"call": None, "waits": ws, "needed": False, "dma": None})

    def emit(self):
        nc = self.nc
        esems = {}
        for e in ENGS:
            c = 0
            for r in self.instr[e]:
                if r["needed"]:
                    c += 1
                    r["count"] = c
            esems[e] = [self.sem(f"e_{e}_{j}") for j in range((c + EPOCH - 1) // EPOCH)]
        self.ncounts = {e: sum(1 for r in self.instr[e] if r["needed"]) for e in ENGS}

        def resolve(w):
            key, payload, v = w
            if key[0] == "D":
                return payload.h, v
            c = self.instr[payload][v - 1]["count"]
            return esems[payload][(c - 1) // EPOCH], (c - 1) % EPOCH + 1

        with nc.Block() as block:
            def run(e, eng):
                for r in self.ops[eng]:
                    for w in r["waits"]:
                        h, v = resolve(w)
                        e.wait_ge(h, v)
                    if r["call"] is None:
                        continue
                    name, a, kw = r["call"]
                    ins = getattr(e, name)(*a, **kw)
                    if r["dma"] is not None:
                        ins.then_inc(r["dma"].h, 16)
                    elif r["needed"]:
                        c = r["count"]
                        ins.then_inc(esems[eng][(c - 1) // EPOCH], 1)

            @block.tensor
            def _(e):
                run(e, "pe")

            @block.scalar
            def _(e):
                run(e, "act")

            @block.vector
            def _(e):
                run(e, "dve")

            @block.gpsimd
            def _(e):
                run(e, "pool")

            @block.sync
            def _(e):
                run(e, "sp")


class Arena:
    def __init__(self, ap, size):
        self.ap = ap
        self.size = size
        self.off = 0

    def alloc(self, shape, dt):
        esz = 4 if dt == F32 else 2
        n = int(np.prod(shape)) * esz
        n_al = (n + 63) // 64 * 64
        assert self.off + n_al <= self.size, f"arena overflow {self.off}+{n_al}>{self.size}"
        a = self.ap[:, self.off:self.off + n].bitcast(dt)
        self.off += n_al
        if len(shape) == 2:
            a = a.rearrange("p (a b) -> p a b", a=shape[0])
        elif len(shape) == 3:
            a = a.rearrange("p (a b c) -> p a b c", a=shape[0], b=shape[1])
        return a


def build_program(cfg, debug=False, split=True):
    D = cfg["D"]; KC = D // 128; ROWS = cfg["ROWS"]; T = ROWS * GRID_W; CTX = cfg["CTX"]
    NA = cfg["NA"]; GQ = cfg["GQ"]; GKV = cfg["GKV"]; DH = cfg["DH"]; FF = cfg["FF"]; FC = FF // 128
    NKEY = CTX + T; NCH = NKEY // 128; CT = CTX // 128; NT = T // 128
    NAP = NA // 2; GQP = GQ // 2
    W0 = (3 * NA + GQ + 2 * GKV) * 64
    W1 = 3 * DH * 128
    NCLS = 5
    lam_init = 0.8 - 0.6 * math.exp(-0.3 * 1)

    nc = bass.Bass("TRN2", target_bir_lowering=False)

    def din(name, shape, dt=F32):
        return nc.dram_tensor(name, list(shape), dt, kind="ExternalInput").ap()

    def dscr(name, shape, dt):
        return nc.dram_tensor(name, list(shape), dt, kind="ExternalOutput" if debug else "Internal").ap()

    x_in = din("x", [T, D]); ctx_in = din("ctx", [CTX, D])
    cvec = din("cvec", [128, 2 * KC])
    ada_w = din("ada_w", [2, D, 6 * D]); ada_b = din("ada_b", [2, 6 * D])
    w_gate = din("ffn_w_gate", [2, D, FF]); w_up = din("ffn_w_up", [2, D, FF]); w_down = din("ffn_w_down", [2, FF, D])
    w_in0 = din("par_w_in", [D, W0]); w_out0 = din("par_w_out", [D, D])
    w_in1 = din("diff_w_in", [D, W1]); w_out1 = din("diff_w_out", [D, D])
    qgain = din("gqa_q_gain", [1, 64]); kgain = din("gqa_k_gain", [1, 64])
    lamv = din("lamv", [1, 256]); subln = din("diff_subln_gain", [1, 128]); fgain = din("final_norm_gain", [1, D])
    cossin = din("cossin", [NKEY, 64])
    nabias = din("nabias", [NAP, 128, NCLS * 2 * 5 * 128])
    ident_in = din("ident", [128, 128], BF16)
    sel_in = din("sel", [128, 2])
    TO = T // 2 if split else T
    out_d = nc.dram_tensor("out", [TO, D], F32, kind="ExternalOutput").ap()

    XM = dscr("XM", [NKEY, D], F32); X1 = dscr("X1", [NKEY, D], F32)
    AO = dscr("AO", [NKEY, D], BF16)
    NAQT = dscr("NAQT", [NAP, 128, NKEY], BF16); NAKT = dscr("NAKT", [NAP, 128, NKEY], BF16)
    NAV = dscr("NAV", [NAP, 128, NCH * 130], BF16)
    GQT = dscr("GQT", [GQP, 128, NKEY], BF16); GKT = dscr("GKT", [GKV, 128, NKEY], BF16)
    GV = dscr("GV", [GKV, 128, NCH * 65], BF16)
    DQT = dscr("DQT", [DH, 128, NKEY], BF16); DKT = dscr("DKT", [DH, 128, NKEY], BF16)
    DV = dscr("DV", [DH, 128, NCH * 129], BF16)

    with ExitStack() as st:
        k = K(nc, st)
        ARENA = 204 * 1024
        arena_t = st.enter_context(nc.sbuf_tensor("arena", [128, ARENA], U8))
        ar = Arena(arena_t[:, :], ARENA)
        ps = st.enter_context(nc.psum_tensor("ps", [128, 4096], F32))

        def bank(i, n=1):
            return ps[:, i * 512:(i + n) * 512]

        PB = [Buf(f"psb{i}") for i in range(8)]

        ident = ar.alloc([128], BF16); B_ident = k.dbuf("ident")
        modrows = ar.alloc([6 * D], F32); B_mod = Buf("mod")
        csil = ar.alloc([2 * KC], F32); B_csil = k.dbuf("csil")
        ones_f = ar.alloc([128], F32); B_ones = Buf("ones")
        qg_r = ar.alloc([64], F32); kg_r = ar.alloc([64], F32); B_gains = k.dbuf("gains")
        lam_r = ar.alloc([256], F32); sub_r = ar.alloc([128], F32); fg_r = ar.alloc([D], F32)
        lam_s = ar.alloc([8], F32); B_lam = Buf("lam")
        sel = ar.alloc([2], F32)
        PERS = ar.off

        k.dma("sp", ident, ident_in, writes=[B_ident])
        k.dma("sp", csil, cvec, writes=[B_csil])
        k.dma("sp", qg_r, qgain.partition_broadcast(128), writes=[B_gains])
        k.dma("sp", kg_r, kgain.partition_broadcast(128), writes=[B_gains])
        k.dma("sp", lam_r, lamv.partition_broadcast(128), writes=[B_gains])
        k.dma("sp", sub_r, subln.partition_broadcast(128), writes=[B_gains])
        k.dma("sp", fg_r, fgain.partition_broadcast(128), writes=[B_gains])
        k.dma("sp", sel, sel_in, writes=[B_gains])
        k.op("dve", lambda e: e.memset(ones_f, 1.0), writes=[B_ones])
        k.op("act", lambda e: e.activation(out=csil, in_=csil, func=AF.Silu), reads=[B_csil], writes=[B_csil])
        lamtmp = ar.alloc([128], F32)
        PERS = ar.off
        k.op("dve", lambda e: e.tensor_tensor(out=lamtmp[:, 0:64], in0=lam_r[:, 0:64], in1=lam_r[:, 64:128], op=ALU.mult), reads=[B_gains], writes=[B_lam])
        k.op("dve", lambda e: e.tensor_tensor(out=lamtmp[:, 64:128], in0=lam_r[:, 128:192], in1=lam_r[:, 192:256], op=ALU.mult), reads=[B_gains], writes=[B_lam])
        k.op("dve", lambda e: e.tensor_reduce(out=lam_s[:, 0:2], in_=lamtmp.rearrange("p (a b) -> p a b", a=2), axis=AX.X, op=ALU.add), reads=[B_lam], writes=[B_lam])
        k.op("act", lambda e: e.activation(out=lam_s[:, 2:4], in_=lam_s[:, 0:2], func=AF.Exp), reads=[B_lam], writes=[B_lam])
        k.op("dve", lambda e: e.tensor_tensor(out=lam_s[:, 4:5], in0=lam_s[:, 3:4], in1=lam_s[:, 2:3], op=ALU.subtract), reads=[B_lam], writes=[B_lam])
        k.op("dve", lambda e: e.tensor_scalar(out=lam_s[:, 5:6], in0=lam_s[:, 4:5], scalar1=-lam_init, scalar2=None, op0=ALU.add), reads=[B_lam], writes=[B_lam])
        neglam = lam_s[:, 5:6]
        k.op("dve", lambda e: e.tensor_scalar(out=sub_r, in0=sub_r, scalar1=(1.0 - lam_init), scalar2=None, op0=ALU.mult), reads=[B_gains], writes=[B_gains])

        def cast_copy(i, out, in_, reads, writes):
            eng = ("pool", "dve", "act")[i % 3] if True else "dve"
            if eng == "act":
                k.op("act", lambda e: e.activation(out=out, in_=in_, func=AF.Copy), reads=reads, writes=writes)
            else:
                k.op(eng, lambda e: e.tensor_copy(out=out, in_=in_), reads=reads, writes=writes)

        def load_w(dst, src, kcs, n, stg, B_stg, B_dst):
            for kc in range(kcs):
                j = kc % 2
                k.dma("sp", stg[j][:, 0:n], src[kc * 128:(kc + 1) * 128, :], writes=[B_stg[j]])
                cast_copy(kc, dst[:, kc, :], stg[j][:, 0:n], [B_stg[j]], [B_dst])

        def ada_phase(l, cond, ncols):
            ar.off = PERS
            rep = ar.alloc([KC, 128], F32); B_rep = Buf("rep")
            wst = [ar.alloc([KC, 512], F32) for _ in range(2)]; B_wst = [k.dbuf(f"adaw{j}") for j in range(2)]
            bst = [ar.alloc([512], F32) for _ in range(2)]; B_bst = [k.dbuf(f"adab{j}") for j in range(2)]
            for kc in range(KC):
                k.op("dve", lambda e, kc=kc: e.tensor_scalar(out=rep[:, kc, :], in0=ones_f, scalar1=csil[:, cond * KC + kc:cond * KC + kc + 1], scalar2=None, op0=ALU.mult),
                     reads=[B_ones, B_csil], writes=[B_rep])
            nb = ncols // 512
            for n in range(nb):
                j = n % 2
                k.dma("sp", wst[j], ada_w[l, :, n * 512:(n + 1) * 512].rearrange("(a p) n -> p a n", p=128), writes=[B_wst[j]])
                k.dma("sp", bst[j], ada_b[l:l + 1, n * 512:(n + 1) * 512].partition_broadcast(128), writes=[B_bst[j]])
                pb = n % 2
                for kc in range(KC):
                    k.op("pe", lambda e, kc=kc, j=j, pb=pb: e.matmul(bank(pb), lhsT=rep[:, kc, :], rhs=wst[j][:, kc, :], start=(kc == 0), stop=(kc == KC - 1)),
                         reads=[B_rep, B_wst[j]], writes=[PB[pb]])
                k.op("dve", lambda e, n=n, j=j, pb=pb: e.tensor_tensor(out=modrows[:, n * 512:(n + 1) * 512], in0=bank(pb), in1=bst[j], op=ALU.add),
                     reads=[PB[pb], B_bst[j]], writes=[B_mod])
            for off in (D, 4 * D):
                if off < ncols:
                    k.op("dve", lambda e, off=off: e.tensor_scalar(out=modrows[:, off:off + D], in0=modrows[:, off:off + D], scalar1=1.0, scalar2=None, op0=ALU.add),
                         reads=[B_mod], writes=[B_mod])
            k.barrier()

        def rstd_chain(ss, n_inv, nrm_bufs):
            B = nrm_bufs
            w = ss.shape[1] // 3
            k.op("dve", lambda e: e.tensor_scalar(out=ss[:, w:2 * w], in0=ss[:, 0:w], scalar1=n_inv, scalar2=EPS, op0=ALU.mult, op1=ALU.add), reads=[B], writes=[B])
            k.op("act", lambda e: e.activation(out=ss[:, w:2 * w], in_=ss[:, w:2 * w], func=AF.Sqrt), reads=[B], writes=[B])
            k.op("dve", lambda e: e.reciprocal(out=ss[:, 2 * w:3 * w], in_=ss[:, w:2 * w]), reads=[B], writes=[B])

        def norm_mod_T(xt, B_x, sc_off, sh_off, junk, B_junk, ss, B_ss, tmp, B_tmp, hb, B_hb, hT_dst, B_hT, pbank):
            k.op("act", lambda e: e.activation(out=junk, in_=xt, func=AF.Square, accum_out=ss[:, 0:1]), reads=[B_x], writes=[B_junk, B_ss])
            rstd_chain(ss, 1.0 / D, B_ss)
            k.op("dve", lambda e: e.scalar_tensor_tensor(out=tmp, in0=xt, scalar=ss[:, 2:3], in1=modrows[:, sc_off:sc_off + D], op0=ALU.mult, op1=ALU.mult),
                 reads=[B_x, B_ss, B_mod], writes=[B_tmp])
            k.op("pool", lambda e: e.tensor_tensor(out=hb, in0=tmp, in1=modrows[:, sh_off:sh_off + D], op=ALU.add), reads=[B_tmp, B_mod], writes=[B_hb])
            pt = bank(pbank).bitcast(BF16)
            for kc in range(KC):
                k.op("pe", lambda e, kc=kc: e.transpose(out=pt[:, kc * 128:(kc + 1) * 128], in_=hb[:, kc * 128:(kc + 1) * 128], identity=ident),
                     reads=[B_hb, B_ident], writes=[PB[pbank]])
            k.op("act", lambda e: e.activation(out=hT_dst, in_=pt[:, 0:KC * 128].rearrange("p (a b) -> p a b", a=KC), func=AF.Copy), reads=[PB[pbank]], writes=[B_hT])

        def src_tile(l, i):
            if l == 0:
                return ctx_in[i * 128:(i + 1) * 128, :] if i < CT else x_in[(i - CT) * 128:(i - CT + 1) * 128, :]
            return X1[i * 128:(i + 1) * 128, :]

        def proj_phase(l, tiles):
            ar.off = PERS
            WW = W0 if l == 0 else W1
            w_src = w_in0 if l == 0 else w_in1
            wsb = ar.alloc([KC, WW], BF16); B_w = Buf("w_in")
            mark = ar.off
            stg = [ar.alloc([WW], F32) for _ in range(2)]; B_stg = [k.dbuf(f"wstg{j}") for j in range(2)]
            load_w(wsb, w_src, KC, WW, stg, B_stg, B_w)
            k.barrier()
            ar.off = mark
            xt = [ar.alloc([D], F32) for _ in range(2)]; B_x = [k.dbuf(f"px{j}") for j in range(2)]
            cst = [ar.alloc([64], F32) for _ in range(2)]; B_cs = [k.dbuf(f"pcs{j}") for j in range(2)]
            junk = ar.alloc([D], F32); B_junk = Buf("junk")
            ss = [ar.alloc([3], F32) for _ in range(2)]; B_ss = [Buf(f"ss{j}") for j in range(2)]
            tmp = ar.alloc([D], F32); B_tmp = Buf("tmp")
            hb = [ar.alloc([D], BF16) for _ in range(2)]; B_hb = [Buf(f"hb{j}") for j in range(2)]
            hT = [ar.alloc([KC, 128], BF16) for _ in range(2)]; B_hT = [Buf(f"hT{j}") for j in range(2)]
            sq = [ar.alloc([512], F32) for _ in range(2)]; B_sq = [Buf(f"sq{j}") for j in range(2)]
            qn = [ar.alloc([512], F32) for _ in range(2)]; B_qn = [Buf(f"qn{j}") for j in range(2)]
            ra = [ar.alloc([512], F32) for _ in range(2)]; B_ra = [Buf(f"ra{j}") for j in range(2)]
            rb = [ar.alloc([512], F32) for _ in range(2)]; B_rb = [Buf(f"rb{j}") for j in range(2)]
            nss = [ar.alloc([24], F32) for _ in range(2)]; B_nss = [Buf(f"nss{j}") for j in range(2)]
            tm = [ar.alloc([512], BF16) for _ in range(3)]; B_tm = [Buf(f"tm{j}") for j in range(3)]
            stT = [ar.alloc([4, 128], BF16) for _ in range(3)]; B_stT = [k.dbuf(f"stT{j}_{l}{int(tiles[0] < CT)}") for j in range(3)]
            vw = 65 if l == 0 else 129
            nvh = (NA + GKV) if l == 0 else DH
            vst = [ar.alloc([nvh, vw], BF16) for _ in range(2)]; B_vst = [k.dbuf(f"vst{j}_{l}{int(tiles[0] < CT)}") for j in range(2)]
            for j in range(2):
                k.op("pool", lambda e, j=j: e.memset(vst[j], 1.0), writes=[B_vst[j]])
                k.fence(B_vst[j])
            cnt = {"pj": 0, "tp": 0, "pp": 0, "tm": 0, "st": 0}

            def post_qk(pj, nm, s0, dsts, norm_gain, do_rope, csb, B_csb, dup=False):
                w = nm * 64
                src = bank(pj)[:, 0:w]
                srcB = PB[pj]
                pp = cnt["pp"] % 2; cnt["pp"] += 1
                if norm_gain is not None:
                    k.op("act", lambda e: e.activation(out=sq[pp][:, 0:w], in_=src, func=AF.Square), reads=[srcB], writes=[B_sq[pp]])
                    k.op("dve", lambda e: e.tensor_reduce(out=nss[pp][:, 0:nm], in_=sq[pp][:, 0:w].rearrange("p (a b) -> p a b", a=nm), axis=AX.X, op=ALU.add),
                         reads=[B_sq[pp]], writes=[B_nss[pp]])
                    k.op("dve", lambda e: e.tensor_scalar(out=nss[pp][:, 8:8 + nm], in0=nss[pp][:, 0:nm], scalar1=1.0 / 64, scalar2=EPS, op0=ALU.mult, op1=ALU.add), reads=[B_nss[pp]], writes=[B_nss[pp]])
                    k.op("act", lambda e: e.activation(out=nss[pp][:, 8:8 + nm], in_=nss[pp][:, 8:8 + nm], func=AF.Sqrt), reads=[B_nss[pp]], writes=[B_nss[pp]])
                    k.op("dve", lambda e: e.reciprocal(out=nss[pp][:, 16:16 + nm], in_=nss[pp][:, 8:8 + nm]), reads=[B_nss[pp]], writes=[B_nss[pp]])
                    k.op("dve", lambda e: e.tensor_tensor(out=qn[pp][:, 0:w].rearrange("p (a b) -> p a b", a=nm), in0=src.rearrange("p (a b) -> p a b", a=nm),
                                                          in1=nss[pp][:, 16:16 + nm][:, :, None].to_broadcast([128, nm, 64]), op=ALU.mult),
                         reads=[srcB, B_nss[pp]], writes=[B_qn[pp]])
                    k.op("pool", lambda e: e.tensor_tensor(out=qn[pp][:, 0:w].rearrange("p (a b) -> p a b", a=nm), in0=qn[pp][:, 0:w].rearrange("p (a b) -> p a b", a=nm),
                                                           in1=norm_gain[:, None, :].to_broadcast([128, nm, 64]), op=ALU.mult),
                         reads=[B_qn[pp], B_gains], writes=[B_qn[pp]])
                    src = qn[pp][:, 0:w]; srcB = B_qn[pp]
                ti = cnt["tm"] % 3; cnt["tm"] += 1
                if do_rope:
                    s4 = src.rearrange("p (a b c) -> p a b c", a=nm, c=2)
                    cosb = csb[:, 0:32][:, None, :, None].to_broadcast([128, nm, 32, 2])
                    sinb = csb[:, 32:64][:, None, :, None].to_broadcast([128, nm, 32, 2])
                    A = ra[pp][:, 0:w].rearrange("p (a b c) -> p a b c", a=nm, c=2)
                    Bm = rb[pp][:, 0:w].rearrange("p (a b c) -> p a b c", a=nm, c=2)
                    o4 = tm[ti][:, 0:w].rearrange("p (a b c) -> p a b c", a=nm, c=2)
                    k.op("dve", lambda e: e.tensor_tensor(out=A, in0=s4, in1=cosb, op=ALU.mult), reads=[srcB, B_csb], writes=[B_ra[pp]])
                    k.op("dve", lambda e: e.tensor_tensor(out=Bm, in0=s4, in1=sinb, op=ALU.mult), reads=[srcB, B_csb], writes=[B_rb[pp]])
                    k.op("pool", lambda e: e.tensor_tensor(out=o4[:, :, :, 0], in0=A[:, :, :, 0], in1=Bm[:, :, :, 1], op=ALU.subtract), reads=[B_ra[pp], B_rb[pp]], writes=[B_tm[ti]])
                    k.op("pool", lambda e: e.tensor_tensor(out=o4[:, :, :, 1], in0=Bm[:, :, :, 0], in1=A[:, :, :, 1], op=ALU.add), reads=[B_ra[pp], B_rb[pp]], writes=[B_tm[ti]])
                else:
                    k.op("act", lambda e: e.activation(out=tm[ti][:, 0:w], in_=src, func=AF.Copy), reads=[srcB], writes=[B_tm[ti]])
                if dup:
                    chunks = [(m * 64, 64) for m in range(nm)]
                else:
                    chunks = [(c * 128, 128) for c in range(w // 128)]
                tb = 4 + cnt["tp"] % 2; cnt["tp"] += 1
                ptb = bank(tb).bitcast(BF16)
                sti = cnt["st"] % 3; cnt["st"] += 1
                for ci, (c0, cw) in enumerate(chunks):
                    if dup:
                        for hlf in range(2):
                            k.op("pe", lambda e, ci=ci, c0=c0, hlf=hlf: e.transpose(out=ptb[hlf * 64:(hlf + 1) * 64, ci * 128:(ci + 1) * 128], in_=tm[ti][:, c0:c0 + 64], identity=ident),
                                 reads=[B_tm[ti], B_ident], writes=[PB[tb]])
                    else:
                        k.op("pe", lambda e, ci=ci, c0=c0: e.transpose(out=ptb[:, ci * 128:(ci + 1) * 128], in_=tm[ti][:, c0:c0 + 128], identity=ident),
                             reads=[B_tm[ti], B_ident], writes=[PB[tb]])
                nch_ = len(chunks)
                k.op("dve", lambda e: e.tensor_copy(out=stT[sti][:, 0:nch_, :], in_=ptb[:, 0:nch_ * 128].rearrange("p (a b) -> p a b", a=nch_)), reads=[PB[tb]], writes=[B_stT[sti]])
                for ci in range(nch_):
                    k.dma("sp", dsts[ci], stT[sti][:, ci, :], reads=[B_stT[sti]])

            def stageA(it):
                i = tiles[it]
                j = it % 2
                s0 = i * 128
                k.dma("sp", xt[j], src_tile(l, i), writes=[B_x[j]])
                k.dma("sp", cst[j], cossin[s0:s0 + 128, :], writes=[B_cs[j]])
                norm_mod_T(xt[j], B_x[j], 1 * D, 0, junk, B_junk, ss[j], B_ss[j], tmp, B_tmp, hb[j], B_hb[j], hT[j], B_hT[j], 0)

            stageA(0)
            for it, i in enumerate(tiles):
                j = it % 2
                s0 = i * 128
                if it + 1 < len(tiles):
                    stageA(it + 1)
                if l == 0:
                    blocks = []
                    c = 0
                    for nm_total, kind in ((NA, "naq"), (NA, "nak"), (NA, "nav"), (GQ, "gq"), (GKV, "gk"), (GKV, "gv")):
                        m0 = 0
                        while m0 < nm_total:
                            nm = min(8, nm_total - m0)
                            blocks.append((kind, m0, nm, c + m0 * 64))
                            m0 += nm
                        c += nm_total * 64
                else:
                    blocks = []
                    for kind, base in (("dq", 0), ("dk", DH * 128), ("dv", 2 * DH * 128)):
                        m0 = 0
                        while m0 < 2 * DH:
                            nm = min(8, 2 * DH - m0)
                            blocks.append((kind, m0, nm, base + m0 * 64))
                            m0 += nm
                vj = it % 2
                for (kind, m0, nm, c0) in blocks:
                    if kind == "dq" and i < CT:
                        continue
                    w = nm * 64
                    pj = 1 + cnt["pj"] % 3; cnt["pj"] += 1
                    for kc in range(KC):
                        k.op("pe", lambda e, kc=kc, pj=pj, c0=c0, w=w, j=j: e.matmul(bank(pj)[:, 0:w], lhsT=hT[j][:, kc, :], rhs=wsb[:, kc, c0:c0 + w], start=(kc == 0), stop=(kc == KC - 1)),
                             reads=[B_hT[j], B_w], writes=[PB[pj]])
                    if kind in ("naq", "nak"):
                        dst = NAQT if kind == "naq" else NAKT
                        post_qk(pj, nm, s0, [dst[(m0 // 2) + ci, :, s0:s0 + 128] for ci in range(nm // 2)], None, False, None, None)
                    elif kind == "gq":
                        post_qk(pj, nm, s0, [GQT[(m0 // 2) + ci, :, s0:s0 + 128] for ci in range(nm // 2)], qg_r, True, cst[j], B_cs[j])
                    elif kind == "gk":
                        post_qk(pj, nm, s0, [GKT[m0 + ci, :, s0:s0 + 128] for ci in range(nm)], kg_r, True, cst[j], B_cs[j], dup=True)
                    elif kind in ("dq", "dk"):
                        dst = DQT if kind == "dq" else DKT
                        post_qk(pj, nm, s0, [dst[(m0 // 2) + ci, :, s0:s0 + 128] for ci in range(nm // 2)], None, True, cst[j], B_cs[j])
                    elif kind in ("nav", "gv"):
                        h0 = m0 if kind == "nav" else NA + m0
                        k.op("act", lambda e, pj=pj, h0=h0, nm=nm, vj=vj: e.activation(out=vst[vj][:, h0:h0 + nm, 0:64], in_=bank(pj)[:, 0:nm * 64].rearrange("p (a b) -> p a b", a=nm), func=AF.Copy),
                             reads=[PB[pj]], writes=[B_vst[vj]])
                    elif kind == "dv":
                        h0 = m0 // 2
                        k.op("act", lambda e, pj=pj, h0=h0, nm=nm, vj=vj: e.activation(out=vst[vj][:, h0:h0 + nm // 2, 0:128], in_=bank(pj)[:, 0:nm * 64].rearrange("p (a b) -> p a b", a=nm // 2), func=AF.Copy),
                             reads=[PB[pj]], writes=[B_vst[vj]])
                if l == 0:
                    for p in range(NAP):
                        k.dma("sp", NAV[p, :, i * 130:(i + 1) * 130], vst[vj][:, 2 * p:2 * p + 2, :].rearrange("p a b -> p (a b)"), reads=[B_vst[vj]])
                    for g in range(GKV):
                        k.dma("sp", GV[g, :, i * 65:(i + 1) * 65], vst[vj][:, NA + g, :], reads=[B_vst[vj]])
                else:
                    for h in range(DH):
                        k.dma("sp", DV[h, :, i * 129:(i + 1) * 129], vst[vj][:, h, :], reads=[B_vst[vj]])
            k.barrier()

        def attn_phase(units, qtiles, chunks, finish_kind, qblend=None):
            ar.off = PERS
            vtot_max = max(u["vtot"] for u in units)
            KTs = [ar.alloc([NKEY], BF16) for _ in range(2)]; B_KT = [k.dbuf(f"aKT{j}") for j in range(2)]
            Vs = [ar.alloc([NCH * vtot_max], BF16) for _ in range(2)]; B_V = [k.dbuf(f"aV{j}") for j in range(2)]
            QTs = [ar.alloc([512], BF16) for _ in range(2)]; B_QT = [k.dbuf(f"aQT{j}") for j in range(2)]
            QBs = [ar.alloc([512], BF16) for _ in range(2)]; B_QB = [k.dbuf(f"aQB{j}") for j in range(2)]
            QMs = [ar.alloc([512], BF16) for _ in range(2)]; B_QM = [Buf(f"aQM{j}") for j in range(2)]
            Ps = [ar.alloc([2, 512], BF16) for _ in range(3)]; B_P = [Buf(f"aP{j}") for j in range(3)]
            rc = ar.alloc([16], F32); B_rc = Buf("rc")
            stg = [ar.alloc([4, 128], BF16) for _ in range(2)]; B_stg = [k.dbuf(f"aStg{j}") for j in range(2)]
            t1 = [ar.alloc([128], F32) for _ in range(2)]; B_t1 = [Buf(f"t1{j}") for j in range(2)]
            o1 = [ar.alloc([128], F32) for _ in range(2)]; B_o1 = [Buf(f"o1{j}") for j in range(2)]
            jnk = ar.alloc([128], F32); B_jnk = Buf("ajnk")
            ssd = [ar.alloc([3], F32) for _ in range(2)]; B_ssd = [Buf(f"ssd{j}") for j in range(2)]
            B_S = [Buf("S0"), Buf("S1")]
            B_O = Buf("O")
            cn = {"q": 0, "s": 0, "p": 0, "st": 0, "d": 0}
            for ui, u in enumerate(units):
                kj = ui % 2
                vt = u["vtot"]
                nck = max(chunks) + 1
                k.dma("sp", KTs[kj][:, 0:nck * 128], u["KT"][:, 0:nck * 128], writes=[B_KT[kj]])
                k.dma("sp", Vs[kj][:, 0:nck * vt], u["V"][:, 0:nck * vt], writes=[B_V[kj]])
                Vv = Vs[kj][:, 0:NCH * vt].rearrange("p (c w) -> p c w", w=vt)
                wmax = max(u["vsl"][0][1], u["vsl"][1][1])
                per_bank = 512 // wmax
                for (s0, nq) in qtiles:
                    nsub = nq // 128
                    qj = cn["q"] % 2; cn["q"] += 1
                    k.dma("sp", QTs[qj][:, 0:nq], u["QT"][:, s0:s0 + nq], writes=[B_QT[qj]])
                    if qblend is not None:
                        k.dma("sp", QBs[qj][:, 0:nq], u["QT"][:, s0 + qblend:s0 + qblend + nq], writes=[B_QB[qj]])
                        k.op("dve", lambda e: e.tensor_scalar(out=QMs[qj][:, 0:nq], in0=QTs[qj][:, 0:nq], scalar1=sel[:, 0:1], scalar2=None, op0=ALU.mult),
                             reads=[B_QT[qj], B_gains], writes=[B_QM[qj]])
                        k.op("dve", lambda e: e.scalar_tensor_tensor(out=QTs[qj][:, 0:nq], in0=QBs[qj][:, 0:nq], scalar=sel[:, 1:2], in1=QMs[qj][:, 0:nq], op0=ALU.mult, op1=ALU.add),
                             reads=[B_QB[qj], B_QM[qj], B_gains], writes=[B_QT[qj]])
                    accs = []
                    for a in range(nsub * 2):
                        b = 4 + a // per_bank
                        o = (a % per_bank) * wmax
                        accs.append((b, o))

                    def s_mm(c, sj):
                        for m in range(2):
                            k.op("pe", lambda e, c=c, m=m, sj=sj: e.matmul(bank(2 * sj + m)[:, 0:nq], lhsT=KTs[kj][m * 64:(m + 1) * 64, c * 128:(c + 1) * 128],
                                                                           rhs=QTs[qj][m * 64:(m + 1) * 64, 0:nq], start=True, stop=True),
                                 reads=[B_KT[kj], B_QT[qj]], writes=[B_S[sj]])

                    sidx = cn["s"]
                    s_mm(chunks[0], sidx % 2)
                    for ci, c in enumerate(chunks):
                        sj = (sidx + ci) % 2
                        if ci + 1 < len(chunks):
                            s_mm(chunks[ci + 1], (sidx + ci + 1) % 2)
                        pj = cn["p"] % 3; cn["p"] += 1
                        k.op("act", lambda e, sj=sj, pj=pj: e.activation(out=Ps[pj][:, :, 0:nq], in_=bank(2 * sj, 2).rearrange("p (a b) -> p a b", a=2)[:, :, 0:nq], func=AF.Exp, scale=0.125),
                             reads=[B_S[sj]], writes=[B_P[pj]])
                        seen = set()
                        for a in range(nsub * 2):
                            uu, m = a // 2, a % 2
                            b, o = accs[a]
                            off, w = u["vsl"][m]
                            first_in_bank = (ci == 0) and (b not in seen)
                            seen.add(b)
                            k.op("pe", lambda e, b=b, o=o, w=w, off=off, m=m, uu=uu, pj=pj, c=c, fib=first_in_bank, last=(ci == len(chunks) - 1):
                                 e.matmul(bank(b)[:, o:o + w], lhsT=Ps[pj][:, m, uu * 128:(uu + 1) * 128], rhs=Vv[:, c, off:off + w], start=fib, stop=last, skip_group_check=True),
                                 reads=[B_P[pj], B_V[kj]], writes=[B_O])
                    cn["s"] += len(chunks)
                    sti = cn["st"] % 2; cn["st"] += 1
                    for uu in range(nsub):
                        (b0, o0), (b1, o1_) = accs[2 * uu], accs[2 * uu + 1]
                        w0 = u["vsl"][0][1]; w1 = u["vsl"][1][1]
                        k.op("dve", lambda e, b0=b0, o0=o0, w0=w0, uu=uu: e.reciprocal(out=rc[:, 2 * uu:2 * uu + 1], in_=bank(b0)[:, o0 + w0 - 1:o0 + w0]), reads=[B_O], writes=[B_rc])
                        k.op("dve", lambda e, b1=b1, o1_=o1_, w1=w1, uu=uu: e.reciprocal(out=rc[:, 2 * uu + 1:2 * uu + 2], in_=bank(b1)[:, o1_ + w1 - 1:o1_ + w1]), reads=[B_O], writes=[B_rc])
                        if finish_kind == "gqa":
                            k.op("dve", lambda e, b0=b0, o0=o0, uu=uu, sti=sti: e.tensor_scalar(out=stg[sti][:, uu, 0:64], in0=bank(b0)[:, o0:o0 + 64], scalar1=rc[:, 2 * uu:2 * uu + 1], scalar2=None, op0=ALU.mult),
                                 reads=[B_O, B_rc], writes=[B_stg[sti]])
                            k.op("dve", lambda e, b1=b1, o1_=o1_, uu=uu, sti=sti: e.tensor_scalar(out=stg[sti][:, uu, 64:128], in0=bank(b1)[:, o1_:o1_ + 64], scalar1=rc[:, 2 * uu + 1:2 * uu + 2], scalar2=None, op0=ALU.mult),
                                 reads=[B_O, B_rc], writes=[B_stg[sti]])
                        else:
                            dj = cn["d"] % 2; cn["d"] += 1
                            k.op("dve", lambda e, uu=uu: e.tensor_tensor(out=rc[:, 8 + uu:9 + uu], in0=rc[:, 2 * uu + 1:2 * uu + 2], in1=neglam, op=ALU.mult), reads=[B_rc, B_lam], writes=[B_rc])
                            k.op("dve", lambda e, b0=b0, o0=o0, uu=uu, dj=dj: e.tensor_scalar(out=t1[dj], in0=bank(b0)[:, o0:o0 + 128], scalar1=rc[:, 2 * uu:2 * uu + 1], scalar2=None, op0=ALU.mult),
                                 reads=[B_O, B_rc], writes=[B_t1[dj]])
                            k.op("dve", lambda e, b1=b1, o1_=o1_, uu=uu, dj=dj: e.scalar_tensor_tensor(out=o1[dj], in0=bank(b1)[:, o1_:o1_ + 128], scalar=rc[:, 8 + uu:9 + uu], in1=t1[dj], op0=ALU.mult, op1=ALU.add),
                                 reads=[B_O, B_rc, B_t1[dj]], writes=[B_o1[dj]])
                            k.op("act", lambda e, dj=dj: e.activation(out=jnk, in_=o1[dj], func=AF.Square, accum_out=ssd[dj][:, 0:1]), reads=[B_o1[dj]], writes=[B_jnk, B_ssd[dj]])
                            rstd_chain(ssd[dj], 1.0 / 128, B_ssd[dj])
                            k.op("dve", lambda e, uu=uu, dj=dj, sti=sti: e.scalar_tensor_tensor(out=stg[sti][:, uu, :], in0=o1[dj], scalar=ssd[dj][:, 2:3], in1=sub_r, op0=ALU.mult, op1=ALU.mult),
                                 reads=[B_o1[dj], B_ssd[dj], B_gains], writes=[B_stg[sti]])
                    k.dma("sp", AO[s0:s0 + nq, u["col"]:u["col"] + 128].rearrange("(u p) c -> p u c", p=128), stg[sti][:, 0:nsub, :], reads=[B_stg[sti]])
            k.barrier()

        def na_phase():
            ar.off = PERS
            KTs = [ar.alloc([NKEY], BF16) for _ in range(2)]; B_KT = [k.dbuf(f"nKT{j}") for j in range(2)]
            Vs = [ar.alloc([NCH, 130], BF16) for _ in range(2)]; B_V = [k.dbuf(f"nV{j}") for j in range(2)]
            QTs = [ar.alloc([T], BF16) for _ in range(2)]; B_QT = [k.dbuf(f"nQT{j}") for j in range(2)]
            bst = ar.alloc([NCLS * 2 * 5 * 128], F32); B_bst = k.dbuf("nbst")
            Em = [ar.alloc([NCLS, 2, 640], BF16) for _ in range(2)]; B_E = [Buf(f"nE{j}") for j in range(2)]
            Ps = [ar.alloc([7 * 128], BF16) for _ in range(3)]; B_P = [Buf(f"nP{j}") for j in range(3)]
            rc = [ar.alloc([2], F32) for _ in range(2)]; B_rc = [Buf(f"nrc{j}") for j in range(2)]
            stg = [ar.alloc([4, 128], BF16) for _ in range(2)]; B_stg = [k.dbuf(f"nStg{j}") for j in range(2)]
            B_S = [Buf(f"nS{j}") for j in range(3)]
            B_O = [Buf(f"nO{j}") for j in range(2)]
            cn = {"s": 0, "p": 0, "o": 0}
            for p in range(NAP):
                kj = p % 2
                k.dma("sp", KTs[kj], NAKT[p], writes=[B_KT[kj]])
                k.dma("sp", Vs[kj].rearrange("p a b -> p (a b)"), NAV[p], writes=[B_V[kj]])
                k.dma("sp", QTs[kj], NAQT[p, :, CTX:NKEY], writes=[B_QT[kj]])
                k.dma("sp", bst, nabias[p], writes=[B_bst])
                k.op("act", lambda e, kj=kj: e.activation(out=Em[kj].rearrange("p a b c -> p (a b c)"), in_=bst, func=AF.Exp), reads=[B_bst], writes=[B_E[kj]])
                def unit_info(u):
                    r0 = 2 * u
                    csr = min(min(max(r0 - 4, 0), ROWS - 8), ROWS - 10)
                    cls = {0: 1, 2: 2, ROWS - 4: 3, ROWS - 2: 4}.get(r0, 0)
                    cidx = [CT + csr // 2 + j for j in range(5)] + list(range(CT))
                    return cls, cidx

                units_ = [(u, m) for u in range(NT) for m in range(2)]
                sbase = cn["s"]

                def s_stage(ix):
                    u, m = units_[ix]
                    cls, cidx = unit_info(u)
                    sj = (sbase + ix) % 3
                    Sb = bank(2 * sj, 2)
                    for j, c in enumerate(cidx):
                        k.op("pe", lambda e, j=j, c=c: e.matmul(Sb[:, j * 128:(j + 1) * 128], lhsT=KTs[kj][m * 64:(m + 1) * 64, c * 128:(c + 1) * 128],
                                                             rhs=QTs[kj][m * 64:(m + 1) * 64, u * 128:(u + 1) * 128], start=True, stop=True),
                             reads=[B_KT[kj], B_QT[kj]], writes=[B_S[sj]])

                s_stage(0)
                for ix, (u, m) in enumerate(units_):
                    cls, cidx = unit_info(u)
                    sti = (u // 4) % 2
                    sj = (sbase + ix) % 3
                    Sb = bank(2 * sj, 2)
                    if ix + 1 < len(units_):
                        s_stage(ix + 1)
                    pj = cn["p"] % 3; cn["p"] += 1
                    oj = cn["o"] % 2; cn["o"] += 1
                    k.op("act", lambda e: e.activation(out=Ps[pj], in_=Sb[:, 0:7 * 128], func=AF.Exp, scale=0.125), reads=[B_S[sj]], writes=[B_P[pj]])
                    k.op("dve", lambda e: e.tensor_tensor(out=Ps[pj][:, 0:640], in0=Ps[pj][:, 0:640], in1=Em[kj][:, cls, m, :], op=ALU.mult),
                         reads=[B_P[pj], B_E[kj]], writes=[B_P[pj]])
                    Ob = bank(6 + oj)
                    for j, c in enumerate(cidx):
                        k.op("pe", lambda e, j=j, c=c: e.matmul(Ob[:, 0:65], lhsT=Ps[pj][:, j * 128:(j + 1) * 128], rhs=Vs[kj][:, c, m * 65:(m + 1) * 65], start=(j == 0), stop=(j == 6)),
                             reads=[B_P[pj], B_V[kj]], writes=[B_O[oj]])
                    k.op("dve", lambda e: e.reciprocal(out=rc[oj][:, 0:1], in_=Ob[:, 64:65]), reads=[B_O[oj]], writes=[B_rc[oj]])
                    k.op("dve", lambda e: e.tensor_scalar(out=stg[sti][:, u % 4, m * 64:(m + 1) * 64], in0=Ob[:, 0:64], scalar1=rc[oj][:, 0:1], scalar2=None, op0=ALU.mult),
                         reads=[B_O[oj], B_rc[oj]], writes=[B_stg[sti]])
                    if m == 1 and u % 4 == 3:
                        s0 = CTX + (u - 3) * 128
                        k.dma("sp", AO[s0:s0 + 512, p * 128:(p + 1) * 128].rearrange("(u p) c -> p u c", p=128), stg[sti], reads=[B_stg[sti]])
                cn["s"] = sbase + len(units_)
            k.barrier()

        def wout_phase(l, tiles, dst, xblend=None):
            ar.off = PERS
            w_src = w_out0 if l == 0 else w_out1
            wsb = ar.alloc([KC, D], BF16); B_w = Buf("w_out")
            mark = ar.off
            stg = [ar.alloc([D], F32) for _ in range(2)]; B_stg = [k.dbuf(f"wostg{j}") for j in range(2)]
            load_w(wsb, w_src, KC, D, stg, B_stg, B_w)
            k.barrier()
            ar.off = mark
            ao = [ar.alloc([D], BF16) for _ in range(2)]; B_ao = [k.dbuf(f"ao{j}") for j in range(2)]
            aT = [ar.alloc([KC, 128], BF16) for _ in range(2)]; B_aT = [Buf(f"aT{j}") for j in range(2)]
            xt = [ar.alloc([D], F32) for _ in range(2)]; B_x = [k.dbuf(f"wx{j}") for j in range(2)]
            tmp = [ar.alloc([D], F32) for _ in range(2)]; B_tmp = [Buf(f"wtmp{j}") for j in range(2)]
            xb = [ar.alloc([D], F32) for _ in range(2)]; B_xb = [k.dbuf(f"wxb{j}") for j in range(2)]
            NB = (D + 511) // 512
            def stageA(it):
                i = tiles[it]
                j = it % 2
                k.dma("sp", ao[j], AO[i * 128:(i + 1) * 128, :], writes=[B_ao[j]])
                k.dma("sp", xt[j], src_tile(l, i), writes=[B_x[j]])
                if xblend is not None:
                    k.dma("sp", xb[j], src_tile(l, i + xblend), writes=[B_xb[j]])
                    k.op("dve", lambda e: e.tensor_scalar(out=xt[j], in0=xt[j], scalar1=sel[:, 0:1], scalar2=None, op0=ALU.mult), reads=[B_x[j], B_gains], writes=[B_x[j]])
                    k.op("dve", lambda e: e.scalar_tensor_tensor(out=xt[j], in0=xb[j], scalar=sel[:, 1:2], in1=xt[j], op0=ALU.mult, op1=ALU.add), reads=[B_xb[j], B_x[j], B_gains], writes=[B_x[j]])
                tb = j
                ptb = bank(tb).bitcast(BF16)
                for kc in range(KC):
                    k.op("pe", lambda e, kc=kc: e.transpose(out=ptb[:, kc * 128:(kc + 1) * 128], in_=ao[j][:, kc * 128:(kc + 1) * 128], identity=ident),
                         reads=[B_ao[j], B_ident], writes=[PB[tb]])
                k.op("act", lambda e: e.activation(out=aT[j], in_=ptb[:, 0:KC * 128].rearrange("p (a b) -> p a b", a=KC), func=AF.Copy), reads=[PB[tb]], writes=[B_aT[j]])

            stageA(0)
            for it, i in enumerate(tiles):
                j = it % 2
                if it + 1 < len(tiles):
                    stageA(it + 1)
                yb = 2 + 2 * j
                for nb in range(NB):
                    cw = min(512, D - nb * 512)
                    for kc in range(KC):
                        k.op("pe", lambda e, kc=kc, j=j, nb=nb, cw=cw, yb=yb: e.matmul(bank(yb + nb)[:, 0:cw], lhsT=aT[j][:, kc, :], rhs=wsb[:, kc, nb * 512:nb * 512 + cw], start=(kc == 0), stop=(kc == KC - 1)),
                             reads=[B_aT[j], B_w], writes=[PB[yb + nb]])
                    k.op("dve", lambda e, j=j, nb=nb, cw=cw, yb=yb: e.tensor_tensor(out=tmp[j][:, nb * 512:nb * 512 + cw], in0=bank(yb + nb)[:, 0:cw], in1=modrows[:, 2 * D + nb * 512:2 * D + nb * 512 + cw], op=ALU.mult),
                         reads=[PB[yb + nb], B_mod], writes=[B_tmp[j]])
                k.op("pool", lambda e, j=j: e.tensor_tensor(out=xt[j], in0=xt[j], in1=tmp[j], op=ALU.add), reads=[B_tmp[j], B_x[j]], writes=[B_x[j]])
                k.dma("sp", dst[i * 128:(i + 1) * 128, :], xt[j], reads=[B_x[j]])
            k.barrier()

        def ffn_phase(l, tiles, src, dst, final):
            ar.off = PERS
            wg = ar.alloc([KC, FF], BF16); wu = ar.alloc([KC, FF], BF16); wd = ar.alloc([FC, D], BF16)
            B_wg = Buf("wg"); B_wu = Buf("wu"); B_wd = Buf("wd")
            mark = ar.off
            stg = [ar.alloc([max(FF, D)], F32) for _ in range(2)]; B_stg = [k.dbuf(f"fstg{j}") for j in range(2)]
            load_w(wg, w_gate[l], KC, FF, stg, B_stg, B_wg)
            load_w(wu, w_up[l], KC, FF, stg, B_stg, B_wu)
            load_w(wd, w_down[l], FC, D, stg, B_stg, B_wd)
            k.barrier()
            ar.off = mark
            G = 2
            xg = ar.alloc([G, D], F32); B_xg = [k.dbuf(f"fx{j}") for j in range(G)]
            junk = ar.alloc([D], BF16); B_junk = Buf("fjunk")
            tmp = ar.alloc([D], F32); B_tmp = Buf("ftmp")
            hb = ar.alloc([D], BF16); B_hb = Buf("fhb")
            hT = ar.alloc([KC, G * 128], BF16); B_hT = Buf("fhT")
            AT = ar.alloc([FC, G * 128], BF16); B_AT = Buf("fAT")
            sg = [ar.alloc([G * 128], F32) for _ in range(2)]; B_sg = [Buf(f"sg{j}") for j in range(2)]
            ss = [ar.alloc([3], F32) for _ in range(G)]; B_ss = [Buf(f"fss{j}") for j in range(G)]
            fs = [ar.alloc([3], F32) for _ in range(G)]; B_fs = [Buf(f"ffs{j}") for j in range(G)]
            NB = (D + 511) // 512
            groups = [tiles[a:a + G] for a in range(0, len(tiles), G)]
            fi = 0
            for grp in groups:
                ng = len(grp)
                NQ = ng * 128
                for gi, i in enumerate(grp):
                    k.dma("sp", xg[:, gi, :], src[i * 128:(i + 1) * 128, :], writes=[B_xg[gi]])
                    norm_mod_T(xg[:, gi, :], B_xg[gi], 4 * D, 3 * D, junk, B_junk, ss[gi], B_ss[gi], tmp, B_tmp, hb, B_hb, hT[:, :, gi * 128:(gi + 1) * 128], B_hT, 0)
                for f in range(FC):
                    gb = 1 + fi % 2; ub = 3 + fi % 2; sj = fi % 2; fi += 1
                    for kc in range(KC):
                        k.op("pe", lambda e, kc=kc, f=f, gb=gb: e.matmul(bank(gb)[:, 0:NQ], lhsT=wg[:, kc, f * 128:(f + 1) * 128], rhs=hT[:, kc, 0:NQ], start=(kc == 0), stop=(kc == KC - 1)),
                             reads=[B_wg, B_hT], writes=[PB[gb]])
                    for kc in range(KC):
                        k.op("pe", lambda e, kc=kc, f=f, ub=ub: e.matmul(bank(ub)[:, 0:NQ], lhsT=wu[:, kc, f * 128:(f + 1) * 128], rhs=hT[:, kc, 0:NQ], start=(kc == 0), stop=(kc == KC - 1)),
                             reads=[B_wu, B_hT], writes=[PB[ub]])
                    k.op("act", lambda e, gb=gb, sj=sj: e.activation(out=sg[sj][:, 0:NQ], in_=bank(gb)[:, 0:NQ], func=AF.Silu), reads=[PB[gb]], writes=[B_sg[sj]])
                    k.op("dve", lambda e, ub=ub, sj=sj, f=f: e.tensor_tensor(out=AT[:, f, 0:NQ], in0=bank(ub)[:, 0:NQ], in1=sg[sj][:, 0:NQ], op=ALU.mult), reads=[PB[ub], B_sg[sj]], writes=[B_AT])
                for gi, i in enumerate(grp):
                    for nb in range(NB):
                        cw = min(512, D - nb * 512)
                        yb = 5 + nb
                        for f in range(FC):
                            k.op("pe", lambda e, f=f, gi=gi, nb=nb, cw=cw, yb=yb: e.matmul(bank(yb)[:, 0:cw], lhsT=AT[:, f, gi * 128:(gi + 1) * 128], rhs=wd[:, f, nb * 512:nb * 512 + cw], start=(f == 0), stop=(f == FC - 1)),
                                 reads=[B_AT, B_wd], writes=[PB[yb]])
                        k.op("dve", lambda e, nb=nb, cw=cw, yb=yb: e.tensor_tensor(out=tmp[:, nb * 512:nb * 512 + cw], in0=bank(yb)[:, 0:cw], in1=modrows[:, 5 * D + nb * 512:5 * D + nb * 512 + cw], op=ALU.mult),
                             reads=[PB[yb], B_mod], writes=[B_tmp])
                    xv = xg[:, gi, :]
                    k.op("pool", lambda e, xv=xv: e.tensor_tensor(out=xv, in0=xv, in1=tmp, op=ALU.add), reads=[B_tmp, B_xg[gi]], writes=[B_xg[gi]])
                    if final:
                        k.op("act", lambda e, xv=xv, gi=gi: e.activation(out=junk, in_=xv, func=AF.Square, accum_out=fs[gi][:, 0:1]), reads=[B_xg[gi]], writes=[B_junk, B_fs[gi]])
                        rstd_chain(fs[gi], 1.0 / D, B_fs[gi])
                        k.op("dve", lambda e, xv=xv, gi=gi: e.scalar_tensor_tensor(out=xv, in0=xv, scalar=fs[gi][:, 2:3], in1=fg_r, op0=ALU.mult, op1=ALU.mult),
                             reads=[B_xg[gi], B_fs[gi], B_gains], writes=[B_xg[gi]])
                        k.dma("sp", dst[(i - CT) * 128:(i - CT + 1) * 128, :], xv, reads=[B_xg[gi]])
                    else:
                        k.dma("sp", dst[i * 128:(i + 1) * 128, :], xv, reads=[B_xg[gi]])
            k.barrier()

        ctx_tiles = list(range(CT)); x_tiles = list(range(CT, NCH))
        qt_x = [(CTX + a * 512, 512) for a in range(T // 512)]
        all_chunks = list(range(NCH))
        ada_phase(0, 1, 6 * D)
        proj_phase(0, ctx_tiles)
        na_units = [dict(KT=NAKT[p], V=NAV[p], vtot=130, vsl=[(0, 65), (65, 65)], QT=NAQT[p], col=p * 128) for p in range(NAP)]
        g_units = [dict(KT=GKT[(2 * p) // (GQ // GKV)], V=GV[(2 * p) // (GQ // GKV)], vtot=65, vsl=[(0, 65), (0, 65)], QT=GQT[p], col=(NAP + p) * 128) for p in range(GQP)]
        attn_phase(na_units + g_units, [(0, CTX)], list(range(CT)), "gqa")
        wout_phase(0, ctx_tiles, XM)
        ffn_phase(0, ctx_tiles, XM, X1, False)
        ada_phase(0, 0, 6 * D)
        proj_phase(0, x_tiles)
        na_phase()
        attn_phase(g_units, qt_x, all_chunks, "gqa")
        wout_phase(0, x_tiles, XM)
        ffn_phase(0, x_tiles, XM, X1, False)
        ada_phase(1, 1, 2 * D)
        proj_phase(1, ctx_tiles)
        ada_phase(1, 0, 6 * D)
        proj_phase(1, x_tiles)
        d_units = [dict(KT=DKT[h], V=DV[h], vtot=129, vsl=[(0, 129), (0, 129)], QT=DQT[h], col=h * 128) for h in range(DH)]
        if split:
            own_tiles = list(range(CT, CT + NT // 2))
            qt_own = [(CTX + a * 512, 512) for a in range(T // 2 // 512)]
            attn_phase(d_units, qt_own, all_chunks, "diff", qblend=T // 2)
            wout_phase(1, own_tiles, XM, xblend=NT // 2)
            ffn_phase(1, own_tiles, XM, out_d, True)
        else:
            attn_phase(d_units, qt_x, all_chunks, "diff")
            wout_phase(1, x_tiles, XM)
            ffn_phase(1, x_tiles, XM, out_d, True)
        k.emit()
        print("instr counts", {e: len(k.ops[e]) for e in ENGS}, "signalled", k.ncounts, "sems", k.nsem, flush=True)
        print("max dma sem", sorted([(d.count, n) for n, d in k.dpool.items()])[-6:], flush=True)
    return nc


def _cossin_table(cfg):
    T = cfg["ROWS"] * GRID_W; CTX = cfg["CTX"]
    t = np.arange(T)
    row = (t // GRID_W).astype(np.float32); col = (t % GRID_W).astype(np.float32)
    nf = 16
    inv = (np.float32(10000.0) ** (-np.arange(nf, dtype=np.float32) / np.float32(nf))).astype(np.float32)
    ang = np.concatenate([row[:, None] * inv, col[:, None] * inv], axis=-1).astype(np.float32)
    tab = np.zeros((CTX + T, 64), np.float32)
    tab[:CTX, 0:32] = 1.0
    tab[CTX:, 0:32] = np.cos(ang)
    tab[CTX:, 32:64] = np.sin(ang)
    return tab


def _na_bias_table(cfg, rpb):
    ROWS = cfg["ROWS"]; NA = cfg["NA"]
    out = np.full((NA // 2, 128, 5, 2, 5, 128), NEG, np.float32)
    kk = np.arange(128); qq = np.arange(128)
    for cls, r0 in enumerate((4, 0, 2, ROWS - 4, ROWS - 2)):
        csr = min(min(max(r0 - 4, 0), ROWS - 8), ROWS - 10)
        qr = r0 + qq // 64; qc = qq % 64
        rs = np.clip(qr - 4, 0, ROWS - 8); cs = np.clip(qc - 8, 0, GRID_W - 16)
        for j in range(5):
            kr = csr + 2 * j + kk // 64; kc = kk % 64
            valid = ((kr[:, None] >= rs[None, :]) & (kr[:, None] < rs[None, :] + 8) &
                     (kc[:, None] >= cs[None, :]) & (kc[:, None] < cs[None, :] + 16))
            ri = np.clip(kr[:, None] - qr[None, :] + 7, 0, 14); ci = np.clip(kc[:, None] - qc[None, :] + 15, 0, 30)
            for h in range(NA):
                g = rpb[h][ri, ci]
                out[h // 2, :, cls, h % 2, j, :] = np.where(valid, g, np.float32(NEG))
    return out.reshape(NA // 2, 128, 5 * 2 * 5 * 128)


def make_core_inputs(cfg, inp, b):
    D = cfg["D"]; KC = D // 128
    f = lambda a: np.ascontiguousarray(np.asarray(a, dtype=np.float32))
    cvec = np.concatenate([f(inp["c"][b]).reshape(KC, 128).T, f(inp["c_ctx"]).reshape(KC, 128).T], axis=1)
    lamv = np.concatenate([f(inp["diff_lambda_q1"][0]), f(inp["diff_lambda_k1"][0]), f(inp["diff_lambda_q2"][0]), f(inp["diff_lambda_k2"][0])])[None]
    return {
        "x": f(inp["x"][b]), "ctx": f(inp["ctx"][b]), "cvec": f(cvec),
        "ada_w": f(inp["ada_w"]), "ada_b": f(inp["ada_b"]),
        "ffn_w_gate": f(inp["ffn_w_gate"]), "ffn_w_up": f(inp["ffn_w_up"]), "ffn_w_down": f(inp["ffn_w_down"]),
        "par_w_in": f(inp["par_w_in"][0]), "par_w_out": f(inp["par_w_out"][0]),
        "diff_w_in": f(inp["diff_w_in"][0]), "diff_w_out": f(inp["diff_w_out"][0]),
        "gqa_q_gain": f(inp["gqa_q_gain"]).reshape(1, 64), "gqa_k_gain": f(inp["gqa_k_gain"]).reshape(1, 64),
        "lamv": f(lamv), "diff_subln_gain": f(inp["diff_subln_gain"]).reshape(1, 128),
        "final_norm_gain": f(inp["final_norm_gain"]).reshape(1, D),
        "cossin": _cossin_table(cfg), "nabias": _na_bias_table(cfg, f(inp["na_rpb"][0])),
        "ident": np.eye(128, dtype=np.float32).astype(ml_dtypes.bfloat16),
        "sel": np.tile(np.array([[1.0, 0.0]], np.float32), (128, 1)),
    }


def kernel(**inputs):
    cfg = FULL_CFG
    B = inputs["x"].shape[0]
    nc = build_program(cfg)
    shared = None
    in_maps = []
    T = inputs["x"].shape[1]
    for core in range(8):
        b = core % B
        half = core // B
        m = make_core_inputs(cfg, inputs, b) if shared is None else dict(shared)
        if shared is None:
            shared = m
        else:
            f = lambda a: np.ascontiguousarray(np.asarray(a, dtype=np.float32))
            D = cfg["D"]; KC = D // 128
            m["x"] = f(inputs["x"][b]); m["ctx"] = f(inputs["ctx"][b])
            m["cvec"] = f(np.concatenate([f(inputs["c"][b]).reshape(KC, 128).T, f(inputs["c_ctx"]).reshape(KC, 128).T], axis=1))
        m["sel"] = np.tile(np.array([[1.0, 0.0]] if half == 0 else [[0.0, 1.0]], np.float32), (128, 1))
        in_maps.append(m)
    res = run_bass_kernel_spmd(nc, in_maps, core_ids=list(range(8)))
    out = np.empty((B, T, cfg["D"]), np.float32)
    for core in range(8):
        b = core % B
        half = core // B
        out[b, half * (T // 2):(half + 1) * (T // 2)] = np.asarray(res.results[core]["out"], dtype=np.float32)
    return out
```

```python
import math
import numpy as np
import ml_dtypes
from contextlib import ExitStack
import concourse.bass as bass
import concourse.mybir as mybir
from concourse.bass_utils import run_bass_kernel_spmd

F32 = mybir.dt.float32
BF16 = mybir.dt.bfloat16
U8 = mybir.dt.uint8
AF = mybir.ActivationFunctionType
ALU = mybir.AluOpType
AX = mybir.AxisListType

ENGS = ("pe", "act", "dve", "pool", "sp")
EPOCH = 2000
GRID_W = 64
EPS = 1e-6
NEG = -30000.0

FULL_CFG = dict(D=1024, ROWS=128, CTX=256, NA=8, GQ=8, GKV=2, DH=8, FF=2816)


class Buf:
    __slots__ = ("name", "w", "r", "dsem")

    def __init__(self, name, dsem=None):
        self.name = name
        self.w = {}
        self.r = {}
        self.dsem = dsem


class DmaSem:
    __slots__ = ("h", "count")

    def __init__(self, h):
        self.h = h
        self.count = 0


class _Rec:
    def __getattr__(self, name):
        def f(*a, **kw):
            self.call = (name, a, kw)
            return self
        return f


class K:
    def __init__(self, nc, stack):
        self.nc = nc
        self.stack = stack
        self.ops = {e: [] for e in ENGS}
        self.instr = {e: [] for e in ENGS}
        self.known = {e: {} for e in ENGS}
        self.dsems = []
        self.dpool = {}
        self.nsem = 0

    def sem(self, name):
        h = self.stack.enter_context(self.nc.semaphore(name))
        self.nsem += 1
        return h

    def dsem(self, name):
        d = DmaSem(self.sem(name))
        self.dsems.append(d)
        return d

    def dbuf(self, name):
        if name not in self.dpool:
            self.dpool[name] = self.dsem("d_" + name)
        return Buf(name, dsem=self.dpool[name])

    def fence(self, buf):
        self._merge(buf.r, buf.w)

    @staticmethod
    def _merge(deps, d):
        for s, v in d.items():
            if deps.get(s, (None, 0))[1] < v[1]:
                deps[s] = v

    def _deps(self, reads, writes):
        deps = {}
        for b in reads:
            self._merge(deps, b.w)
        for b in writes:
            if b.r:
                self._merge(deps, b.r)
                self._merge(deps, b.w)
        return deps

    def _post(self, reads, writes, key, val):
        for b in writes:
            if b.r:
                b.w = {}
                b.r = {}
            if b.w.get(key, (None, 0))[1] < val[1]:
                b.w[key] = val
        for b in reads:
            if b in writes:
                continue
            if b.r.get(key, (None, 0))[1] < val[1]:
                b.r[key] = val

    def _waits(self, eng, deps):
        ws = []
        kn = self.known[eng]
        for key, (payload, v) in deps.items():
            if kn.get(key, 0) >= v:
                continue
            kn[key] = v
            ws.append((key, payload, v))
            if key[0] == "E":
                self.instr[payload][v - 1]["needed"] = True
        return ws

    @staticmethod
    def _bind(fn):
        rec = _Rec()
        fn(rec)
        return rec.call

    def op(self, eng, fn, reads=(), writes=()):
        deps = self._deps(reads, writes)
        ws = self._waits(eng, deps)
        r = {"call": self._bind(fn), "waits": ws, "needed": False, "dma": None}
        self.ops[eng].append(r)
        self.instr[eng].append(r)
        self._post(reads, writes, ("E", eng), (eng, len(self.instr[eng])))

    def dma(self, eng, out_ap, in_ap, reads=(), writes=()):
        ds = None
        for b in list(writes) + list(reads):
            if b.dsem is not None:
                ds = b.dsem
                break
        assert ds is not None
        deps = self._deps(reads, writes)
        ws = self._waits(eng, deps)
        ds.count += 16
        self.ops[eng].append({"call": ("dma_start", (), {"out": out_ap, "in_": in_ap}), "waits": ws, "needed": False, "dma": ds})
        self._post(reads, writes, ("D", id(ds)), (ds, ds.count))

    def barrier(self):
        evs = {}
        for e in ENGS:
            if self.instr[e]:
                evs[("E", e)] = (e, len(self.instr[e]))
        for d in self.dsems:
            if d.count > 0:
                evs[("D", id(d))] = (d, d.count)
        for e in ENGS:
            ws = self._waits(e, evs)
            if ws:
                self.ops[e].append({"call": None, "waits": ws, "needed": False, "dma": None})

    def emit(self):
        nc = self.nc
        esems = {}
        for e in ENGS:
            c = 0
            for r in self.instr[e]:
                if r["needed"]:
                    c += 1
                    r["count"] = c
            esems[e] = [self.sem(f"e_{e}_{j}") for j in range((c + EPOCH - 1) // EPOCH)]
        self.ncounts = {e: sum(1 for r in self.instr[e] if r["needed"]) for e in ENGS}

        def resolve(w):
            key, payload, v = w
            if key[0] == "D":
                return payload.h, v
            c = self.instr[payload][v - 1]["count"]
            return esems[payload][(c - 1) // EPOCH], (c - 1) % EPOCH + 1

        with nc.Block() as block:
            def run(e, eng):
                for r in self.ops[eng]:
                    for w in r["waits"]:
                        h, v = resolve(w)
                        e.wait_ge(h, v)
                    if r["call"] is None:
                        continue
                    name, a, kw = r["call"]
                    ins = getattr(e, name)(*a, **kw)
                    if r["dma"] is not None:
                        ins.then_inc(r["dma"].h, 16)
                    elif r["needed"]:
                        c = r["count"]
                        ins.then_inc(esems[eng][(c - 1) // EPOCH], 1)

            @block.tensor
            def _(e):
                run(e, "pe")

            @block.scalar
            def _(e):
                run(e, "act")

            @block.vector
            def _(e):
                run(e, "dve")

            @block.gpsimd
            def _(e):
                run(e, "pool")

            @block.sync
            def _(e):
                run(e, "sp")


class Arena:
    def __init__(self, ap, size):
        self.ap = ap
        self.size = size
        self.off = 0

    def alloc(self, shape, dt):
        esz = 4 if dt == F32 else 2
        n = int(np.prod(shape)) * esz
        n_al = (n + 63) // 64 * 64
        assert self.off + n_al <= self.size, f"arena overflow {self.off}+{n_al}>{self.size}"
        a = self.ap[:, self.off:self.off + n].bitcast(dt)
        self.off += n_al
        if len(shape) == 2:
            a = a.rearrange("p (a b) -> p a b", a=shape[0])
        elif len(shape) == 3:
            a = a.rearrange("p (a b c) -> p a b c", a=shape[0], b=shape[1])
        return a


def build_program(cfg, debug=False, split=True):
    D = cfg["D"]; KC = D // 128; ROWS = cfg["ROWS"]; T = ROWS * GRID_W; CTX = cfg["CTX"]
    NA = cfg["NA"]; GQ = cfg["GQ"]; GKV = cfg["GKV"]; DH = cfg["DH"]; FF = cfg["FF"]; FC = FF // 128
    NKEY = CTX + T; NCH = NKEY // 128; CT = CTX // 128; NT = T // 128
    NAP = NA // 2; GQP = GQ // 2
    W0 = (3 * NA + GQ + 2 * GKV) * 64
    W1 = 3 * DH * 128
    NCLS = 5
    lam_init = 0.8 - 0.6 * math.exp(-0.3 * 1)

    nc = bass.Bass("TRN2", target_bir_lowering=False)

    def din(name, shape, dt=F32):
        return nc.dram_tensor(name, list(shape), dt, kind="ExternalInput").ap()

    def dscr(name, shape, dt):
        return nc.dram_tensor(name, list(shape), dt, kind="ExternalOutput" if debug else "Internal").ap()

    x_in = din("x", [T, D]); ctx_in = din("ctx", [CTX, D])
    cvec = din("cvec", [128, 2 * KC])
    ada_w = din("ada_w", [2, D, 6 * D]); ada_b = din("ada_b", [2, 6 * D])
    w_gate = din("ffn_w_gate", [2, D, FF]); w_up = din("ffn_w_up", [2, D, FF]); w_down = din("ffn_w_down", [2, FF, D])
    w_in0 = din("par_w_in", [D, W0]); w_out0 = din("par_w_out", [D, D])
    w_in1 = din("diff_w_in", [D, W1]); w_out1 = din("diff_w_out", [D, D])
    qgain = din("gqa_q_gain", [1, 64]); kgain = din("gqa_k_gain", [1, 64])
    lamv = din("lamv", [1, 256]); subln = din("diff_subln_gain", [1, 128]); fgain = din("final_norm_gain", [1, D])
    cossin = din("cossin", [NKEY, 64])
    nabias = din("nabias", [NAP, 128, NCLS * 2 * 5 * 128])
    ident_in = din("ident", [128, 128], BF16)
    sel_in = din("sel", [128, 2])
    TO = T // 2 if split else T
    out_d = nc.dram_tensor("out", [TO, D], F32, kind="ExternalOutput").ap()

    XM = dscr("XM", [NKEY, D], F32); X1 = dscr("X1", [NKEY, D], F32)
    AO = dscr("AO", [NKEY, D], BF16)
    NAQT = dscr("NAQT", [NAP, 128, NKEY], BF16); NAKT = dscr("NAKT", [NAP, 128, NKEY], BF16)
    NAV = dscr("NAV", [NAP, 128, NCH * 130], BF16)
    GQT = dscr("GQT", [GQP, 128, NKEY], BF16); GKT = dscr("GKT", [GKV, 128, NKEY], BF16)
    GV = dscr("GV", [GKV, 128, NCH * 65], BF16)
    DQT = dscr("DQT", [DH, 128, NKEY], BF16); DKT = dscr("DKT", [DH, 128, NKEY], BF16)
    DV = dscr("DV", [DH, 128, NCH * 129], BF16)

    with ExitStack() as st:
        k = K(nc, st)
        ARENA = 204 * 1024
        arena_t = st.enter_context(nc.sbuf_tensor("arena", [128, ARENA], U8))
        ar = Arena(arena_t[:, :], ARENA)
        ps = st.enter_context(nc.psum_tensor("ps", [128, 4096], F32))

        def bank(i, n=1):
            return ps[:, i * 512:(i + n) * 512]

        PB = [Buf(f"psb{i}") for i in range(8)]

        ident = ar.alloc([128], BF16); B_ident = k.dbuf("ident")
        modrows = ar.alloc([6 * D], F32); B_mod = Buf("mod")
        csil = ar.alloc([2 * KC], F32); B_csil = k.dbuf("csil")
        ones_f = ar.alloc([128], F32); B_ones = Buf("ones")
        qg_r = ar.alloc([64], F32); kg_r = ar.alloc([64], F32); B_gains = k.dbuf("gains")
        lam_r = ar.alloc([256], F32); sub_r = ar.alloc([128], F32); fg_r = ar.alloc([D], F32)
        lam_s = ar.alloc([8], F32); B_lam = Buf("lam")
        sel = ar.alloc([2], F32)
        PERS = ar.off

        k.dma("sp", ident, ident_in, writes=[B_ident])
        k.dma("sp", csil, cvec, writes=[B_csil])
        k.dma("sp", qg_r, qgain.partition_broadcast(128), writes=[B_gains])
        k.dma("sp", kg_r, kgain.partition_broadcast(128), writes=[B_gains])
        k.dma("sp", lam_r, lamv.partition_broadcast(128), writes=[B_gains])
        k.dma("sp", sub_r, subln.partition_broadcast(128), writes=[B_gains])
        k.dma("sp", fg_r, fgain.partition_broadcast(128), writes=[B_gains])
        k.dma("sp", sel, sel_in, writes=[B_gains])
        k.op("dve", lambda e: e.memset(ones_f, 1.0), writes=[B_ones])
        k.op("act", lambda e: e.activation(out=csil, in_=csil, func=AF.Silu), reads=[B_csil], writes=[B_csil])
        lamtmp = ar.alloc([128], F32)
        PERS = ar.off
        k.op("dve", lambda e: e.tensor_tensor(out=lamtmp[:, 0:64], in0=lam_r[:, 0:64], in1=lam_r[:, 64:128], op=ALU.mult), reads=[B_gains], writes=[B_lam])
        k.op("dve", lambda e: e.tensor_tensor(out=lamtmp[:, 64:128], in0=lam_r[:, 128:192], in1=lam_r[:, 192:256], op=ALU.mult), reads=[B_gains], writes=[B_lam])
        k.op("dve", lambda e: e.tensor_reduce(out=lam_s[:, 0:2], in_=lamtmp.rearrange("p (a b) -> p a b", a=2), axis=AX.X, op=ALU.add), reads=[B_lam], writes=[B_lam])
        k.op("act", lambda e: e.activation(out=lam_s[:, 2:4], in_=lam_s[:, 0:2], func=AF.Exp), reads=[B_lam], writes=[B_lam])
        k.op("dve", lambda e: e.tensor_tensor(out=lam_s[:, 4:5], in0=lam_s[:, 3:4], in1=lam_s[:, 2:3], op=ALU.subtract), reads=[B_lam], writes=[B_lam])
        k.op("dve", lambda e: e.tensor_scalar(out=lam_s[:, 5:6], in0=lam_s[:, 4:5], scalar1=-lam_init, scalar2=None, op0=ALU.add), reads=[B_lam], writes=[B_lam])
        neglam = lam_s[:, 5:6]
        k.op("dve", lambda e: e.tensor_scalar(out=sub_r, in0=sub_r, scalar1=(1.0 - lam_init), scalar2=None, op0=ALU.mult), reads=[B_gains], writes=[B_gains])

        def cast_copy(i, out, in_, reads, writes):
            eng = ("pool", "dve", "act")[i % 3] if True else "dve"
            if eng == "act":
                k.op("act", lambda e: e.activation(out=out, in_=in_, func=AF.Copy), reads=reads, writes=writes)
            else:
                k.op(eng, lambda e: e.tensor_copy(out=out, in_=in_), reads=reads, writes=writes)

        def load_w(dst, src, kcs, n, stg, B_stg, B_dst):
            for kc in range(kcs):
                j = kc % 2
                k.dma("sp", stg[j][:, 0:n], src[kc * 128:(kc + 1) * 128, :], writes=[B_stg[j]])
                cast_copy(kc, dst[:, kc, :], stg[j][:, 0:n], [B_stg[j]], [B_dst])

        def ada_phase(l, cond, ncols):
            ar.off = PERS
            rep = ar.alloc([KC, 128], F32); B_rep = Buf("rep")
            wst = [ar.alloc([KC, 512], F32) for _ in range(2)]; B_wst = [k.dbuf(f"adaw{j}") for j in range(2)]
            bst = [ar.alloc([512], F32) for _ in range(2)]; B_bst = [k.dbuf(f"adab{j}") for j in range(2)]
            for kc in range(KC):
                k.op("dve", lambda e, kc=kc: e.tensor_scalar(out=rep[:, kc, :], in0=ones_f, scalar1=csil[:, cond * KC + kc:cond * KC + kc + 1], scalar2=None, op0=ALU.mult),
                     reads=[B_ones, B_csil], writes=[B_rep])
            nb = ncols // 512
            for n in range(nb):
                j = n % 2
                k.dma("sp", wst[j], ada_w[l, :, n * 512:(n + 1) * 512].rearrange("(a p) n -> p a n", p=128), writes=[B_wst[j]])
                k.dma("sp", bst[j], ada_b[l:l + 1, n * 512:(n + 1) * 512].partition_broadcast(128), writes=[B_bst[j]])
                pb = n % 2
                for kc in range(KC):
                    k.op("pe", lambda e, kc=kc, j=j, pb=pb: e.matmul(bank(pb), lhsT=rep[:, kc, :], rhs=wst[j][:, kc, :], start=(kc == 0), stop=(kc == KC - 1)),
                         reads=[B_rep, B_wst[j]], writes=[PB[pb]])
                k.op("dve", lambda e, n=n, j=j, pb=pb: e.tensor_tensor(out=modrows[:, n * 512:(n + 1) * 512], in0=bank(pb), in1=bst[j], op=ALU.add),
                     reads=[PB[pb], B_bst[j]], writes=[B_mod])
            for off in (D, 4 * D):
                if off < ncols:
                    k.op("dve", lambda e, off=off: e.tensor_scalar(out=modrows[:, off:off + D], in0=modrows[:, off:off + D], scalar1=1.0, scalar2=None, op0=ALU.add),
                         reads=[B_mod], writes=[B_mod])
            k.barrier()

        def rstd_chain(ss, n_inv, nrm_bufs):
            B = nrm_bufs
            w = ss.shape[1] // 3
            k.op("dve", lambda e: e.tensor_scalar(out=ss[:, w:2 * w], in0=ss[:, 0:w], scalar1=n_inv, scalar2=EPS, op0=ALU.mult, op1=ALU.add), reads=[B], writes=[B])
            k.op("act", lambda e: e.activation(out=ss[:, w:2 * w], in_=ss[:, w:2 * w], func=AF.Sqrt), reads=[B], writes=[B])
            k.op("dve", lambda e: e.reciprocal(out=ss[:, 2 * w:3 * w], in_=ss[:, w:2 * w]), reads=[B], writes=[B])

        def norm_mod_T(xt, B_x, sc_off, sh_off, junk, B_junk, ss, B_ss, tmp, B_tmp, hb, B_hb, hT_dst, B_hT, pbank):
            k.op("act", lambda e: e.activation(out=junk, in_=xt, func=AF.Square, accum_out=ss[:, 0:1]), reads=[B_x], writes=[B_junk, B_ss])
            rstd_chain(ss, 1.0 / D, B_ss)
            k.op("dve", lambda e: e.scalar_tensor_tensor(out=tmp, in0=xt, scalar=ss[:, 2:3], in1=modrows[:, sc_off:sc_off + D], op0=ALU.mult, op1=ALU.mult),
                 reads=[B_x, B_ss, B_mod], writes=[B_tmp])
            k.op("pool", lambda e: e.tensor_tensor(out=hb, in0=tmp, in1=modrows[:, sh_off:sh_off + D], op=ALU.add), reads=[B_tmp, B_mod], writes=[B_hb])
            pt = bank(pbank).bitcast(BF16)
            for kc in range(KC):
                k.op("pe", lambda e, kc=kc: e.transpose(out=pt[:, kc * 128:(kc + 1) * 128], in_=hb[:, kc * 128:(kc + 1) * 128], identity=ident),
                     reads=[B_hb, B_ident], writes=[PB[pbank]])
            k.op("act", lambda e: e.activation(out=hT_dst, in_=pt[:, 0:KC * 128].rearrange("p (a b) -> p a b", a=KC), func=AF.Copy), reads=[PB[pbank]], writes=[B_hT])

        def src_tile(l, i):
            if l == 0:
                return ctx_in[i * 128:(i + 1) * 128, :] if i < CT else x_in[(i - CT) * 128:(i - CT + 1) * 128, :]
            return X1[i * 128:(i + 1) * 128, :]

        def proj_phase(l, tiles):
            ar.off = PERS
            WW = W0 if l == 0 else W1
            w_src = w_in0 if l == 0 else w_in1
            wsb = ar.alloc([KC, WW], BF16); B_w = Buf("w_in")
            mark = ar.off
            stg = [ar.alloc([WW], F32) for _ in range(2)]; B_stg = [k.dbuf(f"wstg{j}") for j in range(2)]
            load_w(wsb, w_src, KC, WW, stg, B_stg, B_w)
            k.barrier()
            ar.off = mark
            xt = [ar.alloc([D], F32) for _ in range(2)]; B_x = [k.dbuf(f"px{j}") for j in range(2)]
            cst = [ar.alloc([64], F32) for _ in range(2)]; B_cs = [k.dbuf(f"pcs{j}") for j in range(2)]
            junk = ar.alloc([D], F32); B_junk = Buf("junk")
            ss = [ar.alloc([3], F32) for _ in range(2)]; B_ss = [Buf(f"ss{j}") for j in range(2)]
            tmp = ar.alloc([D], F32); B_tmp = Buf("tmp")
            hb = [ar.alloc([D], BF16) for _ in range(2)]; B_hb = [Buf(f"hb{j}") for j in range(2)]
            hT = [ar.alloc([KC, 128], BF16) for _ in range(2)]; B_hT = [Buf(f"hT{j}") for j in range(2)]
            sq = [ar.alloc([512], F32) for _ in range(2)]; B_sq = [Buf(f"sq{j}") for j in range(2)]
            qn = [ar.alloc([512], F32) for _ in range(2)]; B_qn = [Buf(f"qn{j}") for j in range(2)]
            ra = [ar.alloc([512], F32) for _ in range(2)]; B_ra = [Buf(f"ra{j}") for j in range(2)]
            rb = [ar.alloc([512], F32) for _ in range(2)]; B_rb = [Buf(f"rb{j}") for j in range(2)]
            nss = [ar.alloc([24], F32) for _ in range(2)]; B_nss = [Buf(f"nss{j}") for j in range(2)]
            tm = [ar.alloc([512], BF16) for _ in range(3)]; B_tm = [Buf(f"tm{j}") for j in range(3)]
            stT = [ar.alloc([4, 128], BF16) for _ in range(3)]; B_stT = [k.dbuf(f"stT{j}_{l}{int(tiles[0] < CT)}") for j in range(3)]
            vw = 65 if l == 0 else 129
            nvh = (NA + GKV) if l == 0 else DH
            vst = [ar.alloc([nvh, vw], BF16) for _ in range(2)]; B_vst = [k.dbuf(f"vst{j}_{l}{int(tiles[0] < CT)}") for j in range(2)]
            for j in range(2):
                k.op("pool", lambda e, j=j: e.memset(vst[j], 1.0), writes=[B_vst[j]])
                k.fence(B_vst[j])
            cnt = {"pj": 0, "tp": 0, "pp": 0, "tm": 0, "st": 0}

            def post_qk(pj, nm, s0, dsts, norm_gain, do_rope, csb, B_csb, dup=False):
                w = nm * 64
                src = bank(pj)[:, 0:w]
                srcB = PB[pj]
                pp = cnt["pp"] % 2; cnt["pp"] += 1
                if norm_gain is not None:
                    k.op("act", lambda e: e.activation(out=sq[pp][:, 0:w], in_=src, func=AF.Square), reads=[srcB], writes=[B_sq[pp]])
                    k.op("dve", lambda e: e.tensor_reduce(out=nss[pp][:, 0:nm], in_=sq[pp][:, 0:w].rearrange("p (a b) -> p a b", a=nm), axis=AX.X, op=ALU.add),
                         reads=[B_sq[pp]], writes=[B_nss[pp]])
                    k.op("dve", lambda e: e.tensor_scalar(out=nss[pp][:, 8:8 + nm], in0=nss[pp][:, 0:nm], scalar1=1.0 / 64, scalar2=EPS, op0=ALU.mult, op1=ALU.add), reads=[B_nss[pp]], writes=[B_nss[pp]])
                    k.op("act", lambda e: e.activation(out=nss[pp][:, 8:8 + nm], in_=nss[pp][:, 8:8 + nm], func=AF.Sqrt), reads=[B_nss[pp]], writes=[B_nss[pp]])
                    k.op("dve", lambda e: e.reciprocal(out=nss[pp][:, 16:16 + nm], in_=nss[pp][:, 8:8 + nm]), reads=[B_nss[pp]], writes=[B_nss[pp]])
                    k.op("dve", lambda e: e.tensor_tensor(out=qn[pp][:, 0:w].rearrange("p (a b) -> p a b", a=nm), in0=src.rearrange("p (a b) -> p a b", a=nm),
                                                          in1=nss[pp][:, 16:16 + nm][:, :, None].to_broadcast([128, nm, 64]), op=ALU.mult),
                         reads=[srcB, B_nss[pp]], writes=[B_qn[pp]])
                    k.op("pool", lambda e: e.tensor_tensor(out=qn[pp][:, 0:w].rearrange("p (a b) -> p a b", a=nm), in0=qn[pp][:, 0:w].rearrange("p (a b) -> p a b", a=nm),
                                                           in1=norm_gain[:, None, :].to_broadcast([128, nm, 64]), op=ALU.mult),
                         reads=[B_qn[pp], B_gains], writes=[B_qn[pp]])
                    src = qn[pp][:, 0:w]; srcB = B_qn[pp]
                ti = cnt["tm"] % 3; cnt["tm"] += 1
                if do_rope:
                    s4 = src.rearrange("p (a b c) -> p a b c", a=nm, c=2)
                    cosb = csb[:, 0:32][:, None, :, None].to_broadcast([128, nm, 32, 2])
                    sinb = csb[:, 32:64][:, None, :, None].to_broadcast([128, nm, 32, 2])
                    A = ra[pp][:, 0:w].rearrange("p (a b c) -> p a b c", a=nm, c=2)
                    Bm = rb[pp][:, 0:w].rearrange("p (a b c) -> p a b c", a=nm, c=2)
                    o4 = tm[ti][:, 0:w].rearrange("p (a b c) -> p a b c", a=nm, c=2)
                    k.op("dve", lambda e: e.tensor_tensor(out=A, in0=s4, in1=cosb, op=ALU.mult), reads=[srcB, B_csb], writes=[B_ra[pp]])
                    k.op("dve", lambda e: e.tensor_tensor(out=Bm, in0=s4, in1=sinb, op=ALU.mult), reads=[srcB, B_csb], writes=[B_rb[pp]])
                    k.op("pool", lambda e: e.tensor_tensor(out=o4[:, :, :, 0], in0=A[:, :, :, 0], in1=Bm[:, :, :, 1], op=ALU.subtract), reads=[B_ra[pp], B_rb[pp]], writes=[B_tm[ti]])
                    k.op("pool", lambda e: e.tensor_tensor(out=o4[:, :, :, 1], in0=Bm[:, :, :, 0], in1=A[:, :, :, 1], op=ALU.add), reads=[B_ra[pp], B_rb[pp]], writes=[B_tm[ti]])
                else:
                    k.op("act", lambda e: e.activation(out=tm[ti][:, 0:w], in_=src, func=AF.Copy), reads=[srcB], writes=[B_tm[ti]])
                return dict(ti=ti, nm=nm, w=w, dsts=dsts, dup=dup)

            def post_qk2(stt):
                ti = stt["ti"]; nm = stt["nm"]; w = stt["w"]; dsts = stt["dsts"]; dup = stt["dup"]
                if dup:
                    chunks = [(m * 64, 64) for m in range(nm)]
                else:
                    chunks = [(c * 128, 128) for c in range(w // 128)]
                tb = 4 + cnt["tp"] % 2; cnt["tp"] += 1
                ptb = bank(tb).bitcast(BF16)
                sti = cnt["st"] % 3; cnt["st"] += 1
                for ci, (c0, cw) in enumerate(chunks):
                    if dup:
                        for hlf in range(2):
                            k.op("pe", lambda e, ci=ci, c0=c0, hlf=hlf: e.transpose(out=ptb[hlf * 64:(hlf + 1) * 64, ci * 128:(ci + 1) * 128], in_=tm[ti][:, c0:c0 + 64], identity=ident),
                                 reads=[B_tm[ti], B_ident], writes=[PB[tb]])
                    else:
                        k.op("pe", lambda e, ci=ci, c0=c0: e.transpose(out=ptb[:, ci * 128:(ci + 1) * 128], in_=tm[ti][:, c0:c0 + 128], identity=ident),
                             reads=[B_tm[ti], B_ident], writes=[PB[tb]])
                nch_ = len(chunks)
                k.op("dve", lambda e: e.tensor_copy(out=stT[sti][:, 0:nch_, :], in_=ptb[:, 0:nch_ * 128].rearrange("p (a b) -> p a b", a=nch_)), reads=[PB[tb]], writes=[B_stT[sti]])
                for ci in range(nch_):
                    k.dma("sp", dsts[ci], stT[sti][:, ci, :], reads=[B_stT[sti]])

            def stageA(it):
                i = tiles[it]
                j = it % 2
                s0 = i * 128
                k.dma("sp", xt[j], src_tile(l, i), writes=[B_x[j]])
                k.dma("sp", cst[j], cossin[s0:s0 + 128, :], writes=[B_cs[j]])
                norm_mod_T(xt[j], B_x[j], 1 * D, 0, junk, B_junk, ss[j], B_ss[j], tmp, B_tmp, hb[j], B_hb[j], hT[j], B_hT[j], 0)

            stageA(0)
            for it, i in enumerate(tiles):
                j = it % 2
                s0 = i * 128
                if it + 1 < len(tiles):
                    stageA(it + 1)
                if l == 0:
                    blocks = []
                    c = 0
                    for nm_total, kind in ((NA, "naq"), (NA, "nak"), (NA, "nav"), (GQ, "gq"), (GKV, "gk"), (GKV, "gv")):
                        m0 = 0
                        while m0 < nm_total:
                            nm = min(8, nm_total - m0)
                            blocks.append((kind, m0, nm, c + m0 * 64))
                            m0 += nm
                        c += nm_total * 64
                else:
                    blocks = []
                    for kind, base in (("dq", 0), ("dk", DH * 128), ("dv", 2 * DH * 128)):
                        m0 = 0
                        while m0 < 2 * DH:
                            nm = min(8, 2 * DH - m0)
                            blocks.append((kind, m0, nm, base + m0 * 64))
                            m0 += nm
                vj = it % 2

                def do_p1(blk):
                    kind, m0, nm, c0, pj = blk
                    if kind in ("naq", "nak"):
                        dst = NAQT if kind == "naq" else NAKT
                        return post_qk(pj, nm, s0, [dst[(m0 // 2) + ci, :, s0:s0 + 128] for ci in range(nm // 2)], None, False, None, None)
                    elif kind == "gq":
                        return post_qk(pj, nm, s0, [GQT[(m0 // 2) + ci, :, s0:s0 + 128] for ci in range(nm // 2)], qg_r, True, cst[j], B_cs[j])
                    elif kind == "gk":
                        return post_qk(pj, nm, s0, [GKT[m0 + ci, :, s0:s0 + 128] for ci in range(nm)], kg_r, True, cst[j], B_cs[j], dup=True)
                    elif kind in ("dq", "dk"):
                        dst = DQT if kind == "dq" else DKT
                        return post_qk(pj, nm, s0, [dst[(m0 // 2) + ci, :, s0:s0 + 128] for ci in range(nm // 2)], None, True, cst[j], B_cs[j])
                    elif kind in ("nav", "gv"):
                        h0 = m0 if kind == "nav" else NA + m0
                        k.op("act", lambda e: e.activation(out=vst[vj][:, h0:h0 + nm, 0:64], in_=bank(pj)[:, 0:nm * 64].rearrange("p (a b) -> p a b", a=nm), func=AF.Copy),
                             reads=[PB[pj]], writes=[B_vst[vj]])
                    elif kind == "dv":
                        h0 = m0 // 2
                        k.op("act", lambda e: e.activation(out=vst[vj][:, h0:h0 + nm // 2, 0:128], in_=bank(pj)[:, 0:nm * 64].rearrange("p (a b) -> p a b", a=nm // 2), func=AF.Copy),
                             reads=[PB[pj]], writes=[B_vst[vj]])
                    return None

                q1 = []; q2 = []
                todo = [b for b in blocks if not (b[0] == "dq" and i < CT)]
                for blk in todo + [None, None]:
                    if blk is not None:
                        (kind, m0, nm, c0) = blk
                        w = nm * 64
                        pj = 1 + cnt["pj"] % 3; cnt["pj"] += 1
                        for kc in range(KC):
                            k.op("pe", lambda e, kc=kc: e.matmul(bank(pj)[:, 0:w], lhsT=hT[j][:, kc, :], rhs=wsb[:, kc, c0:c0 + w], start=(kc == 0), stop=(kc == KC - 1)),
                                 reads=[B_hT[j], B_w], writes=[PB[pj]])
                    st2 = q2.pop(0) if q2 else None
                    if q1:
                        stt = do_p1(q1.pop(0))
                        if stt is not None:
                            q2.append(stt)
                    if st2 is not None:
                        post_qk2(st2)
                    if blk is not None:
                        q1.append((kind, m0, nm, c0, pj))
                while q1 or q2:
                    st2 = q2.pop(0) if q2 else None
                    if q1:
                        stt = do_p1(q1.pop(0))
                        if stt is not None:
                            q2.append(stt)
                    if st2 is not None:
                        post_qk2(st2)
                if l == 0:
                    for p in range(NAP):
                        k.dma("sp", NAV[p, :, i * 130:(i + 1) * 130], vst[vj][:, 2 * p:2 * p + 2, :].rearrange("p a b -> p (a b)"), reads=[B_vst[vj]])
                    for g in range(GKV):
                        k.dma("sp", GV[g, :, i * 65:(i + 1) * 65], vst[vj][:, NA + g, :], reads=[B_vst[vj]])
                else:
                    for h in range(DH):
                        k.dma("sp", DV[h, :, i * 129:(i + 1) * 129], vst[vj][:, h, :], reads=[B_vst[vj]])
            k.barrier()

        def attn_phase(units, qtiles, chunks, finish_kind, qblend=None):
            ar.off = PERS
            vtot_max = max(u["vtot"] for u in units)
            KTs = [ar.alloc([NKEY], BF16) for _ in range(2)]; B_KT = [k.dbuf(f"aKT{j}") for j in range(2)]
            Vs = [ar.alloc([NCH * vtot_max], BF16) for _ in range(2)]; B_V = [k.dbuf(f"aV{j}") for j in range(2)]
            QTs = [ar.alloc([512], BF16) for _ in range(2)]; B_QT = [k.dbuf(f"aQT{j}") for j in range(2)]
            QBs = [ar.alloc([512], BF16) for _ in range(2)]; B_QB = [k.dbuf(f"aQB{j}") for j in range(2)]
            QMs = [ar.alloc([512], BF16) for _ in range(2)]; B_QM = [Buf(f"aQM{j}") for j in range(2)]
            Ps = [ar.alloc([2, 512], BF16) for _ in range(3)]; B_P = [Buf(f"aP{j}") for j in range(3)]
            rc = ar.alloc([16], F32); B_rc = Buf("rc")
            stg = [ar.alloc([4, 128], BF16) for _ in range(2)]; B_stg = [k.dbuf(f"aStg{j}") for j in range(2)]
            t1 = [ar.alloc([128], F32) for _ in range(2)]; B_t1 = [Buf(f"t1{j}") for j in range(2)]
            o1 = [ar.alloc([128], F32) for _ in range(2)]; B_o1 = [Buf(f"o1{j}") for j in range(2)]
            jnk = ar.alloc([128], F32); B_jnk = Buf("ajnk")
            ssd = [ar.alloc([3], F32) for _ in range(2)]; B_ssd = [Buf(f"ssd{j}") for j in range(2)]
            B_S = [Buf("S0"), Buf("S1")]
            B_O = Buf("O")
            cn = {"q": 0, "s": 0, "p": 0, "st": 0, "d": 0}
            for ui, u in enumerate(units):
                kj = ui % 2
                vt = u["vtot"]
                nck = max(chunks) + 1
                k.dma("sp", KTs[kj][:, 0:nck * 128], u["KT"][:, 0:nck * 128], writes=[B_KT[kj]])
                k.dma("sp", Vs[kj][:, 0:nck * vt], u["V"][:, 0:nck * vt], writes=[B_V[kj]])
                Vv = Vs[kj][:, 0:NCH * vt].rearrange("p (c w) -> p c w", w=vt)
                wmax = max(u["vsl"][0][1], u["vsl"][1][1])
                per_bank = 512 // wmax
                for (s0, nq) in qtiles:
                    nsub = nq // 128
                    qj = cn["q"] % 2; cn["q"] += 1
                    k.dma("sp", QTs[qj][:, 0:nq], u["QT"][:, s0:s0 + nq], writes=[B_QT[qj]])
                    if qblend is not None:
                        k.dma("sp", QBs[qj][:, 0:nq], u["QT"][:, s0 + qblend:s0 + qblend + nq], writes=[B_QB[qj]])
                        k.op("dve", lambda e: e.tensor_scalar(out=QMs[qj][:, 0:nq], in0=QTs[qj][:, 0:nq], scalar1=sel[:, 0:1], scalar2=None, op0=ALU.mult),
                             reads=[B_QT[qj], B_gains], writes=[B_QM[qj]])
                        k.op("dve", lambda e: e.scalar_tensor_tensor(out=QTs[qj][:, 0:nq], in0=QBs[qj][:, 0:nq], scalar=sel[:, 1:2], in1=QMs[qj][:, 0:nq], op0=ALU.mult, op1=ALU.add),
                             reads=[B_QB[qj], B_QM[qj], B_gains], writes=[B_QT[qj]])
                    accs = []
                    for a in range(nsub * 2):
                        b = 4 + a // per_bank
                        o = (a % per_bank) * wmax
                        accs.append((b, o))

                    def s_mm(c, sj):
                        for m in range(2):
                            k.op("pe", lambda e, c=c, m=m, sj=sj: e.matmul(bank(2 * sj + m)[:, 0:nq], lhsT=KTs[kj][m * 64:(m + 1) * 64, c * 128:(c + 1) * 128],
                                                                           rhs=QTs[qj][m * 64:(m + 1) * 64, 0:nq], start=True, stop=True),
                                 reads=[B_KT[kj], B_QT[qj]], writes=[B_S[sj]])

                    sidx = cn["s"]
                    s_mm(chunks[0], sidx % 2)
                    for ci, c in enumerate(chunks):
                        sj = (sidx + ci) % 2
                        if ci + 1 < len(chunks):
                            s_mm(chunks[ci + 1], (sidx + ci + 1) % 2)
                        pj = cn["p"] % 3; cn["p"] += 1
                        k.op("act", lambda e, sj=sj, pj=pj: e.activation(out=Ps[pj][:, :, 0:nq], in_=bank(2 * sj, 2).rearrange("p (a b) -> p a b", a=2)[:, :, 0:nq], func=AF.Exp, scale=0.125),
                             reads=[B_S[sj]], writes=[B_P[pj]])
                        seen = set()
                        for a in range(nsub * 2):
                            uu, m = a // 2, a % 2
                            b, o = accs[a]
                            off, w = u["vsl"][m]
                            first_in_bank = (ci == 0) and (b not in seen)
                            seen.add(b)
                            k.op("pe", lambda e, b=b, o=o, w=w, off=off, m=m, uu=uu, pj=pj, c=c, fib=first_in_bank, last=(ci == len(chunks) - 1):
                                 e.matmul(bank(b)[:, o:o + w], lhsT=Ps[pj][:, m, uu * 128:(uu + 1) * 128], rhs=Vv[:, c, off:off + w], start=fib, stop=last, skip_group_check=True),
                                 reads=[B_P[pj], B_V[kj]], writes=[B_O])
                    cn["s"] += len(chunks)
                    sti = cn["st"] % 2; cn["st"] += 1
                    for uu in range(nsub):
                        (b0, o0), (b1, o1_) = accs[2 * uu], accs[2 * uu + 1]
                        w0 = u["vsl"][0][1]; w1 = u["vsl"][1][1]
                        k.op("dve", lambda e, b0=b0, o0=o0, w0=w0, uu=uu: e.reciprocal(out=rc[:, 2 * uu:2 * uu + 1], in_=bank(b0)[:, o0 + w0 - 1:o0 + w0]), reads=[B_O], writes=[B_rc])
                        k.op("dve", lambda e, b1=b1, o1_=o1_, w1=w1, uu=uu: e.reciprocal(out=rc[:, 2 * uu + 1:2 * uu + 2], in_=bank(b1)[:, o1_ + w1 - 1:o1_ + w1]), reads=[B_O], writes=[B_rc])
                        if finish_kind == "gqa":
                            k.op("dve", lambda e, b0=b0, o0=o0, uu=uu, sti=sti: e.tensor_scalar(out=stg[sti][:, uu, 0:64], in0=bank(b0)[:, o0:o0 + 64], scalar1=rc[:, 2 * uu:2 * uu + 1], scalar2=None, op0=ALU.mult),
                                 reads=[B_O, B_rc], writes=[B_stg[sti]])
                            k.op("dve", lambda e, b1=b1, o1_=o1_, uu=uu, sti=sti: e.tensor_scalar(out=stg[sti][:, uu, 64:128], in0=bank(b1)[:, o1_:o1_ + 64], scalar1=rc[:, 2 * uu + 1:2 * uu + 2], scalar2=None, op0=ALU.mult),
                                 reads=[B_O, B_rc], writes=[B_stg[sti]])
                        else:
                            dj = cn["d"] % 2; cn["d"] += 1
                            k.op("dve", lambda e, uu=uu: e.tensor_tensor(out=rc[:, 8 + uu:9 + uu], in0=rc[:, 2 * uu + 1:2 * uu + 2], in1=neglam, op=ALU.mult), reads=[B_rc, B_lam], writes=[B_rc])
                            k.op("dve", lambda e, b0=b0, o0=o0, uu=uu, dj=dj: e.tensor_scalar(out=t1[dj], in0=bank(b0)[:, o0:o0 + 128], scalar1=rc[:, 2 * uu:2 * uu + 1], scalar2=None, op0=ALU.mult),
                                 reads=[B_O, B_rc], writes=[B_t1[dj]])
                            k.op("dve", lambda e, b1=b1, o1_=o1_, uu=uu, dj=dj: e.scalar_tensor_tensor(out=o1[dj], in0=bank(b1)[:, o1_:o1_ + 128], scalar=rc[:, 8 + uu:9 + uu], in1=t1[dj], op0=ALU.mult, op1=ALU.add),
                                 reads=[B_O, B_rc, B_t1[dj]], writes=[B_o1[dj]])
                            k.op("act", lambda e, dj=dj: e.activation(out=jnk, in_=o1[dj], func=AF.Square, accum_out=ssd[dj][:, 0:1]), reads=[B_o1[dj]], writes=[B_jnk, B_ssd[dj]])
                            rstd_chain(ssd[dj], 1.0 / 128, B_ssd[dj])
                            k.op("dve", lambda e, uu=uu, dj=dj, sti=sti: e.scalar_tensor_tensor(out=stg[sti][:, uu, :], in0=o1[dj], scalar=ssd[dj][:, 2:3], in1=sub_r, op0=ALU.mult, op1=ALU.mult),
                                 reads=[B_o1[dj], B_ssd[dj], B_gains], writes=[B_stg[sti]])
                    k.dma("sp", AO[s0:s0 + nq, u["col"]:u["col"] + 128].rearrange("(u p) c -> p u c", p=128), stg[sti][:, 0:nsub, :], reads=[B_stg[sti]])
            k.barrier()

        def na_phase():
            ar.off = PERS
            KTs = [ar.alloc([NKEY], BF16) for _ in range(2)]; B_KT = [k.dbuf(f"nKT{j}") for j in range(2)]
            Vs = [ar.alloc([NCH, 130], BF16) for _ in range(2)]; B_V = [k.dbuf(f"nV{j}") for j in range(2)]
            QTs = [ar.alloc([T], BF16) for _ in range(2)]; B_QT = [k.dbuf(f"nQT{j}") for j in range(2)]
            bst = ar.alloc([NCLS * 2 * 5 * 128], F32); B_bst = k.dbuf("nbst")
            Em = [ar.alloc([NCLS, 2, 640], BF16) for _ in range(2)]; B_E = [Buf(f"nE{j}") for j in range(2)]
            Ps = [ar.alloc([7 * 128], BF16) for _ in range(3)]; B_P = [Buf(f"nP{j}") for j in range(3)]
            rc = [ar.alloc([2], F32) for _ in range(2)]; B_rc = [Buf(f"nrc{j}") for j in range(2)]
            stg = [ar.alloc([4, 128], BF16) for _ in range(2)]; B_stg = [k.dbuf(f"nStg{j}") for j in range(2)]
            B_S = [Buf(f"nS{j}") for j in range(3)]
            B_O = [Buf(f"nO{j}") for j in range(2)]
            cn = {"s": 0, "p": 0, "o": 0}
            for p in range(NAP):
                kj = p % 2
                k.dma("sp", KTs[kj], NAKT[p], writes=[B_KT[kj]])
                k.dma("sp", Vs[kj].rearrange("p a b -> p (a b)"), NAV[p], writes=[B_V[kj]])
                k.dma("sp", QTs[kj], NAQT[p, :, CTX:NKEY], writes=[B_QT[kj]])
                k.dma("sp", bst, nabias[p], writes=[B_bst])
                k.op("act", lambda e, kj=kj: e.activation(out=Em[kj].rearrange("p a b c -> p (a b c)"), in_=bst, func=AF.Exp), reads=[B_bst], writes=[B_E[kj]])
                def unit_info(u):
                    r0 = 2 * u
                    csr = min(min(max(r0 - 4, 0), ROWS - 8), ROWS - 10)
                    cls = {0: 1, 2: 2, ROWS - 4: 3, ROWS - 2: 4}.get(r0, 0)
                    cidx = [CT + csr // 2 + j for j in range(5)] + list(range(CT))
                    return cls, cidx

                units_ = [(u, m) for u in range(NT) for m in range(2)]
                sbase = cn["s"]

                def s_stage(ix):
                    u, m = units_[ix]
                    cls, cidx = unit_info(u)
                    sj = (sbase + ix) % 3
                    Sb = bank(2 * sj, 2)
                    for j, c in enumerate(cidx):
                        k.op("pe", lambda e, j=j, c=c: e.matmul(Sb[:, j * 128:(j + 1) * 128], lhsT=KTs[kj][m * 64:(m + 1) * 64, c * 128:(c + 1) * 128],
                                                             rhs=QTs[kj][m * 64:(m + 1) * 64, u * 128:(u + 1) * 128], start=True, stop=True),
                             reads=[B_KT[kj], B_QT[kj]], writes=[B_S[sj]])

                s_stage(0)
                for ix, (u, m) in enumerate(units_):
                    cls, cidx = unit_info(u)
                    sti = (u // 4) % 2
                    sj = (sbase + ix) % 3
                    Sb = bank(2 * sj, 2)
                    if ix + 1 < len(units_):
                        s_stage(ix + 1)
                    pj = cn["p"] % 3; cn["p"] += 1
                    oj = cn["o"] % 2; cn["o"] += 1
                    k.op("act", lambda e: e.activation(out=Ps[pj], in_=Sb[:, 0:7 * 128], func=AF.Exp, scale=0.125), reads=[B_S[sj]], writes=[B_P[pj]])
                    k.op("dve", lambda e: e.tensor_tensor(out=Ps[pj][:, 0:640], in0=Ps[pj][:, 0:640], in1=Em[kj][:, cls, m, :], op=ALU.mult),
                         reads=[B_P[pj], B_E[kj]], writes=[B_P[pj]])
                    Ob = bank(6 + oj)
                    for j, c in enumerate(cidx):
                        k.op("pe", lambda e, j=j, c=c: e.matmul(Ob[:, 0:65], lhsT=Ps[pj][:, j * 128:(j + 1) * 128], rhs=Vs[kj][:, c, m * 65:(m + 1) * 65], start=(j == 0), stop=(j == 6)),
                             reads=[B_P[pj], B_V[kj]], writes=[B_O[oj]])
                    k.op("dve", lambda e: e.reciprocal(out=rc[oj][:, 0:1], in_=Ob[:, 64:65]), reads=[B_O[oj]], writes=[B_rc[oj]])
                    k.op("dve", lambda e: e.tensor_scalar(out=stg[sti][:, u % 4, m * 64:(m + 1) * 64], in0=Ob[:, 0:64], scalar1=rc[oj][:, 0:1], scalar2=None, op0=ALU.mult),
                         reads=[B_O[oj], B_rc[oj]], writes=[B_stg[sti]])
                    if m == 1 and u % 4 == 3:
                        s0 = CTX + (u - 3) * 128
                        k.dma("sp", AO[s0:s0 + 512, p * 128:(p + 1) * 128].rearrange("(u p) c -> p u c", p=128), stg[sti], reads=[B_stg[sti]])
                cn["s"] = sbase + len(units_)
            k.barrier()

        def wout_phase(l, tiles, dst, xblend=None):
            ar.off = PERS
            w_src = w_out0 if l == 0 else w_out1
            wsb = ar.alloc([KC, D], BF16); B_w = Buf("w_out")
            mark = ar.off
            stg = [ar.alloc([D], F32) for _ in range(2)]; B_stg = [k.dbuf(f"wostg{j}") for j in range(2)]
            load_w(wsb, w_src, KC, D, stg, B_stg, B_w)
            k.barrier()
            ar.off = mark
            ao = [ar.alloc([D], BF16) for _ in range(2)]; B_ao = [k.dbuf(f"ao{j}") for j in range(2)]
            aT = [ar.alloc([KC, 128], BF16) for _ in range(2)]; B_aT = [Buf(f"aT{j}") for j in range(2)]
            xt = [ar.alloc([D], F32) for _ in range(2)]; B_x = [k.dbuf(f"wx{j}") for j in range(2)]
            tmp = [ar.alloc([D], F32) for _ in range(2)]; B_tmp = [Buf(f"wtmp{j}") for j in range(2)]
            xb = [ar.alloc([D], F32) for _ in range(2)]; B_xb = [k.dbuf(f"wxb{j}") for j in range(2)]
            NB = (D + 511) // 512
            def stageA(it):
                i = tiles[it]
                j = it % 2
                k.dma("sp", ao[j], AO[i * 128:(i + 1) * 128, :], writes=[B_ao[j]])
                k.dma("sp", xt[j], src_tile(l, i), writes=[B_x[j]])
                if xblend is not None:
                    k.dma("sp", xb[j], src_tile(l, i + xblend), writes=[B_xb[j]])
                    k.op("dve", lambda e: e.tensor_scalar(out=xt[j], in0=xt[j], scalar1=sel[:, 0:1], scalar2=None, op0=ALU.mult), reads=[B_x[j], B_gains], writes=[B_x[j]])
                    k.op("dve", lambda e: e.scalar_tensor_tensor(out=xt[j], in0=xb[j], scalar=sel[:, 1:2], in1=xt[j], op0=ALU.mult, op1=ALU.add), reads=[B_xb[j], B_x[j], B_gains], writes=[B_x[j]])
                tb = j
                ptb = bank(tb).bitcast(BF16)
                for kc in range(KC):
                    k.op("pe", lambda e, kc=kc: e.transpose(out=ptb[:, kc * 128:(kc + 1) * 128], in_=ao[j][:, kc * 128:(kc + 1) * 128], identity=ident),
                         reads=[B_ao[j], B_ident], writes=[PB[tb]])
                k.op("act", lambda e: e.activation(out=aT[j], in_=ptb[:, 0:KC * 128].rearrange("p (a b) -> p a b", a=KC), func=AF.Copy), reads=[PB[tb]], writes=[B_aT[j]])

            stageA(0)
            for it, i in enumerate(tiles):
                j = it % 2
                if it + 1 < len(tiles):
                    stageA(it + 1)
                yb = 2 + 2 * j
                for nb in range(NB):
                    cw = min(512, D - nb * 512)
                    for kc in range(KC):
                        k.op("pe", lambda e, kc=kc, j=j, nb=nb, cw=cw, yb=yb: e.matmul(bank(yb + nb)[:, 0:cw], lhsT=aT[j][:, kc, :], rhs=wsb[:, kc, nb * 512:nb * 512 + cw], start=(kc == 0), stop=(kc == KC - 1)),
                             reads=[B_aT[j], B_w], writes=[PB[yb + nb]])
                    k.op("dve", lambda e, j=j, nb=nb, cw=cw, yb=yb: e.tensor_tensor(out=tmp[j][:, nb * 512:nb * 512 + cw], in0=bank(yb + nb)[:, 0:cw], in1=modrows[:, 2 * D + nb * 512:2 * D + nb * 512 + cw], op=ALU.mult),
                         reads=[PB[yb + nb], B_mod], writes=[B_tmp[j]])
                k.op("pool", lambda e, j=j: e.tensor_tensor(out=xt[j], in0=xt[j], in1=tmp[j], op=ALU.add), reads=[B_tmp[j], B_x[j]], writes=[B_x[j]])
                k.dma("sp", dst[i * 128:(i + 1) * 128, :], xt[j], reads=[B_x[j]])
            k.barrier()

        def ffn_phase(l, tiles, src, dst, final):
            ar.off = PERS
            wg = ar.alloc([KC, FF], BF16); wu = ar.alloc([KC, FF], BF16); wd = ar.alloc([FC, D], BF16)
            B_wg = Buf("wg"); B_wu = Buf("wu"); B_wd = Buf("wd")
            mark = ar.off
            stg = [ar.alloc([max(FF, D)], F32) for _ in range(2)]; B_stg = [k.dbuf(f"fstg{j}") for j in range(2)]
            load_w(wg, w_gate[l], KC, FF, stg, B_stg, B_wg)
            load_w(wu, w_up[l], KC, FF, stg, B_stg, B_wu)
            load_w(wd, w_down[l], FC, D, stg, B_stg, B_wd)
            k.barrier()
            ar.off = mark
            G = 2
            xg = ar.alloc([G, D], F32); B_xg = [k.dbuf(f"fx{j}") for j in range(G)]
            junk = ar.alloc([D], BF16); B_junk = Buf("fjunk")
            tmp = ar.alloc([D], F32); B_tmp = Buf("ftmp")
            hb = ar.alloc([D], BF16); B_hb = Buf("fhb")
            hT = ar.alloc([KC, G * 128], BF16); B_hT = Buf("fhT")
            AT = ar.alloc([FC, G * 128], BF16); B_AT = Buf("fAT")
            sg = [ar.alloc([G * 128], F32) for _ in range(2)]; B_sg = [Buf(f"sg{j}") for j in range(2)]
            ss = [ar.alloc([3], F32) for _ in range(G)]; B_ss = [Buf(f"fss{j}") for j in range(G)]
            fs = [ar.alloc([3], F32) for _ in range(G)]; B_fs = [Buf(f"ffs{j}") for j in range(G)]
            NB = (D + 511) // 512
            groups = [tiles[a:a + G] for a in range(0, len(tiles), G)]
            fi = 0
            for grp in groups:
                ng = len(grp)
                NQ = ng * 128
                for gi, i in enumerate(grp):
                    k.dma("sp", xg[:, gi, :], src[i * 128:(i + 1) * 128, :], writes=[B_xg[gi]])
                    norm_mod_T(xg[:, gi, :], B_xg[gi], 4 * D, 3 * D, junk, B_junk, ss[gi], B_ss[gi], tmp, B_tmp, hb, B_hb, hT[:, :, gi * 128:(gi + 1) * 128], B_hT, 0)
                for f in range(FC):
                    gb = 1 + fi % 2; ub = 3 + fi % 2; sj = fi % 2; fi += 1
                    for kc in range(KC):
                        k.op("pe", lambda e, kc=kc, f=f, gb=gb: e.matmul(bank(gb)[:, 0:NQ], lhsT=wg[:, kc, f * 128:(f + 1) * 128], rhs=hT[:, kc, 0:NQ], start=(kc == 0), stop=(kc == KC - 1)),
                             reads=[B_wg, B_hT], writes=[PB[gb]])
                    for kc in range(KC):
                        k.op("pe", lambda e, kc=kc, f=f, ub=ub: e.matmul(bank(ub)[:, 0:NQ], lhsT=wu[:, kc, f * 128:(f + 1) * 128], rhs=hT[:, kc, 0:NQ], start=(kc == 0), stop=(kc == KC - 1)),
                             reads=[B_wu, B_hT], writes=[PB[ub]])
                    k.op("act", lambda e, gb=gb, sj=sj: e.activation(out=sg[sj][:, 0:NQ], in_=bank(gb)[:, 0:NQ], func=AF.Silu), reads=[PB[gb]], writes=[B_sg[sj]])
                    k.op("dve", lambda e, ub=ub, sj=sj, f=f: e.tensor_tensor(out=AT[:, f, 0:NQ], in0=bank(ub)[:, 0:NQ], in1=sg[sj][:, 0:NQ], op=ALU.mult), reads=[PB[ub], B_sg[sj]], writes=[B_AT])
                for gi, i in enumerate(grp):
                    for nb in range(NB):
                        cw = min(512, D - nb * 512)
                        yb = 5 + nb
                        for f in range(FC):
                            k.op("pe", lambda e, f=f, gi=gi, nb=nb, cw=cw, yb=yb: e.matmul(bank(yb)[:, 0:cw], lhsT=AT[:, f, gi * 128:(gi + 1) * 128], rhs=wd[:, f, nb * 512:nb * 512 + cw], start=(f == 0), stop=(f == FC - 1)),
                                 reads=[B_AT, B_wd], writes=[PB[yb]])
                        k.op("dve", lambda e, nb=nb, cw=cw, yb=yb: e.tensor_tensor(out=tmp[:, nb * 512:nb * 512 + cw], in0=bank(yb)[:, 0:cw], in1=modrows[:, 5 * D + nb * 512:5 * D + nb * 512 + cw], op=ALU.mult),
                             reads=[PB[yb], B_mod], writes=[B_tmp])
                    xv = xg[:, gi, :]
                    k.op("pool", lambda e, xv=xv: e.tensor_tensor(out=xv, in0=xv, in1=tmp, op=ALU.add), reads=[B_tmp, B_xg[gi]], writes=[B_xg[gi]])
                    if final:
                        k.op("act", lambda e, xv=xv, gi=gi: e.activation(out=junk, in_=xv, func=AF.Square, accum_out=fs[gi][:, 0:1]), reads=[B_xg[gi]], writes=[B_junk, B_fs[gi]])
                        rstd_chain(fs[gi], 1.0 / D, B_fs[gi])
                        k.op("dve", lambda e, xv=xv, gi=gi: e.scalar_tensor_tensor(out=xv, in0=xv, scalar=fs[gi][:, 2:3], in1=fg_r, op0=ALU.mult, op1=ALU.mult),
                             reads=[B_xg[gi], B_fs[gi], B_gains], writes=[B_xg[gi]])
                        k.dma("sp", dst[(i - CT) * 128:(i - CT + 1) * 128, :], xv, reads=[B_xg[gi]])
                    else:
                        k.dma("sp", dst[i * 128:(i + 1) * 128, :], xv, reads=[B_xg[gi]])
            k.barrier()

        ctx_tiles = list(range(CT)); x_tiles = list(range(CT, NCH))
        qt_x = [(CTX + a * 512, 512) for a in range(T // 512)]
        all_chunks = list(range(NCH))
        ada_phase(0, 1, 6 * D)
        proj_phase(0, ctx_tiles)
        na_units = [dict(KT=NAKT[p], V=NAV[p], vtot=130, vsl=[(0, 65), (65, 65)], QT=NAQT[p], col=p * 128) for p in range(NAP)]
        g_units = [dict(KT=GKT[(2 * p) // (GQ // GKV)], V=GV[(2 * p) // (GQ // GKV)], vtot=65, vsl=[(0, 65), (0, 65)], QT=GQT[p], col=(NAP + p) * 128) for p in range(GQP)]
        attn_phase(na_units + g_units, [(0, CTX)], list(range(CT)), "gqa")
        wout_phase(0, ctx_tiles, XM)
        ffn_phase(0, ctx_tiles, XM, X1, False)
        ada_phase(0, 0, 6 * D)
        proj_phase(0, x_tiles)
        na_phase()
        attn_phase(g_units, qt_x, all_chunks, "gqa")
        wout_phase(0, x_tiles, XM)
        ffn_phase(0, x_tiles, XM, X1, False)
        ada_phase(1, 1, 2 * D)
        proj_phase(1, ctx_tiles)
        ada_phase(1, 0, 6 * D)
        proj_phase(1, x_tiles)
        d_units = [dict(KT=DKT[h], V=DV[h], vtot=129, vsl=[(0, 129), (0, 129)], QT=DQT[h], col=h * 128) for h in range(DH)]
        if split:
            own_tiles = list(range(CT, CT + NT // 2))
            qt_own = [(CTX + a * 512, 512) for a in range(T // 2 // 512)]
            attn_phase(d_units, qt_own, all_chunks, "diff", qblend=T // 2)
            wout_phase(1, own_tiles, XM, xblend=NT // 2)
            ffn_phase(1, own_tiles, XM, out_d, True)
        else:
            attn_phase(d_units, qt_x, all_chunks, "diff")
            wout_phase(1, x_tiles, XM)
            ffn_phase(1, x_tiles, XM, out_d, True)
        k.emit()
        print("instr counts", {e: len(k.ops[e]) for e in ENGS}, "signalled", k.ncounts, "sems", k.nsem, flush=True)
        print("max dma sem", sorted([(d.count, n) for n, d in k.dpool.items()])[-6:], flush=True)
    return nc


def _cossin_table(cfg):
    T = cfg["ROWS"] * GRID_W; CTX = cfg["CTX"]
    t = np.arange(T)
    row = (t // GRID_W).astype(np.float32); col = (t % GRID_W).astype(np.float32)
    nf = 16
    inv = (np.float32(10000.0) ** (-np.arange(nf, dtype=np.float32) / np.float32(nf))).astype(np.float32)
    ang = np.concatenate([row[:, None] * inv, col[:, None] * inv], axis=-1).astype(np.float32)
    tab = np.zeros((CTX + T, 64), np.float32)
    tab[:CTX, 0:32] = 1.0
    tab[CTX:, 0:32] = np.cos(ang)
    tab[CTX:, 32:64] = np.sin(ang)
    return tab


def _na_bias_table(cfg, rpb):
    ROWS = cfg["ROWS"]; NA = cfg["NA"]
    out = np.full((NA // 2, 128, 5, 2, 5, 128), NEG, np.float32)
    kk = np.arange(128); qq = np.arange(128)
    for cls, r0 in enumerate((4, 0, 2, ROWS - 4, ROWS - 2)):
        csr = min(min(max(r0 - 4, 0), ROWS - 8), ROWS - 10)
        qr = r0 + qq // 64; qc = qq % 64
        rs = np.clip(qr - 4, 0, ROWS - 8); cs = np.clip(qc - 8, 0, GRID_W - 16)
        for j in range(5):
            kr = csr + 2 * j + kk // 64; kc = kk % 64
            valid = ((kr[:, None] >= rs[None, :]) & (kr[:, None] < rs[None, :] + 8) &
                     (kc[:, None] >= cs[None, :]) & (kc[:, None] < cs[None, :] + 16))
            ri = np.clip(kr[:, None] - qr[None, :] + 7, 0, 14); ci = np.clip(kc[:, None] - qc[None, :] + 15, 0, 30)
            for h in range(NA):
                g = rpb[h][ri, ci]
                out[h // 2, :, cls, h % 2, j, :] = np.where(valid, g, np.float32(NEG))
    return out.reshape(NA // 2, 128, 5 * 2 * 5 * 128)


def make_core_inputs(cfg, inp, b):
    D = cfg["D"]; KC = D // 128
    f = lambda a: np.ascontiguousarray(np.asarray(a, dtype=np.float32))
    cvec = np.concatenate([f(inp["c"][b]).reshape(KC, 128).T, f(inp["c_ctx"]).reshape(KC, 128).T], axis=1)
    lamv = np.concatenate([f(inp["diff_lambda_q1"][0]), f(inp["diff_lambda_k1"][0]), f(inp["diff_lambda_q2"][0]), f(inp["diff_lambda_k2"][0])])[None]
    return {
        "x": f(inp["x"][b]), "ctx": f(inp["ctx"][b]), "cvec": f(cvec),
        "ada_w": f(inp["ada_w"]), "ada_b": f(inp["ada_b"]),
        "ffn_w_gate": f(inp["ffn_w_gate"]), "ffn_w_up": f(inp["ffn_w_up"]), "ffn_w_down": f(inp["ffn_w_down"]),
        "par_w_in": f(inp["par_w_in"][0]), "par_w_out": f(inp["par_w_out"][0]),
        "diff_w_in": f(inp["diff_w_in"][0]), "diff_w_out": f(inp["diff_w_out"][0]),
        "gqa_q_gain": f(inp["gqa_q_gain"]).reshape(1, 64), "gqa_k_gain": f(inp["gqa_k_gain"]).reshape(1, 64),
        "lamv": f(lamv), "diff_subln_gain": f(inp["diff_subln_gain"]).reshape(1, 128),
        "final_norm_gain": f(inp["final_norm_gain"]).reshape(1, D),
        "cossin": _cossin_table(cfg), "nabias": _na_bias_table(cfg, f(inp["na_rpb"][0])),
        "ident": np.eye(128, dtype=np.float32).astype(ml_dtypes.bfloat16),
        "sel": np.tile(np.array([[1.0, 0.0]], np.float32), (128, 1)),
    }


def kernel(**inputs):
    cfg = FULL_CFG
    B = inputs["x"].shape[0]
    nc = build_program(cfg)
    shared = None
    in_maps = []
    T = inputs["x"].shape[1]
    for core in range(8):
        b = core % B
        half = core // B
        m = make_core_inputs(cfg, inputs, b) if shared is None else dict(shared)
        if shared is None:
            shared = m
        else:
            f = lambda a: np.ascontiguousarray(np.asarray(a, dtype=np.float32))
            D = cfg["D"]; KC = D // 128
            m["x"] = f(inputs["x"][b]); m["ctx"] = f(inputs["ctx"][b])
            m["cvec"] = f(np.concatenate([f(inputs["c"][b]).reshape(KC, 128).T, f(inputs["c_ctx"]).reshape(KC, 128).T], axis=1))
        m["sel"] = np.tile(np.array([[1.0, 0.0]] if half == 0 else [[0.0, 1.0]], np.float32), (128, 1))
        in_maps.append(m)
    res = run_bass_kernel_spmd(nc, in_maps, core_ids=list(range(8)))
    out = np.empty((B, T, cfg["D"]), np.float32)
    for core in range(8):
        b = core % B
        half = core // B
        out[b, half * (T // 2):(half + 1) * (T // 2)] = np.asarray(res.results[core]["out"], dtype=np.float32)
    return out
```

```python
import math
import numpy as np
import ml_dtypes
from contextlib import ExitStack
import concourse.bass as bass
import concourse.mybir as mybir
from concourse.bass_utils import run_bass_kernel_spmd

F32 = mybir.dt.float32
BF16 = mybir.dt.bfloat16
U8 = mybir.dt.uint8
AF = mybir.ActivationFunctionType
ALU = mybir.AluOpType
AX = mybir.AxisListType

ENGS = ("pe", "act", "dve", "pool", "sp")
EPOCH = 2000
GRID_W = 64
EPS = 1e-6
NEG = -30000.0

FULL_CFG = dict(D=1024, ROWS=128, CTX=256, NA=8, GQ=8, GKV=2, DH=8, FF=2816)


class Buf:
    __slots__ = ("name", "w", "r", "dsem")

    def __init__(self, name, dsem=None):
        self.name = name
        self.w = {}
        self.r = {}
        self.dsem = dsem


class DmaSem:
    __slots__ = ("h", "count")

    def __init__(self, h):
        self.h = h
        self.count = 0


class _Rec:
    def __getattr__(self, name):
        def f(*a, **kw):
            self.call = (name, a, kw)
            return self
        return f


class K:
    def __init__(self, nc, stack):
        self.nc = nc
        self.stack = stack
        self.ops = {e: [] for e in ENGS}
        self.instr = {e: [] for e in ENGS}
        self.known = {e: {} for e in ENGS}
        self.dsems = []
        self.dpool = {}
        self.nsem = 0

    def sem(self, name):
        h = self.stack.enter_context(self.nc.semaphore(name))
        self.nsem += 1
        return h

    def dsem(self, name):
        d = DmaSem(self.sem(name))
        self.dsems.append(d)
        return d

    def dbuf(self, name):
        if name not in self.dpool:
            self.dpool[name] = self.dsem("d_" + name)
        return Buf(name, dsem=self.dpool[name])

    def fence(self, buf):
        self._merge(buf.r, buf.w)

    @staticmethod
    def _merge(deps, d):
        for s, v in d.items():
            if deps.get(s, (None, 0))[1] < v[1]:
                deps[s] = v

    def _deps(self, reads, writes):
        deps = {}
        for b in reads:
            self._merge(deps, b.w)
        for b in writes:
            if b.r:
                self._merge(deps, b.r)
                self._merge(deps, b.w)
        return deps

    def _post(self, reads, writes, key, val):
        for b in writes:
            if b.r:
                b.w = {}
                b.r = {}
            if b.w.get(key, (None, 0))[1] < val[1]:
                b.w[key] = val
        for b in reads:
            if b in writes:
                continue
            if b.r.get(key, (None, 0))[1] < val[1]:
                b.r[key] = val

    def _waits(self, eng, deps):
        ws = []
        kn = self.known[eng]
        for key, (payload, v) in deps.items():
            if kn.get(key, 0) >= v:
                continue
            kn[key] = v
            ws.append((key, payload, v))
            if key[0] == "E":
                self.instr[payload][v - 1]["needed"] = True
        return ws

    @staticmethod
    def _bind(fn):
        rec = _Rec()
        fn(rec)
        return rec.call

    def op(self, eng, fn, reads=(), writes=()):
        deps = self._deps(reads, writes)
        ws = self._waits(eng, deps)
        r = {"call": self._bind(fn), "waits": ws, "needed": False, "dma": None}
        self.ops[eng].append(r)
        self.instr[eng].append(r)
        self._post(reads, writes, ("E", eng), (eng, len(self.instr[eng])))

    def dma(self, eng, out_ap, in_ap, reads=(), writes=()):
        ds = None
        for b in list(writes) + list(reads):
            if b.dsem is not None:
                ds = b.dsem
                break
        assert ds is not None
        deps = self._deps(reads, writes)
        ws = self._waits(eng, deps)
        ds.count += 16
        self.ops[eng].append({"call": ("dma_start", (), {"out": out_ap, "in_": in_ap}), "waits": ws, "needed": False, "dma": ds})
        self._post(reads, writes, ("D", id(ds)), (ds, ds.count))

    def barrier(self):
        evs = {}
        for e in ENGS:
            if self.instr[e]:
                evs[("E", e)] = (e, len(self.instr[e]))
        for d in self.dsems:
            if d.count > 0:
                evs[("D", id(d))] = (d, d.count)
        for e in ENGS:
            ws = self._waits(e, evs)
            if ws:
                self.ops[e].append({"call": None, "waits": ws, "needed": False, "dma": None})

    def emit(self):
        nc = self.nc
        esems = {}
        for e in ENGS:
            c = 0
            for r in self.instr[e]:
                if r["needed"]:
                    c += 1
                    r["count"] = c
            esems[e] = [self.sem(f"e_{e}_{j}") for j in range((c + EPOCH - 1) // EPOCH)]
        self.ncounts = {e: sum(1 for r in self.instr[e] if r["needed"]) for e in ENGS}

        def resolve(w):
            key, payload, v = w
            if key[0] == "D":
                return payload.h, v
            c = self.instr[payload][v - 1]["count"]
            return esems[payload][(c - 1) // EPOCH], (c - 1) % EPOCH + 1

        with nc.Block() as block:
            def run(e, eng):
                for r in self.ops[eng]:
                    for w in r["waits"]:
                        h, v = resolve(w)
                        e.wait_ge(h, v)
                    if r["call"] is None:
                        continue
                    name, a, kw = r["call"]
                    ins = getattr(e, name)(*a, **kw)
                    if r["dma"] is not None:
                        ins.then_inc(r["dma"].h, 16)
                    elif r["needed"]:
                        c = r["count"]
                        ins.then_inc(esems[eng][(c - 1) // EPOCH], 1)

            @block.tensor
            def _(e):
                run(e, "pe")

            @block.scalar
            def _(e):
                run(e, "act")

            @block.vector
            def _(e):
                run(e, "dve")

            @block.gpsimd
            def _(e):
                run(e, "pool")

            @block.sync
            def _(e):
                run(e, "sp")


class Arena:
    def __init__(self, ap, size):
        self.ap = ap
        self.size = size
        self.off = 0

    def alloc(self, shape, dt):
        esz = 4 if dt == F32 else 2
        n = int(np.prod(shape)) * esz
        n_al = (n + 63) // 64 * 64
        assert self.off + n_al <= self.size, f"arena overflow {self.off}+{n_al}>{self.size}"
        a = self.ap[:, self.off:self.off + n].bitcast(dt)
        self.off += n_al
        if len(shape) == 2:
            a = a.rearrange("p (a b) -> p a b", a=shape[0])
        elif len(shape) == 3:
            a = a.rearrange("p (a b c) -> p a b c", a=shape[0], b=shape[1])
        return a


def build_program(cfg, debug=False, split=True):
    D = cfg["D"]; KC = D // 128; ROWS = cfg["ROWS"]; T = ROWS * GRID_W; CTX = cfg["CTX"]
    NA = cfg["NA"]; GQ = cfg["GQ"]; GKV = cfg["GKV"]; DH = cfg["DH"]; FF = cfg["FF"]; FC = FF // 128
    NKEY = CTX + T; NCH = NKEY // 128; CT = CTX // 128; NT = T // 128
    NAP = NA // 2; GQP = GQ // 2
    W0 = (3 * NA + GQ + 2 * GKV) * 64
    W1 = 3 * DH * 128
    NCLS = 5
    lam_init = 0.8 - 0.6 * math.exp(-0.3 * 1)

    nc = bass.Bass("TRN2", target_bir_lowering=False)

    def din(name, shape, dt=F32):
        return nc.dram_tensor(name, list(shape), dt, kind="ExternalInput").ap()

    def dscr(name, shape, dt):
        return nc.dram_tensor(name, list(shape), dt, kind="ExternalOutput" if debug else "Internal").ap()

    x_in = din("x", [T, D]); ctx_in = din("ctx", [CTX, D])
    cvec = din("cvec", [128, 2 * KC])
    ada_w = din("ada_w", [2, D, 6 * D]); ada_b = din("ada_b", [2, 6 * D])
    w_gate = din("ffn_w_gate", [2, D, FF]); w_up = din("ffn_w_up", [2, D, FF]); w_down = din("ffn_w_down", [2, FF, D])
    w_in0 = din("par_w_in", [D, W0]); w_out0 = din("par_w_out", [D, D])
    w_in1 = din("diff_w_in", [D, W1]); w_out1 = din("diff_w_out", [D, D])
    qgain = din("gqa_q_gain", [1, 64]); kgain = din("gqa_k_gain", [1, 64])
    lamv = din("lamv", [1, 256]); subln = din("diff_subln_gain", [1, 128]); fgain = din("final_norm_gain", [1, D])
    cossin = din("cossin", [NKEY, 64])
    nabias = din("nabias", [NAP, 128, NCLS * 2 * 5 * 128])
    ident_in = din("ident", [128, 128], BF16)
    sel_in = din("sel", [128, 2])
    TO = T // 2 if split else T
    out_d = nc.dram_tensor("out", [TO, D], F32, kind="ExternalOutput").ap()

    XM = dscr("XM", [NKEY, D], F32); X1 = dscr("X1", [NKEY, D], F32)
    AO = dscr("AO", [NKEY, D], BF16)
    NAQT = dscr("NAQT", [NAP, 128, NKEY], BF16); NAKT = dscr("NAKT", [NAP, 128, NKEY], BF16)
    NAV = dscr("NAV", [NAP, 128, NCH * 130], BF16)
    GQT = dscr("GQT", [GQP, 128, NKEY], BF16); GKT = dscr("GKT", [GKV, 128, NKEY], BF16)
    GV = dscr("GV", [GKV, 128, NCH * 65], BF16)
    DQT = dscr("DQT", [DH, 128, NKEY], BF16); DKT = dscr("DKT", [DH, 128, NKEY], BF16)
    DV = dscr("DV", [DH, 128, NCH * 129], BF16)

    with ExitStack() as st:
        k = K(nc, st)
        ARENA = 204 * 1024
        arena_t = st.enter_context(nc.sbuf_tensor("arena", [128, ARENA], U8))
        ar = Arena(arena_t[:, :], ARENA)
        ps = st.enter_context(nc.psum_tensor("ps", [128, 4096], F32))

        def bank(i, n=1):
            return ps[:, i * 512:(i + n) * 512]

        PB = [Buf(f"psb{i}") for i in range(8)]

        ident = ar.alloc([128], BF16); B_ident = k.dbuf("ident")
        modrows = ar.alloc([6 * D], F32); B_mod = Buf("mod")
        csil = ar.alloc([2 * KC], F32); B_csil = k.dbuf("csil")
        ones_f = ar.alloc([128], F32); B_ones = Buf("ones")
        qg_r = ar.alloc([64], F32); kg_r = ar.alloc([64], F32); B_gains = k.dbuf("gains")
        lam_r = ar.alloc([256], F32); sub_r = ar.alloc([128], F32); fg_r = ar.alloc([D], F32)
        lam_s = ar.alloc([8], F32); B_lam = Buf("lam")
        sel = ar.alloc([2], F32)
        PERS = ar.off

        k.dma("sp", ident, ident_in, writes=[B_ident])
        k.dma("sp", csil, cvec, writes=[B_csil])
        k.dma("sp", qg_r, qgain.partition_broadcast(128), writes=[B_gains])
        k.dma("sp", kg_r, kgain.partition_broadcast(128), writes=[B_gains])
        k.dma("sp", lam_r, lamv.partition_broadcast(128), writes=[B_gains])
        k.dma("sp", sub_r, subln.partition_broadcast(128), writes=[B_gains])
        k.dma("sp", fg_r, fgain.partition_broadcast(128), writes=[B_gains])
        k.dma("sp", sel, sel_in, writes=[B_gains])
        k.op("dve", lambda e: e.memset(ones_f, 1.0), writes=[B_ones])
        k.op("act", lambda e: e.activation(out=csil, in_=csil, func=AF.Silu), reads=[B_csil], writes=[B_csil])
        lamtmp = ar.alloc([128], F32)
        PERS = ar.off
        k.op("dve", lambda e: e.tensor_tensor(out=lamtmp[:, 0:64], in0=lam_r[:, 0:64], in1=lam_r[:, 64:128], op=ALU.mult), reads=[B_gains], writes=[B_lam])
        k.op("dve", lambda e: e.tensor_tensor(out=lamtmp[:, 64:128], in0=lam_r[:, 128:192], in1=lam_r[:, 192:256], op=ALU.mult), reads=[B_gains], writes=[B_lam])
        k.op("dve", lambda e: e.tensor_reduce(out=lam_s[:, 0:2], in_=lamtmp.rearrange("p (a b) -> p a b", a=2), axis=AX.X, op=ALU.add), reads=[B_lam], writes=[B_lam])
        k.op("act", lambda e: e.activation(out=lam_s[:, 2:4], in_=lam_s[:, 0:2], func=AF.Exp), reads=[B_lam], writes=[B_lam])
        k.op("dve", lambda e: e.tensor_tensor(out=lam_s[:, 4:5], in0=lam_s[:, 3:4], in1=lam_s[:, 2:3], op=ALU.subtract), reads=[B_lam], writes=[B_lam])
        k.op("dve", lambda e: e.tensor_scalar(out=lam_s[:, 5:6], in0=lam_s[:, 4:5], scalar1=-lam_init, scalar2=None, op0=ALU.add), reads=[B_lam], writes=[B_lam])
        neglam = lam_s[:, 5:6]
        k.op("dve", lambda e: e.tensor_scalar(out=sub_r, in0=sub_r, scalar1=(1.0 - lam_init), scalar2=None, op0=ALU.mult), reads=[B_gains], writes=[B_gains])

        def cast_copy(i, out, in_, reads, writes):
            eng = ("pool", "dve", "act")[i % 3] if True else "dve"
            if eng == "act":
                k.op("act", lambda e: e.activation(out=out, in_=in_, func=AF.Copy), reads=reads, writes=writes)
            else:
                k.op(eng, lambda e: e.tensor_copy(out=out, in_=in_), reads=reads, writes=writes)

        def load_w(dst, src, kcs, n, stg, B_stg, B_dst):
            for kc in range(kcs):
                j = kc % 2
                k.dma("sp", stg[j][:, 0:n], src[kc * 128:(kc + 1) * 128, :], writes=[B_stg[j]])
                cast_copy(kc, dst[:, kc, :], stg[j][:, 0:n], [B_stg[j]], [B_dst])

        def ada_phase(l, cond, ncols):
            ar.off = PERS
            rep = ar.alloc([KC, 128], F32); B_rep = Buf("rep")
            wst = [ar.alloc([KC, 512], F32) for _ in range(2)]; B_wst = [k.dbuf(f"adaw{j}") for j in range(2)]
            bst = [ar.alloc([512], F32) for _ in range(2)]; B_bst = [k.dbuf(f"adab{j}") for j in range(2)]
            for kc in range(KC):
                k.op("dve", lambda e, kc=kc: e.tensor_scalar(out=rep[:, kc, :], in0=ones_f, scalar1=csil[:, cond * KC + kc:cond * KC + kc + 1], scalar2=None, op0=ALU.mult),
                     reads=[B_ones, B_csil], writes=[B_rep])
            nb = ncols // 512
            for n in range(nb):
                j = n % 2
                k.dma("sp", wst[j], ada_w[l, :, n * 512:(n + 1) * 512].rearrange("(a p) n -> p a n", p=128), writes=[B_wst[j]])
                k.dma("sp", bst[j], ada_b[l:l + 1, n * 512:(n + 1) * 512].partition_broadcast(128), writes=[B_bst[j]])
                pb = n % 2
                for kc in range(KC):
                    k.op("pe", lambda e, kc=kc, j=j, pb=pb: e.matmul(bank(pb), lhsT=rep[:, kc, :], rhs=wst[j][:, kc, :], start=(kc == 0), stop=(kc == KC - 1)),
                         reads=[B_rep, B_wst[j]], writes=[PB[pb]])
                k.op("dve", lambda e, n=n, j=j, pb=pb: e.tensor_tensor(out=modrows[:, n * 512:(n + 1) * 512], in0=bank(pb), in1=bst[j], op=ALU.add),
                     reads=[PB[pb], B_bst[j]], writes=[B_mod])
            for off in (D, 4 * D):
                if off < ncols:
                    k.op("dve", lambda e, off=off: e.tensor_scalar(out=modrows[:, off:off + D], in0=modrows[:, off:off + D], scalar1=1.0, scalar2=None, op0=ALU.add),
                         reads=[B_mod], writes=[B_mod])
            k.barrier()

        def rstd_chain(ss, n_inv, nrm_bufs):
            B = nrm_bufs
            w = ss.shape[1] // 3
            k.op("dve", lambda e: e.tensor_scalar(out=ss[:, w:2 * w], in0=ss[:, 0:w], scalar1=n_inv, scalar2=EPS, op0=ALU.mult, op1=ALU.add), reads=[B], writes=[B])
            k.op("act", lambda e: e.activation(out=ss[:, w:2 * w], in_=ss[:, w:2 * w], func=AF.Sqrt), reads=[B], writes=[B])
            k.op("dve", lambda e: e.reciprocal(out=ss[:, 2 * w:3 * w], in_=ss[:, w:2 * w]), reads=[B], writes=[B])

        def norm_mod_T(xt, B_x, sc_off, sh_off, junk, B_junk, ss, B_ss, tmp, B_tmp, hb, B_hb, hT_dst, B_hT, pbank):
            k.op("act", lambda e: e.activation(out=junk, in_=xt, func=AF.Square, accum_out=ss[:, 0:1]), reads=[B_x], writes=[B_junk, B_ss])
            rstd_chain(ss, 1.0 / D, B_ss)
            k.op("dve", lambda e: e.scalar_tensor_tensor(out=tmp, in0=xt, scalar=ss[:, 2:3], in1=modrows[:, sc_off:sc_off + D], op0=ALU.mult, op1=ALU.mult),
                 reads=[B_x, B_ss, B_mod], writes=[B_tmp])
            k.op("pool", lambda e: e.tensor_tensor(out=hb, in0=tmp, in1=modrows[:, sh_off:sh_off + D], op=ALU.add), reads=[B_tmp, B_mod], writes=[B_hb])
            pt = bank(pbank).bitcast(BF16)
            for kc in range(KC):
                k.op("pe", lambda e, kc=kc: e.transpose(out=pt[:, kc * 128:(kc + 1) * 128], in_=hb[:, kc * 128:(kc + 1) * 128], identity=ident),
                     reads=[B_hb, B_ident], writes=[PB[pbank]])
            k.op("act", lambda e: e.activation(out=hT_dst, in_=pt[:, 0:KC * 128].rearrange("p (a b) -> p a b", a=KC), func=AF.Copy), reads=[PB[pbank]], writes=[B_hT])

        def src_tile(l, i):
            if l == 0:
                return ctx_in[i * 128:(i + 1) * 128, :] if i < CT else x_in[(i - CT) * 128:(i - CT + 1) * 128, :]
            return X1[i * 128:(i + 1) * 128, :]

        def proj_phase(l, tiles):
            ar.off = PERS
            WW = W0 if l == 0 else W1
            w_src = w_in0 if l == 0 else w_in1
            wsb = ar.alloc([KC, WW], BF16); B_w = Buf("w_in")
            mark = ar.off
            stg = [ar.alloc([WW], F32) for _ in range(2)]; B_stg = [k.dbuf(f"wstg{j}") for j in range(2)]
            load_w(wsb, w_src, KC, WW, stg, B_stg, B_w)
            k.barrier()
            ar.off = mark
            xt = [ar.alloc([D], F32) for _ in range(2)]; B_x = [k.dbuf(f"px{j}") for j in range(2)]
            cst = [ar.alloc([64], F32) for _ in range(2)]; B_cs = [k.dbuf(f"pcs{j}") for j in range(2)]
            junk = ar.alloc([D], F32); B_junk = Buf("junk")
            ss = [ar.alloc([3], F32) for _ in range(2)]; B_ss = [Buf(f"ss{j}") for j in range(2)]
            tmp = ar.alloc([D], F32); B_tmp = Buf("tmp")
            hb = [ar.alloc([D], BF16) for _ in range(2)]; B_hb = [Buf(f"hb{j}") for j in range(2)]
            hT = [ar.alloc([KC, 128], BF16) for _ in range(2)]; B_hT = [Buf(f"hT{j}") for j in range(2)]
            sq = [ar.alloc([512], F32) for _ in range(2)]; B_sq = [Buf(f"sq{j}") for j in range(2)]
            qn = [ar.alloc([512], F32) for _ in range(2)]; B_qn = [Buf(f"qn{j}") for j in range(2)]
            ra = [ar.alloc([512], F32) for _ in range(2)]; B_ra = [Buf(f"ra{j}") for j in range(2)]
            rb = [ar.alloc([512], F32) for _ in range(2)]; B_rb = [Buf(f"rb{j}") for j in range(2)]
            nss = [ar.alloc([24], F32) for _ in range(2)]; B_nss = [Buf(f"nss{j}") for j in range(2)]
            tm = [ar.alloc([512], BF16) for _ in range(3)]; B_tm = [Buf(f"tm{j}") for j in range(3)]
            stT = [ar.alloc([4, 128], BF16) for _ in range(3)]; B_stT = [k.dbuf(f"stT{j}_{l}{int(tiles[0] < CT)}") for j in range(3)]
            vw = 65 if l == 0 else 129
            nvh = (NA + GKV) if l == 0 else DH
            vst = [ar.alloc([nvh, vw], BF16) for _ in range(2)]; B_vst = [k.dbuf(f"vst{j}_{l}{int(tiles[0] < CT)}") for j in range(2)]
            for j in range(2):
                k.op("pool", lambda e, j=j: e.memset(vst[j], 1.0), writes=[B_vst[j]])
                k.fence(B_vst[j])
            cnt = {"pj": 0, "tp": 0, "pp": 0, "tm": 0, "st": 0}

            def post_qk(pj, nm, s0, dsts, norm_gain, do_rope, csb, B_csb, dup=False):
                w = nm * 64
                src = bank(pj)[:, 0:w]
                srcB = PB[pj]
                pp = cnt["pp"] % 2; cnt["pp"] += 1
                if norm_gain is not None:
                    k.op("act", lambda e: e.activation(out=sq[pp][:, 0:w], in_=src, func=AF.Square), reads=[srcB], writes=[B_sq[pp]])
                    k.op("dve", lambda e: e.tensor_reduce(out=nss[pp][:, 0:nm], in_=sq[pp][:, 0:w].rearrange("p (a b) -> p a b", a=nm), axis=AX.X, op=ALU.add),
                         reads=[B_sq[pp]], writes=[B_nss[pp]])
                    k.op("dve", lambda e: e.tensor_scalar(out=nss[pp][:, 8:8 + nm], in0=nss[pp][:, 0:nm], scalar1=1.0 / 64, scalar2=EPS, op0=ALU.mult, op1=ALU.add), reads=[B_nss[pp]], writes=[B_nss[pp]])
                    k.op("act", lambda e: e.activation(out=nss[pp][:, 8:8 + nm], in_=nss[pp][:, 8:8 + nm], func=AF.Sqrt), reads=[B_nss[pp]], writes=[B_nss[pp]])
                    k.op("dve", lambda e: e.reciprocal(out=nss[pp][:, 16:16 + nm], in_=nss[pp][:, 8:8 + nm]), reads=[B_nss[pp]], writes=[B_nss[pp]])
                    k.op("dve", lambda e: e.tensor_tensor(out=qn[pp][:, 0:w].rearrange("p (a b) -> p a b", a=nm), in0=src.rearrange("p (a b) -> p a b", a=nm),
                                                          in1=nss[pp][:, 16:16 + nm][:, :, None].to_broadcast([128, nm, 64]), op=ALU.mult),
                         reads=[srcB, B_nss[pp]], writes=[B_qn[pp]])
                    k.op("pool", lambda e: e.tensor_tensor(out=qn[pp][:, 0:w].rearrange("p (a b) -> p a b", a=nm), in0=qn[pp][:, 0:w].rearrange("p (a b) -> p a b", a=nm),
                                                           in1=norm_gain[:, None, :].to_broadcast([128, nm, 64]), op=ALU.mult),
                         reads=[B_qn[pp], B_gains], writes=[B_qn[pp]])
                    src = qn[pp][:, 0:w]; srcB = B_qn[pp]
                ti = cnt["tm"] % 3; cnt["tm"] += 1
                if do_rope:
                    s4 = src.rearrange("p (a b c) -> p a b c", a=nm, c=2)
                    cosb = csb[:, 0:32][:, None, :, None].to_broadcast([128, nm, 32, 2])
                    sinb = csb[:, 32:64][:, None, :, None].to_broadcast([128, nm, 32, 2])
                    A = ra[pp][:, 0:w].rearrange("p (a b c) -> p a b c", a=nm, c=2)
                    Bm = rb[pp][:, 0:w].rearrange("p (a b c) -> p a b c", a=nm, c=2)
                    o4 = tm[ti][:, 0:w].rearrange("p (a b c) -> p a b c", a=nm, c=2)
                    k.op("dve", lambda e: e.tensor_tensor(out=A, in0=s4, in1=cosb, op=ALU.mult), reads=[srcB, B_csb], writes=[B_ra[pp]])
                    k.op("dve", lambda e: e.tensor_tensor(out=Bm, in0=s4, in1=sinb, op=ALU.mult), reads=[srcB, B_csb], writes=[B_rb[pp]])
                    k.op("pool", lambda e: e.tensor_tensor(out=o4[:, :, :, 0], in0=A[:, :, :, 0], in1=Bm[:, :, :, 1], op=ALU.subtract), reads=[B_ra[pp], B_rb[pp]], writes=[B_tm[ti]])
                    k.op("pool", lambda e: e.tensor_tensor(out=o4[:, :, :, 1], in0=Bm[:, :, :, 0], in1=A[:, :, :, 1], op=ALU.add), reads=[B_ra[pp], B_rb[pp]], writes=[B_tm[ti]])
                else:
                    k.op("act", lambda e: e.activation(out=tm[ti][:, 0:w], in_=src, func=AF.Copy), reads=[srcB], writes=[B_tm[ti]])
                return dict(ti=ti, nm=nm, w=w, dsts=dsts, dup=dup)

            def post_qk2(stt):
                ti = stt["ti"]; nm = stt["nm"]; w = stt["w"]; dsts = stt["dsts"]; dup = stt["dup"]
                if dup:
                    chunks = [(m * 64, 64) for m in range(nm)]
                else:
                    chunks = [(c * 128, 128) for c in range(w // 128)]
                tb = 4 + cnt["tp"] % 2; cnt["tp"] += 1
                ptb = bank(tb).bitcast(BF16)
                sti = cnt["st"] % 3; cnt["st"] += 1
                for ci, (c0, cw) in enumerate(chunks):
                    if dup:
                        for hlf in range(2):
                            k.op("pe", lambda e, ci=ci, c0=c0, hlf=hlf: e.transpose(out=ptb[hlf * 64:(hlf + 1) * 64, ci * 128:(ci + 1) * 128], in_=tm[ti][:, c0:c0 + 64], identity=ident),
                                 reads=[B_tm[ti], B_ident], writes=[PB[tb]])
                    else:
                        k.op("pe", lambda e, ci=ci, c0=c0: e.transpose(out=ptb[:, ci * 128:(ci + 1) * 128], in_=tm[ti][:, c0:c0 + 128], identity=ident),
                             reads=[B_tm[ti], B_ident], writes=[PB[tb]])
                nch_ = len(chunks)
                k.op("dve", lambda e: e.tensor_copy(out=stT[sti][:, 0:nch_, :], in_=ptb[:, 0:nch_ * 128].rearrange("p (a b) -> p a b", a=nch_)), reads=[PB[tb]], writes=[B_stT[sti]])
                for ci in range(nch_):
                    k.dma("sp", dsts[ci], stT[sti][:, ci, :], reads=[B_stT[sti]])

            def stageA(it):
                i = tiles[it]
                j = it % 2
                s0 = i * 128
                k.dma("sp", xt[j], src_tile(l, i), writes=[B_x[j]])
                k.dma("sp", cst[j], cossin[s0:s0 + 128, :], writes=[B_cs[j]])
                norm_mod_T(xt[j], B_x[j], 1 * D, 0, junk, B_junk, ss[j], B_ss[j], tmp, B_tmp, hb[j], B_hb[j], hT[j], B_hT[j], 0)

            stageA(0)
            for it, i in enumerate(tiles):
                j = it % 2
                s0 = i * 128
                if it + 1 < len(tiles):
                    stageA(it + 1)
                if l == 0:
                    blocks = []
                    c = 0
                    for nm_total, kind in ((NA, "naq"), (NA, "nak"), (NA, "nav"), (GQ, "gq"), (GKV, "gk"), (GKV, "gv")):
                        m0 = 0
                        while m0 < nm_total:
                            nm = min(8, nm_total - m0)
                            blocks.append((kind, m0, nm, c + m0 * 64))
                            m0 += nm
                        c += nm_total * 64
                else:
                    blocks = []
                    for kind, base in (("dq", 0), ("dk", DH * 128), ("dv", 2 * DH * 128)):
                        m0 = 0
                        while m0 < 2 * DH:
                            nm = min(8, 2 * DH - m0)
                            blocks.append((kind, m0, nm, base + m0 * 64))
                            m0 += nm
                vj = it % 2

                def do_p1(blk):
                    kind, m0, nm, c0, pj = blk
                    if kind in ("naq", "nak"):
                        dst = NAQT if kind == "naq" else NAKT
                        return post_qk(pj, nm, s0, [dst[(m0 // 2) + ci, :, s0:s0 + 128] for ci in range(nm // 2)], None, False, None, None)
                    elif kind == "gq":
                        return post_qk(pj, nm, s0, [GQT[(m0 // 2) + ci, :, s0:s0 + 128] for ci in range(nm // 2)], qg_r, True, cst[j], B_cs[j])
                    elif kind == "gk":
                        return post_qk(pj, nm, s0, [GKT[m0 + ci, :, s0:s0 + 128] for ci in range(nm)], kg_r, True, cst[j], B_cs[j], dup=True)
                    elif kind in ("dq", "dk"):
                        dst = DQT if kind == "dq" else DKT
                        return post_qk(pj, nm, s0, [dst[(m0 // 2) + ci, :, s0:s0 + 128] for ci in range(nm // 2)], None, True, cst[j], B_cs[j])
                    elif kind in ("nav", "gv"):
                        h0 = m0 if kind == "nav" else NA + m0
                        k.op("act", lambda e: e.activation(out=vst[vj][:, h0:h0 + nm, 0:64], in_=bank(pj)[:, 0:nm * 64].rearrange("p (a b) -> p a b", a=nm), func=AF.Copy),
                             reads=[PB[pj]], writes=[B_vst[vj]])
                    elif kind == "dv":
                        h0 = m0 // 2
                        k.op("act", lambda e: e.activation(out=vst[vj][:, h0:h0 + nm // 2, 0:128], in_=bank(pj)[:, 0:nm * 64].rearrange("p (a b) -> p a b", a=nm // 2), func=AF.Copy),
                             reads=[PB[pj]], writes=[B_vst[vj]])
                    return None

                q1 = []; q2 = []
                todo = [b for b in blocks if not (b[0] == "dq" and i < CT)]
                for blk in todo + [None, None]:
                    if blk is not None:
                        (kind, m0, nm, c0) = blk
                        w = nm * 64
                        pj = 1 + cnt["pj"] % 3; cnt["pj"] += 1
                        for kc in range(KC):
                            k.op("pe", lambda e, kc=kc: e.matmul(bank(pj)[:, 0:w], lhsT=hT[j][:, kc, :], rhs=wsb[:, kc, c0:c0 + w], start=(kc == 0), stop=(kc == KC - 1)),
                                 reads=[B_hT[j], B_w], writes=[PB[pj]])
                    st2 = q2.pop(0) if q2 else None
                    if q1:
                        stt = do_p1(q1.pop(0))
                        if stt is not None:
                            q2.append(stt)
                    if st2 is not None:
                        post_qk2(st2)
                    if blk is not None:
                        q1.append((kind, m0, nm, c0, pj))
                while q1 or q2:
                    st2 = q2.pop(0) if q2 else None
                    if q1:
                        stt = do_p1(q1.pop(0))
                        if stt is not None:
                            q2.append(stt)
                    if st2 is not None:
                        post_qk2(st2)
                if l == 0:
                    for p in range(NAP):
                        k.dma("sp", NAV[p, :, i * 130:(i + 1) * 130], vst[vj][:, 2 * p:2 * p + 2, :].rearrange("p a b -> p (a b)"), reads=[B_vst[vj]])
                    for g in range(GKV):
                        k.dma("sp", GV[g, :, i * 65:(i + 1) * 65], vst[vj][:, NA + g, :], reads=[B_vst[vj]])
                else:
                    for h in range(DH):
                        k.dma("sp", DV[h, :, i * 129:(i + 1) * 129], vst[vj][:, h, :], reads=[B_vst[vj]])
            k.barrier()

        def attn_phase(units, qtiles, chunks, finish_kind, qblend=None):
            ar.off = PERS
            vtot_max = max(u["vtot"] for u in units)
            KTs = [ar.alloc([NKEY], BF16) for _ in range(2)]; B_KT = [k.dbuf(f"aKT{j}") for j in range(2)]
            Vs = [ar.alloc([NCH * vtot_max], BF16) for _ in range(2)]; B_V = [k.dbuf(f"aV{j}") for j in range(2)]
            QTs = [ar.alloc([512], BF16) for _ in range(2)]; B_QT = [k.dbuf(f"aQT{j}") for j in range(2)]
            QBs = [ar.alloc([512], BF16) for _ in range(2)]; B_QB = [k.dbuf(f"aQB{j}") for j in range(2)]
            QMs = [ar.alloc([512], BF16) for _ in range(2)]; B_QM = [Buf(f"aQM{j}") for j in range(2)]
            Ps = [ar.alloc([2, 512], BF16) for _ in range(3)]; B_P = [Buf(f"aP{j}") for j in range(3)]
            rc = ar.alloc([16], F32); B_rc = Buf("rc")
            stg = [ar.alloc([4, 128], BF16) for _ in range(2)]; B_stg = [k.dbuf(f"aStg{j}") for j in range(2)]
            t1 = [ar.alloc([128], F32) for _ in range(2)]; B_t1 = [Buf(f"t1{j}") for j in range(2)]
            o1 = [ar.alloc([128], F32) for _ in range(2)]; B_o1 = [Buf(f"o1{j}") for j in range(2)]
            jnk = ar.alloc([128], F32); B_jnk = Buf("ajnk")
            ssd = [ar.alloc([3], F32) for _ in range(2)]; B_ssd = [Buf(f"ssd{j}") for j in range(2)]
            nsb = 3 if finish_kind == "gqa" else 2
            B_S = [Buf(f"S{j}") for j in range(nsb)]
            B_O = Buf("O")
            cn = {"q": 0, "s": 0, "p": 0, "st": 0, "d": 0}
            for ui, u in enumerate(units):
                kj = ui % 2
                vt = u["vtot"]
                nck = max(chunks) + 1
                k.dma("sp", KTs[kj][:, 0:nck * 128], u["KT"][:, 0:nck * 128], writes=[B_KT[kj]])
                k.dma("sp", Vs[kj][:, 0:nck * vt], u["V"][:, 0:nck * vt], writes=[B_V[kj]])
                Vv = Vs[kj][:, 0:NCH * vt].rearrange("p (c w) -> p c w", w=vt)
                wmax = max(u["vsl"][0][1], u["vsl"][1][1])
                per_bank = 512 // wmax
                for (s0, nq) in qtiles:
                    nsub = nq // 128
                    qj = cn["q"] % 2; cn["q"] += 1
                    k.dma("sp", QTs[qj][:, 0:nq], u["QT"][:, s0:s0 + nq], writes=[B_QT[qj]])
                    if qblend is not None:
                        k.dma("sp", QBs[qj][:, 0:nq], u["QT"][:, s0 + qblend:s0 + qblend + nq], writes=[B_QB[qj]])
                        k.op("dve", lambda e: e.tensor_scalar(out=QMs[qj][:, 0:nq], in0=QTs[qj][:, 0:nq], scalar1=sel[:, 0:1], scalar2=None, op0=ALU.mult),
                             reads=[B_QT[qj], B_gains], writes=[B_QM[qj]])
                        k.op("dve", lambda e: e.scalar_tensor_tensor(out=QTs[qj][:, 0:nq], in0=QBs[qj][:, 0:nq], scalar=sel[:, 1:2], in1=QMs[qj][:, 0:nq], op0=ALU.mult, op1=ALU.add),
                             reads=[B_QB[qj], B_QM[qj], B_gains], writes=[B_QT[qj]])
                    accs = []
                    for a in range(nsub * 2):
                        b = 2 * nsb + a // per_bank
                        o = (a % per_bank) * wmax
                        accs.append((b, o))

                    def s_mm(c, sj):
                        for m in range(2):
                            k.op("pe", lambda e, c=c, m=m, sj=sj: e.matmul(bank(2 * sj + m)[:, 0:nq], lhsT=KTs[kj][m * 64:(m + 1) * 64, c * 128:(c + 1) * 128],
                                                                           rhs=QTs[qj][m * 64:(m + 1) * 64, 0:nq], start=True, stop=True),
                                 reads=[B_KT[kj], B_QT[qj]], writes=[B_S[sj]])

                    sidx = cn["s"]
                    la = nsb - 1
                    for pre in range(min(la, len(chunks))):
                        s_mm(chunks[pre], (sidx + pre) % nsb)
                    for ci, c in enumerate(chunks):
                        sj = (sidx + ci) % nsb
                        if ci + la < len(chunks):
                            s_mm(chunks[ci + la], (sidx + ci + la) % nsb)
                        pj = cn["p"] % 3; cn["p"] += 1
                        k.op("act", lambda e, sj=sj, pj=pj: e.activation(out=Ps[pj][:, :, 0:nq], in_=bank(2 * sj, 2).rearrange("p (a b) -> p a b", a=2)[:, :, 0:nq], func=AF.Exp, scale=0.125),
                             reads=[B_S[sj]], writes=[B_P[pj]])
                        seen = set()
                        for a in range(nsub * 2):
                            uu, m = a // 2, a % 2
                            b, o = accs[a]
                            off, w = u["vsl"][m]
                            first_in_bank = (ci == 0) and (b not in seen)
                            seen.add(b)
                            k.op("pe", lambda e, b=b, o=o, w=w, off=off, m=m, uu=uu, pj=pj, c=c, fib=first_in_bank, last=(ci == len(chunks) - 1):
                                 e.matmul(bank(b)[:, o:o + w], lhsT=Ps[pj][:, m, uu * 128:(uu + 1) * 128], rhs=Vv[:, c, off:off + w], start=fib, stop=last, skip_group_check=True),
                                 reads=[B_P[pj], B_V[kj]], writes=[B_O])
                    cn["s"] += len(chunks)
                    sti = cn["st"] % 2; cn["st"] += 1
                    for uu in range(nsub):
                        (b0, o0), (b1, o1_) = accs[2 * uu], accs[2 * uu + 1]
                        w0 = u["vsl"][0][1]; w1 = u["vsl"][1][1]
                        k.op("dve", lambda e, b0=b0, o0=o0, w0=w0, uu=uu: e.reciprocal(out=rc[:, 2 * uu:2 * uu + 1], in_=bank(b0)[:, o0 + w0 - 1:o0 + w0]), reads=[B_O], writes=[B_rc])
                        k.op("dve", lambda e, b1=b1, o1_=o1_, w1=w1, uu=uu: e.reciprocal(out=rc[:, 2 * uu + 1:2 * uu + 2], in_=bank(b1)[:, o1_ + w1 - 1:o1_ + w1]), reads=[B_O], writes=[B_rc])
                        if finish_kind == "gqa":
                            k.op("dve", lambda e, b0=b0, o0=o0, uu=uu, sti=sti: e.tensor_scalar(out=stg[sti][:, uu, 0:64], in0=bank(b0)[:, o0:o0 + 64], scalar1=rc[:, 2 * uu:2 * uu + 1], scalar2=None, op0=ALU.mult),
                                 reads=[B_O, B_rc], writes=[B_stg[sti]])
                            k.op("dve", lambda e, b1=b1, o1_=o1_, uu=uu, sti=sti: e.tensor_scalar(out=stg[sti][:, uu, 64:128], in0=bank(b1)[:, o1_:o1_ + 64], scalar1=rc[:, 2 * uu + 1:2 * uu + 2], scalar2=None, op0=ALU.mult),
                                 reads=[B_O, B_rc], writes=[B_stg[sti]])
                        else:
                            dj = cn["d"] % 2; cn["d"] += 1
                            k.op("dve", lambda e, uu=uu: e.tensor_tensor(out=rc[:, 8 + uu:9 + uu], in0=rc[:, 2 * uu + 1:2 * uu + 2], in1=neglam, op=ALU.mult), reads=[B_rc, B_lam], writes=[B_rc])
                            k.op("dve", lambda e, b0=b0, o0=o0, uu=uu, dj=dj: e.tensor_scalar(out=t1[dj], in0=bank(b0)[:, o0:o0 + 128], scalar1=rc[:, 2 * uu:2 * uu + 1], scalar2=None, op0=ALU.mult),
                                 reads=[B_O, B_rc], writes=[B_t1[dj]])
                            k.op("dve", lambda e, b1=b1, o1_=o1_, uu=uu, dj=dj: e.scalar_tensor_tensor(out=o1[dj], in0=bank(b1)[:, o1_:o1_ + 128], scalar=rc[:, 8 + uu:9 + uu], in1=t1[dj], op0=ALU.mult, op1=ALU.add),
                                 reads=[B_O, B_rc, B_t1[dj]], writes=[B_o1[dj]])
                            k.op("act", lambda e, dj=dj: e.activation(out=jnk, in_=o1[dj], func=AF.Square, accum_out=ssd[dj][:, 0:1]), reads=[B_o1[dj]], writes=[B_jnk, B_ssd[dj]])
                            rstd_chain(ssd[dj], 1.0 / 128, B_ssd[dj])
                            k.op("dve", lambda e, uu=uu, dj=dj, sti=sti: e.scalar_tensor_tensor(out=stg[sti][:, uu, :], in0=o1[dj], scalar=ssd[dj][:, 2:3], in1=sub_r, op0=ALU.mult, op1=ALU.mult),
                                 reads=[B_o1[dj], B_ssd[dj], B_gains], writes=[B_stg[sti]])
                    k.dma("sp", AO[s0:s0 + nq, u["col"]:u["col"] + 128].rearrange("(u p) c -> p u c", p=128), stg[sti][:, 0:nsub, :], reads=[B_stg[sti]])
            k.barrier()

        def na_phase():
            ar.off = PERS
            KTs = [ar.alloc([NKEY], BF16) for _ in range(2)]; B_KT = [k.dbuf(f"nKT{j}") for j in range(2)]
            Vs = [ar.alloc([NCH, 130], BF16) for _ in range(2)]; B_V = [k.dbuf(f"nV{j}") for j in range(2)]
            QTs = [ar.alloc([T], BF16) for _ in range(2)]; B_QT = [k.dbuf(f"nQT{j}") for j in range(2)]
            bst = ar.alloc([NCLS * 2 * 5 * 128], F32); B_bst = k.dbuf("nbst")
            Em = [ar.alloc([NCLS, 2, 640], BF16) for _ in range(2)]; B_E = [Buf(f"nE{j}") for j in range(2)]
            Ps = [ar.alloc([7 * 128], BF16) for _ in range(3)]; B_P = [Buf(f"nP{j}") for j in range(3)]
            rc = [ar.alloc([2], F32) for _ in range(2)]; B_rc = [Buf(f"nrc{j}") for j in range(2)]
            stg = [ar.alloc([4, 128], BF16) for _ in range(2)]; B_stg = [k.dbuf(f"nStg{j}") for j in range(2)]
            B_S = [Buf(f"nS{j}") for j in range(3)]
            B_O = [Buf(f"nO{j}") for j in range(2)]
            cn = {"s": 0, "p": 0, "o": 0}
            for p in range(NAP):
                kj = p % 2
                k.dma("sp", KTs[kj], NAKT[p], writes=[B_KT[kj]])
                k.dma("sp", Vs[kj].rearrange("p a b -> p (a b)"), NAV[p], writes=[B_V[kj]])
                k.dma("sp", QTs[kj], NAQT[p, :, CTX:NKEY], writes=[B_QT[kj]])
                k.dma("sp", bst, nabias[p], writes=[B_bst])
                k.op("act", lambda e, kj=kj: e.activation(out=Em[kj].rearrange("p a b c -> p (a b c)"), in_=bst, func=AF.Exp), reads=[B_bst], writes=[B_E[kj]])
                def unit_info(u):
                    r0 = 2 * u
                    csr = min(min(max(r0 - 4, 0), ROWS - 8), ROWS - 10)
                    cls = {0: 1, 2: 2, ROWS - 4: 3, ROWS - 2: 4}.get(r0, 0)
                    cidx = [CT + csr // 2 + j for j in range(5)] + list(range(CT))
                    return cls, cidx

                units_ = [(u, m) for u in range(NT) for m in range(2)]
                sbase = cn["s"]

                def s_stage(ix):
                    u, m = units_[ix]
                    cls, cidx = unit_info(u)
                    sj = (sbase + ix) % 3
                    Sb = bank(2 * sj, 2)
                    for j, c in enumerate(cidx):
                        k.op("pe", lambda e, j=j, c=c: e.matmul(Sb[:, j * 128:(j + 1) * 128], lhsT=KTs[kj][m * 64:(m + 1) * 64, c * 128:(c + 1) * 128],
                                                             rhs=QTs[kj][m * 64:(m + 1) * 64, u * 128:(u + 1) * 128], start=True, stop=True),
                             reads=[B_KT[kj], B_QT[kj]], writes=[B_S[sj]])

                s_stage(0)
                for ix, (u, m) in enumerate(units_):
                    cls, cidx = unit_info(u)
                    sti = (u // 4) % 2
                    sj = (sbase + ix) % 3
                    Sb = bank(2 * sj, 2)
                    if ix + 1 < len(units_):
                        s_stage(ix + 1)
                    pj = cn["p"] % 3; cn["p"] += 1
                    oj = cn["o"] % 2; cn["o"] += 1
                    k.op("act", lambda e: e.activation(out=Ps[pj], in_=Sb[:, 0:7 * 128], func=AF.Exp, scale=0.125), reads=[B_S[sj]], writes=[B_P[pj]])
                    k.op("dve", lambda e: e.tensor_tensor(out=Ps[pj][:, 0:640], in0=Ps[pj][:, 0:640], in1=Em[kj][:, cls, m, :], op=ALU.mult),
                         reads=[B_P[pj], B_E[kj]], writes=[B_P[pj]])
                    Ob = bank(6 + oj)
                    for j, c in enumerate(cidx):
                        k.op("pe", lambda e, j=j, c=c: e.matmul(Ob[:, 0:65], lhsT=Ps[pj][:, j * 128:(j + 1) * 128], rhs=Vs[kj][:, c, m * 65:(m + 1) * 65], start=(j == 0), stop=(j == 6)),
                             reads=[B_P[pj], B_V[kj]], writes=[B_O[oj]])
                    k.op("dve", lambda e: e.reciprocal(out=rc[oj][:, 0:1], in_=Ob[:, 64:65]), reads=[B_O[oj]], writes=[B_rc[oj]])
                    k.op("dve", lambda e: e.tensor_scalar(out=stg[sti][:, u % 4, m * 64:(m + 1) * 64], in0=Ob[:, 0:64], scalar1=rc[oj][:, 0:1], scalar2=None, op0=ALU.mult),
                         reads=[B_O[oj], B_rc[oj]], writes=[B_stg[sti]])
                    if m == 1 and u % 4 == 3:
                        s0 = CTX + (u - 3) * 128
                        k.dma("sp", AO[s0:s0 + 512, p * 128:(p + 1) * 128].rearrange("(u p) c -> p u c", p=128), stg[sti], reads=[B_stg[sti]])
                cn["s"] = sbase + len(units_)
            k.barrier()

        def wout_phase(l, tiles, dst, xblend=None):
            ar.off = PERS
            w_src = w_out0 if l == 0 else w_out1
            wsb = ar.alloc([KC, D], BF16); B_w = Buf("w_out")
            mark = ar.off
            stg = [ar.alloc([D], F32) for _ in range(2)]; B_stg = [k.dbuf(f"wostg{j}") for j in range(2)]
            load_w(wsb, w_src, KC, D, stg, B_stg, B_w)
            k.barrier()
            ar.off = mark
            ao = [ar.alloc([D], BF16) for _ in range(2)]; B_ao = [k.dbuf(f"ao{j}") for j in range(2)]
            aT = [ar.alloc([KC, 128], BF16) for _ in range(2)]; B_aT = [Buf(f"aT{j}") for j in range(2)]
            xt = [ar.alloc([D], F32) for _ in range(2)]; B_x = [k.dbuf(f"wx{j}") for j in range(2)]
            tmp = [ar.alloc([D], F32) for _ in range(2)]; B_tmp = [Buf(f"wtmp{j}") for j in range(2)]
            xb = [ar.alloc([D], F32) for _ in range(2)]; B_xb = [k.dbuf(f"wxb{j}") for j in range(2)]
            NB = (D + 511) // 512
            def stageA(it):
                i = tiles[it]
                j = it % 2
                k.dma("sp", ao[j], AO[i * 128:(i + 1) * 128, :], writes=[B_ao[j]])
                k.dma("sp", xt[j], src_tile(l, i), writes=[B_x[j]])
                if xblend is not None:
                    k.dma("sp", xb[j], src_tile(l, i + xblend), writes=[B_xb[j]])
                    k.op("dve", lambda e: e.tensor_scalar(out=xt[j], in0=xt[j], scalar1=sel[:, 0:1], scalar2=None, op0=ALU.mult), reads=[B_x[j], B_gains], writes=[B_x[j]])
                    k.op("dve", lambda e: e.scalar_tensor_tensor(out=xt[j], in0=xb[j], scalar=sel[:, 1:2], in1=xt[j], op0=ALU.mult, op1=ALU.add), reads=[B_xb[j], B_x[j], B_gains], writes=[B_x[j]])
                tb = j
                ptb = bank(tb).bitcast(BF16)
                for kc in range(KC):
                    k.op("pe", lambda e, kc=kc: e.transpose(out=ptb[:, kc * 128:(kc + 1) * 128], in_=ao[j][:, kc * 128:(kc + 1) * 128], identity=ident),
                         reads=[B_ao[j], B_ident], writes=[PB[tb]])
                k.op("act", lambda e: e.activation(out=aT[j], in_=ptb[:, 0:KC * 128].rearrange("p (a b) -> p a b", a=KC), func=AF.Copy), reads=[PB[tb]], writes=[B_aT[j]])

            stageA(0)
            for it, i in enumerate(tiles):
                j = it % 2
                if it + 1 < len(tiles):
                    stageA(it + 1)
                yb = 2 + 2 * j
                for nb in range(NB):
                    cw = min(512, D - nb * 512)
                    for kc in range(KC):
                        k.op("pe", lambda e, kc=kc, j=j, nb=nb, cw=cw, yb=yb: e.matmul(bank(yb + nb)[:, 0:cw], lhsT=aT[j][:, kc, :], rhs=wsb[:, kc, nb * 512:nb * 512 + cw], start=(kc == 0), stop=(kc == KC - 1)),
                             reads=[B_aT[j], B_w], writes=[PB[yb + nb]])
                    k.op("dve", lambda e, j=j, nb=nb, cw=cw, yb=yb: e.tensor_tensor(out=tmp[j][:, nb * 512:nb * 512 + cw], in0=bank(yb + nb)[:, 0:cw], in1=modrows[:, 2 * D + nb * 512:2 * D + nb * 512 + cw], op=ALU.mult),
                         reads=[PB[yb + nb], B_mod], writes=[B_tmp[j]])
                k.op("pool", lambda e, j=j: e.tensor_tensor(out=xt[j], in0=xt[j], in1=tmp[j], op=ALU.add), reads=[B_tmp[j], B_x[j]], writes=[B_x[j]])
                k.dma("sp", dst[i * 128:(i + 1) * 128, :], xt[j], reads=[B_x[j]])
            k.barrier()

        def ffn_phase(l, tiles, src, dst, final):
            ar.off = PERS
            wg = ar.alloc([KC, FF], BF16); wu = ar.alloc([KC, FF], BF16); wd = ar.alloc([FC, D], BF16)
            B_wg = Buf("wg"); B_wu = Buf("wu"); B_wd = Buf("wd")
            mark = ar.off
            stg = [ar.alloc([max(FF, D)], F32) for _ in range(2)]; B_stg = [k.dbuf(f"fstg{j}") for j in range(2)]
            load_w(wg, w_gate[l], KC, FF, stg, B_stg, B_wg)
            load_w(wu, w_up[l], KC, FF, stg, B_stg, B_wu)
            load_w(wd, w_down[l], FC, D, stg, B_stg, B_wd)
            k.barrier()
            ar.off = mark
            G = 2
            xg = ar.alloc([G, D], F32); B_xg = [k.dbuf(f"fx{j}") for j in range(G)]
            junk = ar.alloc([D], BF16); B_junk = Buf("fjunk")
            tmp = ar.alloc([D], F32); B_tmp = Buf("ftmp")
            hb = ar.alloc([D], BF16); B_hb = Buf("fhb")
            hT = ar.alloc([KC, G * 128], BF16); B_hT = Buf("fhT")
            AT = ar.alloc([FC, G * 128], BF16); B_AT = Buf("fAT")
            sg = [ar.alloc([G * 128], F32) for _ in range(2)]; B_sg = [Buf(f"sg{j}") for j in range(2)]
            ss = [ar.alloc([3], F32) for _ in range(G)]; B_ss = [Buf(f"fss{j}") for j in range(G)]
            fs = [ar.alloc([3], F32) for _ in range(G)]; B_fs = [Buf(f"ffs{j}") for j in range(G)]
            NB = (D + 511) // 512
            groups = [tiles[a:a + G] for a in range(0, len(tiles), G)]
            fi = 0
            for grp in groups:
                ng = len(grp)
                NQ = ng * 128
                for gi, i in enumerate(grp):
                    k.dma("sp", xg[:, gi, :], src[i * 128:(i + 1) * 128, :], writes=[B_xg[gi]])
                    norm_mod_T(xg[:, gi, :], B_xg[gi], 4 * D, 3 * D, junk, B_junk, ss[gi], B_ss[gi], tmp, B_tmp, hb, B_hb, hT[:, :, gi * 128:(gi + 1) * 128], B_hT, 0)
                for f in range(FC):
                    gb = 1 + fi % 2; ub = 3 + fi % 2; sj = fi % 2; fi += 1
                    for kc in range(KC):
                        k.op("pe", lambda e, kc=kc, f=f, gb=gb: e.matmul(bank(gb)[:, 0:NQ], lhsT=wg[:, kc, f * 128:(f + 1) * 128], rhs=hT[:, kc, 0:NQ], start=(kc == 0), stop=(kc == KC - 1)),
                             reads=[B_wg, B_hT], writes=[PB[gb]])
                    for kc in range(KC):
                        k.op("pe", lambda e, kc=kc, f=f, ub=ub: e.matmul(bank(ub)[:, 0:NQ], lhsT=wu[:, kc, f * 128:(f + 1) * 128], rhs=hT[:, kc, 0:NQ], start=(kc == 0), stop=(kc == KC - 1)),
                             reads=[B_wu, B_hT], writes=[PB[ub]])
                    k.op("act", lambda e, gb=gb, sj=sj: e.activation(out=sg[sj][:, 0:NQ], in_=bank(gb)[:, 0:NQ], func=AF.Silu), reads=[PB[gb]], writes=[B_sg[sj]])
                    k.op("dve", lambda e, ub=ub, sj=sj, f=f: e.tensor_tensor(out=AT[:, f, 0:NQ], in0=bank(ub)[:, 0:NQ], in1=sg[sj][:, 0:NQ], op=ALU.mult), reads=[PB[ub], B_sg[sj]], writes=[B_AT])
                for gi, i in enumerate(grp):
                    for nb in range(NB):
                        cw = min(512, D - nb * 512)
                        yb = 5 + nb
                        for f in range(FC):
                            k.op("pe", lambda e, f=f, gi=gi, nb=nb, cw=cw, yb=yb: e.matmul(bank(yb)[:, 0:cw], lhsT=AT[:, f, gi * 128:(gi + 1) * 128], rhs=wd[:, f, nb * 512:nb * 512 + cw], start=(f == 0), stop=(f == FC - 1)),
                                 reads=[B_AT, B_wd], writes=[PB[yb]])
                        k.op("dve", lambda e, nb=nb, cw=cw, yb=yb: e.tensor_tensor(out=tmp[:, nb * 512:nb * 512 + cw], in0=bank(yb)[:, 0:cw], in1=modrows[:, 5 * D + nb * 512:5 * D + nb * 512 + cw], op=ALU.mult),
                             reads=[PB[yb], B_mod], writes=[B_tmp])
                    xv = xg[:, gi, :]
                    k.op("pool", lambda e, xv=xv: e.tensor_tensor(out=xv, in0=xv, in1=tmp, op=ALU.add), reads=[B_tmp, B_xg[gi]], writes=[B_xg[gi]])
                    if final:
                        k.op("act", lambda e, xv=xv, gi=gi: e.activation(out=junk, in_=xv, func=AF.Square, accum_out=fs[gi][:, 0:1]), reads=[B_xg[gi]], writes=[B_junk, B_fs[gi]])
                        rstd_chain(fs[gi], 1.0 / D, B_fs[gi])
                        k.op("dve", lambda e, xv=xv, gi=gi: e.scalar_tensor_tensor(out=xv, in0=xv, scalar=fs[gi][:, 2:3], in1=fg_r, op0=ALU.mult, op1=ALU.mult),
                             reads=[B_xg[gi], B_fs[gi], B_gains], writes=[B_xg[gi]])
                        k.dma("sp", dst[(i - CT) * 128:(i - CT + 1) * 128, :], xv, reads=[B_xg[gi]])
                    else:
                        k.dma("sp", dst[i * 128:(i + 1) * 128, :], xv, reads=[B_xg[gi]])
            k.barrier()

        ctx_tiles = list(range(CT)); x_tiles = list(range(CT, NCH))
        qt_x = [(CTX + a * 512, 512) for a in range(T // 512)]
        all_chunks = list(range(NCH))
        ada_phase(0, 1, 6 * D)
        proj_phase(0, ctx_tiles)
        na_units = [dict(KT=NAKT[p], V=NAV[p], vtot=130, vsl=[(0, 65), (65, 65)], QT=NAQT[p], col=p * 128) for p in range(NAP)]
        g_units = [dict(KT=GKT[(2 * p) // (GQ // GKV)], V=GV[(2 * p) // (GQ // GKV)], vtot=65, vsl=[(0, 65), (0, 65)], QT=GQT[p], col=(NAP + p) * 128) for p in range(GQP)]
        attn_phase(na_units + g_units, [(0, CTX)], list(range(CT)), "gqa")
        wout_phase(0, ctx_tiles, XM)
        ffn_phase(0, ctx_tiles, XM, X1, False)
        ada_phase(0, 0, 6 * D)
        proj_phase(0, x_tiles)
        na_phase()
        attn_phase(g_units, qt_x, all_chunks, "gqa")
        wout_phase(0, x_tiles, XM)
        ffn_phase(0, x_tiles, XM, X1, False)
        ada_phase(1, 1, 2 * D)
        proj_phase(1, ctx_tiles)
        ada_phase(1, 0, 6 * D)
        proj_phase(1, x_tiles)
        d_units = [dict(KT=DKT[h], V=DV[h], vtot=129, vsl=[(0, 129), (0, 129)], QT=DQT[h], col=h * 128) for h in range(DH)]
        if split:
            own_tiles = list(range(CT, CT + NT // 2))
            qt_own = [(CTX + a * 512, 512) for a in range(T // 2 // 512)]
            attn_phase(d_units, qt_own, all_chunks, "diff", qblend=T // 2)
            wout_phase(1, own_tiles, XM, xblend=NT // 2)
            ffn_phase(1, own_tiles, XM, out_d, True)
        else:
            attn_phase(d_units, qt_x, all_chunks, "diff")
            wout_phase(1, x_tiles, XM)
            ffn_phase(1, x_tiles, XM, out_d, True)
        k.emit()
        print("instr counts", {e: len(k.ops[e]) for e in ENGS}, "signalled", k.ncounts, "sems", k.nsem, flush=True)
        print("max dma sem", sorted([(d.count, n) for n, d in k.dpool.items()])[-6:], flush=True)
    return nc


def _cossin_table(cfg):
    T = cfg["ROWS"] * GRID_W; CTX = cfg["CTX"]
    t = np.arange(T)
    row = (t // GRID_W).astype(np.float32); col = (t % GRID_W).astype(np.float32)
    nf = 16
    inv = (np.float32(10000.0) ** (-np.arange(nf, dtype=np.float32) / np.float32(nf))).astype(np.float32)
    ang = np.concatenate([row[:, None] * inv, col[:, None] * inv], axis=-1).astype(np.float32)
    tab = np.zeros((CTX + T, 64), np.float32)
    tab[:CTX, 0:32] = 1.0
    tab[CTX:, 0:32] = np.cos(ang)
    tab[CTX:, 32:64] = np.sin(ang)
    return tab


def _na_bias_table(cfg, rpb):
    ROWS = cfg["ROWS"]; NA = cfg["NA"]
    out = np.full((NA // 2, 128, 5, 2, 5, 128), NEG, np.float32)
    kk = np.arange(128); qq = np.arange(128)
    for cls, r0 in enumerate((4, 0, 2, ROWS - 4, ROWS - 2)):
        csr = min(min(max(r0 - 4, 0), ROWS - 8), ROWS - 10)
        qr = r0 + qq // 64; qc = qq % 64
        rs = np.clip(qr - 4, 0, ROWS - 8); cs = np.clip(qc - 8, 0, GRID_W - 16)
        for j in range(5):
            kr = csr + 2 * j + kk // 64; kc = kk % 64
            valid = ((kr[:, None] >= rs[None, :]) & (kr[:, None] < rs[None, :] + 8) &
                     (kc[:, None] >= cs[None, :]) & (kc[:, None] < cs[None, :] + 16))
            ri = np.clip(kr[:, None] - qr[None, :] + 7, 0, 14); ci = np.clip(kc[:, None] - qc[None, :] + 15, 0, 30)
            for h in range(NA):
                g = rpb[h][ri, ci]
                out[h // 2, :, cls, h % 2, j, :] = np.where(valid, g, np.float32(NEG))
    return out.reshape(NA // 2, 128, 5 * 2 * 5 * 128)


def make_core_inputs(cfg, inp, b):
    D = cfg["D"]; KC = D // 128
    f = lambda a: np.ascontiguousarray(np.asarray(a, dtype=np.float32))
    cvec = np.concatenate([f(inp["c"][b]).reshape(KC, 128).T, f(inp["c_ctx"]).reshape(KC, 128).T], axis=1)
    lamv = np.concatenate([f(inp["diff_lambda_q1"][0]), f(inp["diff_lambda_k1"][0]), f(inp["diff_lambda_q2"][0]), f(inp["diff_lambda_k2"][0])])[None]
    return {
        "x": f(inp["x"][b]), "ctx": f(inp["ctx"][b]), "cvec": f(cvec),
        "ada_w": f(inp["ada_w"]), "ada_b": f(inp["ada_b"]),
        "ffn_w_gate": f(inp["ffn_w_gate"]), "ffn_w_up": f(inp["ffn_w_up"]), "ffn_w_down": f(inp["ffn_w_down"]),
        "par_w_in": f(inp["par_w_in"][0]), "par_w_out": f(inp["par_w_out"][0]),
        "diff_w_in": f(inp["diff_w_in"][0]), "diff_w_out": f(inp["diff_w_out"][0]),
        "gqa_q_gain": f(inp["gqa_q_gain"]).reshape(1, 64), "gqa_k_gain": f(inp["gqa_k_gain"]).reshape(1, 64),
        "lamv": f(lamv), "diff_subln_gain": f(inp["diff_subln_gain"]).reshape(1, 128),
        "final_norm_gain": f(inp["final_norm_gain"]).reshape(1, D),
        "cossin": _cossin_table(cfg), "nabias": _na_bias_table(cfg, f(inp["na_rpb"][0])),
        "ident": np.eye(128, dtype=np.float32).astype(ml_dtypes.bfloat16),
        "sel": np.tile(np.array([[1.0, 0.0]], np.float32), (128, 1)),
    }


def kernel(**inputs):
    cfg = FULL_CFG
    B = inputs["x"].shape[0]
    nc = build_program(cfg)
    shared = None
    in_maps = []
    T = inputs["x"].shape[1]
    for core in range(8):
        b = core % B
        half = core // B
        m = make_core_inputs(cfg, inputs, b) if shared is None else dict(shared)
        if shared is None:
            shared = m
        else:
            f = lambda a: np.ascontiguousarray(np.asarray(a, dtype=np.float32))
            D = cfg["D"]; KC = D // 128
            m["x"] = f(inputs["x"][b]); m["ctx"] = f(inputs["ctx"][b])
            m["cvec"] = f(np.concatenate([f(inputs["c"][b]).reshape(KC, 128).T, f(inputs["c_ctx"]).reshape(KC, 128).T], axis=1))
        m["sel"] = np.tile(np.array([[1.0, 0.0]] if half == 0 else [[0.0, 1.0]], np.float32), (128, 1))
        in_maps.append(m)
    res = run_bass_kernel_spmd(nc, in_maps, core_ids=list(range(8)))
    out = np.empty((B, T, cfg["D"]), np.float32)
    for core in range(8):
        b = core % B
        half = core // B
        out[b, half * (T // 2):(half + 1) * (T // 2)] = np.asarray(res.results[core]["out"], dtype=np.float32)
    return out
```

```python
import math
import numpy as np
import ml_dtypes
from contextlib import ExitStack
import concourse.bass as bass
import concourse.mybir as mybir
from concourse.bass_utils import run_bass_kernel_spmd

F32 = mybir.dt.float32
BF16 = mybir.dt.bfloat16
U8 = mybir.dt.uint8
AF = mybir.ActivationFunctionType
ALU = mybir.AluOpType
AX = mybir.AxisListType

ENGS = ("pe", "act", "dve", "pool", "sp")
EPOCH = 2000
GRID_W = 64
EPS = 1e-6
NEG = -30000.0

FULL_CFG = dict(D=1024, ROWS=128, CTX=256, NA=8, GQ=8, GKV=2, DH=8, FF=2816)


class Buf:
    __slots__ = ("name", "w", "r", "dsem")

    def __init__(self, name, dsem=None):
        self.name = name
        self.w = {}
        self.r = {}
        self.dsem = dsem


class DmaSem:
    __slots__ = ("h", "count")

    def __init__(self, h):
        self.h = h
        self.count = 0


class _Rec:
    def __getattr__(self, name):
        def f(*a, **kw):
            self.call = (name, a, kw)
            return self
        return f


class K:
    def __init__(self, nc, stack):
        self.nc = nc
        self.stack = stack
        self.ops = {e: [] for e in ENGS}
        self.instr = {e: [] for e in ENGS}
        self.known = {e: {} for e in ENGS}
        self.dsems = []
        self.dpool = {}
        self.nsem = 0

    def sem(self, name):
        h = self.stack.enter_context(self.nc.semaphore(name))
        self.nsem += 1
        return h

    def dsem(self, name):
        d = DmaSem(self.sem(name))
        self.dsems.append(d)
        return d

    def dbuf(self, name):
        if name not in self.dpool:
            self.dpool[name] = self.dsem("d_" + name)
        return Buf(name, dsem=self.dpool[name])

    def fence(self, buf):
        self._merge(buf.r, buf.w)

    @staticmethod
    def _merge(deps, d):
        for s, v in d.items():
            if deps.get(s, (None, 0))[1] < v[1]:
                deps[s] = v

    def _deps(self, reads, writes):
        deps = {}
        for b in reads:
            self._merge(deps, b.w)
        for b in writes:
            if b.r:
                self._merge(deps, b.r)
                self._merge(deps, b.w)
        return deps

    def _post(self, reads, writes, key, val):
        for b in writes:
            if b.r:
                b.w = {}
                b.r = {}
            if b.w.get(key, (None, 0))[1] < val[1]:
                b.w[key] = val
        for b in reads:
            if b in writes:
                continue
            if b.r.get(key, (None, 0))[1] < val[1]:
                b.r[key] = val

    def _waits(self, eng, deps):
        ws = []
        kn = self.known[eng]
        for key, (payload, v) in deps.items():
            if kn.get(key, 0) >= v:
                continue
            kn[key] = v
            ws.append((key, payload, v))
            if key[0] == "E":
                self.instr[payload][v - 1]["needed"] = True
        return ws

    @staticmethod
    def _bind(fn):
        rec = _Rec()
        fn(rec)
        return rec.call

    def op(self, eng, fn, reads=(), writes=()):
        deps = self._deps(reads, writes)
        ws = self._waits(eng, deps)
        r = {"call": self._bind(fn), "waits": ws, "needed": False, "dma": None}
        self.ops[eng].append(r)
        self.instr[eng].append(r)
        self._post(reads, writes, ("E", eng), (eng, len(self.instr[eng])))

    def dma(self, eng, out_ap, in_ap, reads=(), writes=()):
        ds = None
        for b in list(writes) + list(reads):
            if b.dsem is not None:
                ds = b.dsem
                break
        assert ds is not None
        deps = self._deps(reads, writes)
        ws = self._waits(eng, deps)
        ds.count += 16
        self.ops[eng].append({"call": ("dma_start", (), {"out": out_ap, "in_": in_ap}), "waits": ws, "needed": False, "dma": ds})
        self._post(reads, writes, ("D", id(ds)), (ds, ds.count))

    def barrier(self):
        evs = {}
        for e in ENGS:
            if self.instr[e]:
                evs[("E", e)] = (e, len(self.instr[e]))
        for d in self.dsems:
            if d.count > 0:
                evs[("D", id(d))] = (d, d.count)
        for e in ENGS:
            ws = self._waits(e, evs)
            if ws:
                self.ops[e].append({"call": None, "waits": ws, "needed": False, "dma": None})

    def emit(self):
        nc = self.nc
        esems = {}
        for e in ENGS:
            c = 0
            for r in self.instr[e]:
                if r["needed"]:
                    c += 1
                    r["count"] = c
            esems[e] = [self.sem(f"e_{e}_{j}") for j in range((c + EPOCH - 1) // EPOCH)]
        self.ncounts = {e: sum(1 for r in self.instr[e] if r["needed"]) for e in ENGS}

        def resolve(w):
            key, payload, v = w
            if key[0] == "D":
                return payload.h, v
            c = self.instr[payload][v - 1]["count"]
            return esems[payload][(c - 1) // EPOCH], (c - 1) % EPOCH + 1

        with nc.Block() as block:
            def run(e, eng):
                for r in self.ops[eng]:
                    for w in r["waits"]:
                        h, v = resolve(w)
                        e.wait_ge(h, v)
                    if r["call"] is None:
                        continue
                    name, a, kw = r["call"]
                    ins = getattr(e, name)(*a, **kw)
                    if r["dma"] is not None:
                        ins.then_inc(r["dma"].h, 16)
                    elif r["needed"]:
                        c = r["count"]
                        ins.then_inc(esems[eng][(c - 1) // EPOCH], 1)

            @block.tensor
            def _(e):
                run(e, "pe")

            @block.scalar
            def _(e):
                run(e, "act")

            @block.vector
            def _(e):
                run(e, "dve")

            @block.gpsimd
            def _(e):
                run(e, "pool")

            @block.sync
            def _(e):
                run(e, "sp")


class Arena:
    def __init__(self, ap, size):
        self.ap = ap
        self.size = size
        self.off = 0

    def alloc(self, shape, dt):
        esz = 4 if dt == F32 else 2
        n = int(np.prod(shape)) * esz
        n_al = (n + 63) // 64 * 64
        assert self.off + n_al <= self.size, f"arena overflow {self.off}+{n_al}>{self.size}"
        a = self.ap[:, self.off:self.off + n].bitcast(dt)
        self.off += n_al
        if len(shape) == 2:
            a = a.rearrange("p (a b) -> p a b", a=shape[0])
        elif len(shape) == 3:
            a = a.rearrange("p (a b c) -> p a b c", a=shape[0], b=shape[1])
        return a


def build_program(cfg, debug=False, split=True):
    D = cfg["D"]; KC = D // 128; ROWS = cfg["ROWS"]; T = ROWS * GRID_W; CTX = cfg["CTX"]
    NA = cfg["NA"]; GQ = cfg["GQ"]; GKV = cfg["GKV"]; DH = cfg["DH"]; FF = cfg["FF"]; FC = FF // 128
    NKEY = CTX + T; NCH = NKEY // 128; CT = CTX // 128; NT = T // 128
    NAP = NA // 2; GQP = GQ // 2
    W0 = (3 * NA + GQ + 2 * GKV) * 64
    W1 = 3 * DH * 128
    NCLS = 5
    lam_init = 0.8 - 0.6 * math.exp(-0.3 * 1)

    nc = bass.Bass("TRN2", target_bir_lowering=False)

    def din(name, shape, dt=F32):
        return nc.dram_tensor(name, list(shape), dt, kind="ExternalInput").ap()

    def dscr(name, shape, dt):
        return nc.dram_tensor(name, list(shape), dt, kind="ExternalOutput" if debug else "Internal").ap()

    x_in = din("x", [T, D]); ctx_in = din("ctx", [CTX, D])
    cvec = din("cvec", [128, 2 * KC])
    ada_w = din("ada_w", [2, D, 6 * D]); ada_b = din("ada_b", [2, 6 * D])
    w_gate = din("ffn_w_gate", [2, D, FF]); w_up = din("ffn_w_up", [2, D, FF]); w_down = din("ffn_w_down", [2, FF, D])
    w_in0 = din("par_w_in", [D, W0]); w_out0 = din("par_w_out", [D, D])
    w_in1 = din("diff_w_in", [D, W1]); w_out1 = din("diff_w_out", [D, D])
    qgain = din("gqa_q_gain", [1, 64]); kgain = din("gqa_k_gain", [1, 64])
    lamv = din("lamv", [1, 256]); subln = din("diff_subln_gain", [1, 128]); fgain = din("final_norm_gain", [1, D])
    cossin = din("cossin", [NKEY, 64])
    nabias = din("nabias", [NAP, 128, NCLS * 2 * 5 * 128])
    ident_in = din("ident", [128, 128], BF16)
    sel_in = din("sel", [128, 2])
    TO = T // 2 if split else T
    out_d = nc.dram_tensor("out", [TO, D], F32, kind="ExternalOutput").ap()

    XM = dscr("XM", [NKEY, D], F32); X1 = dscr("X1", [NKEY, D], F32)
    AO = dscr("AO", [NKEY, D], BF16)
    NAQT = dscr("NAQT", [NAP, 128, NKEY], BF16); NAKT = dscr("NAKT", [NAP, 128, NKEY], BF16)
    NAV = dscr("NAV", [NAP, 128, NCH * 130], BF16)
    GQT = dscr("GQT", [GQP, 128, NKEY], BF16); GKT = dscr("GKT", [GKV, 128, NKEY], BF16)
    GV = dscr("GV", [GKV, 128, NCH * 65], BF16)
    DQT = dscr("DQT", [DH, 128, NKEY], BF16); DKT = dscr("DKT", [DH, 128, NKEY], BF16)
    DV = dscr("DV", [DH, 128, NCH * 129], BF16)

    with ExitStack() as st:
        k = K(nc, st)
        ARENA = 204 * 1024
        arena_t = st.enter_context(nc.sbuf_tensor("arena", [128, ARENA], U8))
        ar = Arena(arena_t[:, :], ARENA)
        ps = st.enter_context(nc.psum_tensor("ps", [128, 4096], F32))

        def bank(i, n=1):
            return ps[:, i * 512:(i + n) * 512]

        PB = [Buf(f"psb{i}") for i in range(8)]

        ident = ar.alloc([128], BF16); B_ident = k.dbuf("ident")
        modrows = ar.alloc([6 * D], F32); B_mod = Buf("mod")
        csil = ar.alloc([2 * KC], F32); B_csil = k.dbuf("csil")
        ones_f = ar.alloc([128], F32); B_ones = Buf("ones")
        qg_r = ar.alloc([64], F32); kg_r = ar.alloc([64], F32); B_gains = k.dbuf("gains")
        lam_r = ar.alloc([256], F32); sub_r = ar.alloc([128], F32); fg_r = ar.alloc([D], F32)
        lam_s = ar.alloc([8], F32); B_lam = Buf("lam")
        sel = ar.alloc([2], F32)
        PERS = ar.off

        k.dma("sp", ident, ident_in, writes=[B_ident])
        k.dma("sp", csil, cvec, writes=[B_csil])
        k.dma("sp", qg_r, qgain.partition_broadcast(128), writes=[B_gains])
        k.dma("sp", kg_r, kgain.partition_broadcast(128), writes=[B_gains])
        k.dma("sp", lam_r, lamv.partition_broadcast(128), writes=[B_gains])
        k.dma("sp", sub_r, subln.partition_broadcast(128), writes=[B_gains])
        k.dma("sp", fg_r, fgain.partition_broadcast(128), writes=[B_gains])
        k.dma("sp", sel, sel_in, writes=[B_gains])
        k.op("dve", lambda e: e.memset(ones_f, 1.0), writes=[B_ones])
        k.op("act", lambda e: e.activation(out=csil, in_=csil, func=AF.Silu), reads=[B_csil], writes=[B_csil])
        lamtmp = ar.alloc([128], F32)
        PERS = ar.off
        k.op("dve", lambda e: e.tensor_tensor(out=lamtmp[:, 0:64], in0=lam_r[:, 0:64], in1=lam_r[:, 64:128], op=ALU.mult), reads=[B_gains], writes=[B_lam])
        k.op("dve", lambda e: e.tensor_tensor(out=lamtmp[:, 64:128], in0=lam_r[:, 128:192], in1=lam_r[:, 192:256], op=ALU.mult), reads=[B_gains], writes=[B_lam])
        k.op("dve", lambda e: e.tensor_reduce(out=lam_s[:, 0:2], in_=lamtmp.rearrange("p (a b) -> p a b", a=2), axis=AX.X, op=ALU.add), reads=[B_lam], writes=[B_lam])
        k.op("act", lambda e: e.activation(out=lam_s[:, 2:4], in_=lam_s[:, 0:2], func=AF.Exp), reads=[B_lam], writes=[B_lam])
        k.op("dve", lambda e: e.tensor_tensor(out=lam_s[:, 4:5], in0=lam_s[:, 3:4], in1=lam_s[:, 2:3], op=ALU.subtract), reads=[B_lam], writes=[B_lam])
        k.op("dve", lambda e: e.tensor_scalar(out=lam_s[:, 5:6], in0=lam_s[:, 4:5], scalar1=-lam_init, scalar2=None, op0=ALU.add), reads=[B_lam], writes=[B_lam])
        neglam = lam_s[:, 5:6]
        k.op("dve", lambda e: e.tensor_scalar(out=sub_r, in0=sub_r, scalar1=(1.0 - lam_init), scalar2=None, op0=ALU.mult), reads=[B_gains], writes=[B_gains])

        def cast_copy(i, out, in_, reads, writes):
            eng = ("pool", "dve", "act")[i % 3] if True else "dve"
            if eng == "act":
                k.op("act", lambda e: e.activation(out=out, in_=in_, func=AF.Copy), reads=reads, writes=writes)
            else:
                k.op(eng, lambda e: e.tensor_copy(out=out, in_=in_), reads=reads, writes=writes)

        def load_w(dst, src, kcs, n, stg, B_stg, B_dst):
            for kc in range(kcs):
                j = kc % 2
                k.dma("sp", stg[j][:, 0:n], src[kc * 128:(kc + 1) * 128, :], writes=[B_stg[j]])
                cast_copy(kc, dst[:, kc, :], stg[j][:, 0:n], [B_stg[j]], [B_dst])

        def ada_phase(l, cond, ncols):
            ar.off = PERS
            rep = ar.alloc([KC, 128], F32); B_rep = Buf("rep")
            wst = [ar.alloc([KC, 512], F32) for _ in range(2)]; B_wst = [k.dbuf(f"adaw{j}") for j in range(2)]
            bst = [ar.alloc([512], F32) for _ in range(2)]; B_bst = [k.dbuf(f"adab{j}") for j in range(2)]
            for kc in range(KC):
                k.op("dve", lambda e, kc=kc: e.tensor_scalar(out=rep[:, kc, :], in0=ones_f, scalar1=csil[:, cond * KC + kc:cond * KC + kc + 1], scalar2=None, op0=ALU.mult),
                     reads=[B_ones, B_csil], writes=[B_rep])
            nb = ncols // 512
            for n in range(nb):
                j = n % 2
                k.dma("sp", wst[j], ada_w[l, :, n * 512:(n + 1) * 512].rearrange("(a p) n -> p a n", p=128), writes=[B_wst[j]])
                k.dma("sp", bst[j], ada_b[l:l + 1, n * 512:(n + 1) * 512].partition_broadcast(128), writes=[B_bst[j]])
                pb = n % 2
                for kc in range(KC):
                    k.op("pe", lambda e, kc=kc, j=j, pb=pb: e.matmul(bank(pb), lhsT=rep[:, kc, :], rhs=wst[j][:, kc, :], start=(kc == 0), stop=(kc == KC - 1)),
                         reads=[B_rep, B_wst[j]], writes=[PB[pb]])
                k.op("dve", lambda e, n=n, j=j, pb=pb: e.tensor_tensor(out=modrows[:, n * 512:(n + 1) * 512], in0=bank(pb), in1=bst[j], op=ALU.add),
                     reads=[PB[pb], B_bst[j]], writes=[B_mod])
            for off in (D, 4 * D):
                if off < ncols:
                    k.op("dve", lambda e, off=off: e.tensor_scalar(out=modrows[:, off:off + D], in0=modrows[:, off:off + D], scalar1=1.0, scalar2=None, op0=ALU.add),
                         reads=[B_mod], writes=[B_mod])
            k.barrier()

        def rstd_chain(ss, n_inv, nrm_bufs):
            B = nrm_bufs
            w = ss.shape[1] // 3
            k.op("dve", lambda e: e.tensor_scalar(out=ss[:, w:2 * w], in0=ss[:, 0:w], scalar1=n_inv, scalar2=EPS, op0=ALU.mult, op1=ALU.add), reads=[B], writes=[B])
            k.op("act", lambda e: e.activation(out=ss[:, w:2 * w], in_=ss[:, w:2 * w], func=AF.Sqrt), reads=[B], writes=[B])
            k.op("dve", lambda e: e.reciprocal(out=ss[:, 2 * w:3 * w], in_=ss[:, w:2 * w]), reads=[B], writes=[B])

        def norm_mod_T(xt, B_x, sc_off, sh_off, junk, B_junk, ss, B_ss, tmp, B_tmp, hb, B_hb, hT_dst, B_hT, pbank):
            k.op("act", lambda e: e.activation(out=junk, in_=xt, func=AF.Square, accum_out=ss[:, 0:1]), reads=[B_x], writes=[B_junk, B_ss])
            rstd_chain(ss, 1.0 / D, B_ss)
            k.op("dve", lambda e: e.scalar_tensor_tensor(out=tmp, in0=xt, scalar=ss[:, 2:3], in1=modrows[:, sc_off:sc_off + D], op0=ALU.mult, op1=ALU.mult),
                 reads=[B_x, B_ss, B_mod], writes=[B_tmp])
            k.op("pool", lambda e: e.tensor_tensor(out=hb, in0=tmp, in1=modrows[:, sh_off:sh_off + D], op=ALU.add), reads=[B_tmp, B_mod], writes=[B_hb])
            pt = bank(pbank).bitcast(BF16)
            for kc in range(KC):
                k.op("pe", lambda e, kc=kc: e.transpose(out=pt[:, kc * 128:(kc + 1) * 128], in_=hb[:, kc * 128:(kc + 1) * 128], identity=ident),
                     reads=[B_hb, B_ident], writes=[PB[pbank]])
            k.op("act", lambda e: e.activation(out=hT_dst, in_=pt[:, 0:KC * 128].rearrange("p (a b) -> p a b", a=KC), func=AF.Copy), reads=[PB[pbank]], writes=[B_hT])

        def src_tile(l, i):
            if l == 0:
                return ctx_in[i * 128:(i + 1) * 128, :] if i < CT else x_in[(i - CT) * 128:(i - CT + 1) * 128, :]
            return X1[i * 128:(i + 1) * 128, :]

        def proj_phase(l, tiles):
            ar.off = PERS
            WW = W0 if l == 0 else W1
            w_src = w_in0 if l == 0 else w_in1
            wsb = ar.alloc([KC, WW], BF16); B_w = Buf("w_in")
            mark = ar.off
            stg = [ar.alloc([WW], F32) for _ in range(2)]; B_stg = [k.dbuf(f"wstg{j}") for j in range(2)]
            load_w(wsb, w_src, KC, WW, stg, B_stg, B_w)
            k.barrier()
            ar.off = mark
            xt = [ar.alloc([D], F32) for _ in range(2)]; B_x = [k.dbuf(f"px{j}") for j in range(2)]
            cst = [ar.alloc([64], F32) for _ in range(2)]; B_cs = [k.dbuf(f"pcs{j}") for j in range(2)]
            junk = ar.alloc([D], F32); B_junk = Buf("junk")
            ss = [ar.alloc([3], F32) for _ in range(2)]; B_ss = [Buf(f"ss{j}") for j in range(2)]
            tmp = ar.alloc([D], F32); B_tmp = Buf("tmp")
            hb = [ar.alloc([D], BF16) for _ in range(2)]; B_hb = [Buf(f"hb{j}") for j in range(2)]
            hT = [ar.alloc([KC, 128], BF16) for _ in range(2)]; B_hT = [Buf(f"hT{j}") for j in range(2)]
            sq = [ar.alloc([512], F32) for _ in range(2)]; B_sq = [Buf(f"sq{j}") for j in range(2)]
            qn = [ar.alloc([512], F32) for _ in range(2)]; B_qn = [Buf(f"qn{j}") for j in range(2)]
            ra = [ar.alloc([512], F32) for _ in range(2)]; B_ra = [Buf(f"ra{j}") for j in range(2)]
            rb = [ar.alloc([512], F32) for _ in range(2)]; B_rb = [Buf(f"rb{j}") for j in range(2)]
            nss = [ar.alloc([24], F32) for _ in range(2)]; B_nss = [Buf(f"nss{j}") for j in range(2)]
            tm = [ar.alloc([512], BF16) for _ in range(3)]; B_tm = [Buf(f"tm{j}") for j in range(3)]
            stT = [ar.alloc([4, 128], BF16) for _ in range(3)]; B_stT = [k.dbuf(f"stT{j}_{l}{int(tiles[0] < CT)}") for j in range(3)]
            vw = 65 if l == 0 else 129
            nvh = (NA + GKV) if l == 0 else DH
            vst = [ar.alloc([nvh, vw], BF16) for _ in range(2)]; B_vst = [k.dbuf(f"vst{j}_{l}{int(tiles[0] < CT)}") for j in range(2)]
            for j in range(2):
                k.op("pool", lambda e, j=j: e.memset(vst[j], 1.0), writes=[B_vst[j]])
                k.fence(B_vst[j])
            cnt = {"pj": 0, "tp": 0, "pp": 0, "tm": 0, "st": 0}

            def post_qk(pj, nm, s0, dsts, norm_gain, do_rope, csb, B_csb, dup=False):
                w = nm * 64
                src = bank(pj)[:, 0:w]
                srcB = PB[pj]
                pp = cnt["pp"] % 2; cnt["pp"] += 1
                if norm_gain is not None:
                    k.op("act", lambda e: e.activation(out=sq[pp][:, 0:w], in_=src, func=AF.Square), reads=[srcB], writes=[B_sq[pp]])
                    k.op("dve", lambda e: e.tensor_reduce(out=nss[pp][:, 0:nm], in_=sq[pp][:, 0:w].rearrange("p (a b) -> p a b", a=nm), axis=AX.X, op=ALU.add),
                         reads=[B_sq[pp]], writes=[B_nss[pp]])
                    k.op("dve", lambda e: e.tensor_scalar(out=nss[pp][:, 8:8 + nm], in0=nss[pp][:, 0:nm], scalar1=1.0 / 64, scalar2=EPS, op0=ALU.mult, op1=ALU.add), reads=[B_nss[pp]], writes=[B_nss[pp]])
                    k.op("act", lambda e: e.activation(out=nss[pp][:, 8:8 + nm], in_=nss[pp][:, 8:8 + nm], func=AF.Sqrt), reads=[B_nss[pp]], writes=[B_nss[pp]])
                    k.op("dve", lambda e: e.reciprocal(out=nss[pp][:, 16:16 + nm], in_=nss[pp][:, 8:8 + nm]), reads=[B_nss[pp]], writes=[B_nss[pp]])
                    k.op("dve", lambda e: e.tensor_tensor(out=qn[pp][:, 0:w].rearrange("p (a b) -> p a b", a=nm), in0=src.rearrange("p (a b) -> p a b", a=nm),
                                                          in1=nss[pp][:, 16:16 + nm][:, :, None].to_broadcast([128, nm, 64]), op=ALU.mult),
                         reads=[srcB, B_nss[pp]], writes=[B_qn[pp]])
                    k.op("pool", lambda e: e.tensor_tensor(out=qn[pp][:, 0:w].rearrange("p (a b) -> p a b", a=nm), in0=qn[pp][:, 0:w].rearrange("p (a b) -> p a b", a=nm),
                                                           in1=norm_gain[:, None, :].to_broadcast([128, nm, 64]), op=ALU.mult),
                         reads=[B_qn[pp], B_gains], writes=[B_qn[pp]])
                    src = qn[pp][:, 0:w]; srcB = B_qn[pp]
                ti = cnt["tm"] % 3; cnt["tm"] += 1
                if do_rope:
                    s4 = src.rearrange("p (a b c) -> p a b c", a=nm, c=2)
                    cosb = csb[:, 0:32][:, None, :, None].to_broadcast([128, nm, 32, 2])
                    sinb = csb[:, 32:64][:, None, :, None].to_broadcast([128, nm, 32, 2])
                    A = ra[pp][:, 0:w].rearrange("p (a b c) -> p a b c", a=nm, c=2)
                    Bm = rb[pp][:, 0:w].rearrange("p (a b c) -> p a b c", a=nm, c=2)
                    o4 = tm[ti][:, 0:w].rearrange("p (a b c) -> p a b c", a=nm, c=2)
                    k.op("dve", lambda e: e.tensor_tensor(out=A, in0=s4, in1=cosb, op=ALU.mult), reads=[srcB, B_csb], writes=[B_ra[pp]])
                    k.op("dve", lambda e: e.tensor_tensor(out=Bm, in0=s4, in1=sinb, op=ALU.mult), reads=[srcB, B_csb], writes=[B_rb[pp]])
                    k.op("pool", lambda e: e.tensor_tensor(out=o4[:, :, :, 0], in0=A[:, :, :, 0], in1=Bm[:, :, :, 1], op=ALU.subtract), reads=[B_ra[pp], B_rb[pp]], writes=[B_tm[ti]])
                    k.op("pool", lambda e: e.tensor_tensor(out=o4[:, :, :, 1], in0=Bm[:, :, :, 0], in1=A[:, :, :, 1], op=ALU.add), reads=[B_ra[pp], B_rb[pp]], writes=[B_tm[ti]])
                else:
                    k.op("act", lambda e: e.activation(out=tm[ti][:, 0:w], in_=src, func=AF.Copy), reads=[srcB], writes=[B_tm[ti]])
                return dict(ti=ti, nm=nm, w=w, dsts=dsts, dup=dup)

            def post_qk2(stt):
                ti = stt["ti"]; nm = stt["nm"]; w = stt["w"]; dsts = stt["dsts"]; dup = stt["dup"]
                if dup:
                    chunks = [(m * 64, 64) for m in range(nm)]
                else:
                    chunks = [(c * 128, 128) for c in range(w // 128)]
                tb = 4 + cnt["tp"] % 2; cnt["tp"] += 1
                ptb = bank(tb).bitcast(BF16)
                sti = cnt["st"] % 3; cnt["st"] += 1
                for ci, (c0, cw) in enumerate(chunks):
                    if dup:
                        for hlf in range(2):
                            k.op("pe", lambda e, ci=ci, c0=c0, hlf=hlf: e.transpose(out=ptb[hlf * 64:(hlf + 1) * 64, ci * 128:(ci + 1) * 128], in_=tm[ti][:, c0:c0 + 64], identity=ident),
                                 reads=[B_tm[ti], B_ident], writes=[PB[tb]])
                    else:
                        k.op("pe", lambda e, ci=ci, c0=c0: e.transpose(out=ptb[:, ci * 128:(ci + 1) * 128], in_=tm[ti][:, c0:c0 + 128], identity=ident),
                             reads=[B_tm[ti], B_ident], writes=[PB[tb]])
                nch_ = len(chunks)
                k.op("dve", lambda e: e.tensor_copy(out=stT[sti][:, 0:nch_, :], in_=ptb[:, 0:nch_ * 128].rearrange("p (a b) -> p a b", a=nch_)), reads=[PB[tb]], writes=[B_stT[sti]])
                for ci in range(nch_):
                    k.dma("sp", dsts[ci], stT[sti][:, ci, :], reads=[B_stT[sti]])

            def stageA(it):
                i = tiles[it]
                j = it % 2
                s0 = i * 128
                k.dma("sp", xt[j], src_tile(l, i), writes=[B_x[j]])
                k.dma("sp", cst[j], cossin[s0:s0 + 128, :], writes=[B_cs[j]])
                norm_mod_T(xt[j], B_x[j], 1 * D, 0, junk, B_junk, ss[j], B_ss[j], tmp, B_tmp, hb[j], B_hb[j], hT[j], B_hT[j], 0)

            stageA(0)
            for it, i in enumerate(tiles):
                j = it % 2
                s0 = i * 128
                if it + 1 < len(tiles):
                    stageA(it + 1)
                if l == 0:
                    blocks = []
                    c = 0
                    for nm_total, kind in ((NA, "naq"), (NA, "nak"), (NA, "nav"), (GQ, "gq"), (GKV, "gk"), (GKV, "gv")):
                        m0 = 0
                        while m0 < nm_total:
                            nm = min(8, nm_total - m0)
                            blocks.append((kind, m0, nm, c + m0 * 64))
                            m0 += nm
                        c += nm_total * 64
                else:
                    blocks = []
                    for kind, base in (("dq", 0), ("dk", DH * 128), ("dv", 2 * DH * 128)):
                        m0 = 0
                        while m0 < 2 * DH:
                            nm = min(8, 2 * DH - m0)
                            blocks.append((kind, m0, nm, base + m0 * 64))
                            m0 += nm
                vj = it % 2

                def do_p1(blk):
                    kind, m0, nm, c0, pj = blk
                    if kind in ("naq", "nak"):
                        dst = NAQT if kind == "naq" else NAKT
                        return post_qk(pj, nm, s0, [dst[(m0 // 2) + ci, :, s0:s0 + 128] for ci in range(nm // 2)], None, False, None, None)
                    elif kind == "gq":
                        return post_qk(pj, nm, s0, [GQT[(m0 // 2) + ci, :, s0:s0 + 128] for ci in range(nm // 2)], qg_r, True, cst[j], B_cs[j])
                    elif kind == "gk":
                        return post_qk(pj, nm, s0, [GKT[m0 + ci, :, s0:s0 + 128] for ci in range(nm)], kg_r, True, cst[j], B_cs[j], dup=True)
                    elif kind in ("dq", "dk"):
                        dst = DQT if kind == "dq" else DKT
                        return post_qk(pj, nm, s0, [dst[(m0 // 2) + ci, :, s0:s0 + 128] for ci in range(nm // 2)], None, True, cst[j], B_cs[j])
                    elif kind in ("nav", "gv"):
                        h0 = m0 if kind == "nav" else NA + m0
                        k.op("act", lambda e: e.activation(out=vst[vj][:, h0:h0 + nm, 0:64], in_=bank(pj)[:, 0:nm * 64].rearrange("p (a b) -> p a b", a=nm), func=AF.Copy),
                             reads=[PB[pj]], writes=[B_vst[vj]])
                    elif kind == "dv":
                        h0 = m0 // 2
                        k.op("act", lambda e: e.activation(out=vst[vj][:, h0:h0 + nm // 2, 0:128], in_=bank(pj)[:, 0:nm * 64].rearrange("p (a b) -> p a b", a=nm // 2), func=AF.Copy),
                             reads=[PB[pj]], writes=[B_vst[vj]])
                    return None

                q1 = []; q2 = []
                todo = [b for b in blocks if not (b[0] == "dq" and i < CT)]
                for blk in todo + [None, None]:
                    if blk is not None:
                        (kind, m0, nm, c0) = blk
                        w = nm * 64
                        pj = 1 + cnt["pj"] % 3; cnt["pj"] += 1
                        for kc in range(KC):
                            k.op("pe", lambda e, kc=kc: e.matmul(bank(pj)[:, 0:w], lhsT=hT[j][:, kc, :], rhs=wsb[:, kc, c0:c0 + w], start=(kc == 0), stop=(kc == KC - 1)),
                                 reads=[B_hT[j], B_w], writes=[PB[pj]])
                    st2 = q2.pop(0) if q2 else None
                    if q1:
                        stt = do_p1(q1.pop(0))
                        if stt is not None:
                            q2.append(stt)
                    if st2 is not None:
                        post_qk2(st2)
                    if blk is not None:
                        q1.append((kind, m0, nm, c0, pj))
                while q1 or q2:
                    st2 = q2.pop(0) if q2 else None
                    if q1:
                        stt = do_p1(q1.pop(0))
                        if stt is not None:
                            q2.append(stt)
                    if st2 is not None:
                        post_qk2(st2)
                if l == 0:
                    for p in range(NAP):
                        k.dma("sp", NAV[p, :, i * 130:(i + 1) * 130], vst[vj][:, 2 * p:2 * p + 2, :].rearrange("p a b -> p (a b)"), reads=[B_vst[vj]])
                    for g in range(GKV):
                        k.dma("sp", GV[g, :, i * 65:(i + 1) * 65], vst[vj][:, NA + g, :], reads=[B_vst[vj]])
                else:
                    for h in range(DH):
                        k.dma("sp", DV[h, :, i * 129:(i + 1) * 129], vst[vj][:, h, :], reads=[B_vst[vj]])
            k.barrier()

        def attn_phase(units, qtiles, chunks, finish_kind, qblend=None):
            ar.off = PERS
            vtot_max = max(u["vtot"] for u in units)
            KTs = [ar.alloc([NKEY], BF16) for _ in range(2)]; B_KT = [k.dbuf(f"aKT{j}") for j in range(2)]
            Vs = [ar.alloc([NCH * vtot_max], BF16) for _ in range(2)]; B_V = [k.dbuf(f"aV{j}") for j in range(2)]
            QTs = [ar.alloc([512], BF16) for _ in range(2)]; B_QT = [k.dbuf(f"aQT{j}") for j in range(2)]
            QBs = [ar.alloc([512], BF16) for _ in range(2)]; B_QB = [k.dbuf(f"aQB{j}") for j in range(2)]
            QMs = [ar.alloc([512], BF16) for _ in range(2)]; B_QM = [Buf(f"aQM{j}") for j in range(2)]
            Ps = [ar.alloc([2, 512], BF16) for _ in range(3)]; B_P = [Buf(f"aP{j}") for j in range(3)]
            rc = ar.alloc([16], F32); B_rc = Buf("rc")
            stg = [ar.alloc([4, 128], BF16) for _ in range(2)]; B_stg = [k.dbuf(f"aStg{j}") for j in range(2)]
            t1 = [ar.alloc([128], F32) for _ in range(2)]; B_t1 = [Buf(f"t1{j}") for j in range(2)]
            o1 = [ar.alloc([128], F32) for _ in range(2)]; B_o1 = [Buf(f"o1{j}") for j in range(2)]
            jnk = ar.alloc([128], F32); B_jnk = Buf("ajnk")
            ssd = [ar.alloc([3], F32) for _ in range(2)]; B_ssd = [Buf(f"ssd{j}") for j in range(2)]
            nsb = 3 if finish_kind == "gqa" else 2
            B_S = [Buf(f"S{j}") for j in range(nsb)]
            B_O = Buf("O")
            cn = {"q": 0, "s": 0, "p": 0, "st": 0, "d": 0}
            for ui, u in enumerate(units):
                kj = ui % 2
                vt = u["vtot"]
                nck = max(chunks) + 1
                k.dma("sp", KTs[kj][:, 0:nck * 128], u["KT"][:, 0:nck * 128], writes=[B_KT[kj]])
                k.dma("sp", Vs[kj][:, 0:nck * vt], u["V"][:, 0:nck * vt], writes=[B_V[kj]])
                Vv = Vs[kj][:, 0:NCH * vt].rearrange("p (c w) -> p c w", w=vt)
                wmax = max(u["vsl"][0][1], u["vsl"][1][1])
                per_bank = 512 // wmax
                for (s0, nq) in qtiles:
                    nsub = nq // 128
                    qj = cn["q"] % 2; cn["q"] += 1
                    k.dma("sp", QTs[qj][:, 0:nq], u["QT"][:, s0:s0 + nq], writes=[B_QT[qj]])
                    if qblend is not None:
                        k.dma("sp", QBs[qj][:, 0:nq], u["QT"][:, s0 + qblend:s0 + qblend + nq], writes=[B_QB[qj]])
                        k.op("dve", lambda e: e.tensor_scalar(out=QMs[qj][:, 0:nq], in0=QTs[qj][:, 0:nq], scalar1=sel[:, 0:1], scalar2=None, op0=ALU.mult),
                             reads=[B_QT[qj], B_gains], writes=[B_QM[qj]])
                        k.op("dve", lambda e: e.scalar_tensor_tensor(out=QTs[qj][:, 0:nq], in0=QBs[qj][:, 0:nq], scalar=sel[:, 1:2], in1=QMs[qj][:, 0:nq], op0=ALU.mult, op1=ALU.add),
                             reads=[B_QB[qj], B_QM[qj], B_gains], writes=[B_QT[qj]])
                    accs = []
                    for a in range(nsub * 2):
                        b = 2 * nsb + a // per_bank
                        o = (a % per_bank) * wmax
                        accs.append((b, o))

                    def s_mm(c, sj):
                        for m in range(2):
                            k.op("pe", lambda e, c=c, m=m, sj=sj: e.matmul(bank(2 * sj + m)[:, 0:nq], lhsT=KTs[kj][m * 64:(m + 1) * 64, c * 128:(c + 1) * 128],
                                                                           rhs=QTs[qj][m * 64:(m + 1) * 64, 0:nq], start=True, stop=True),
                                 reads=[B_KT[kj], B_QT[qj]], writes=[B_S[sj]])

                    sidx = cn["s"]
                    la = 2
                    for pre in range(min(la, len(chunks))):
                        s_mm(chunks[pre], (sidx + pre) % nsb)
                    for ci, c in enumerate(chunks):
                        sj = (sidx + ci) % nsb
                        pj = cn["p"] % 3; cn["p"] += 1
                        k.op("act", lambda e, sj=sj, pj=pj: e.activation(out=Ps[pj][:, :, 0:nq], in_=bank(2 * sj, 2).rearrange("p (a b) -> p a b", a=2)[:, :, 0:nq], func=AF.Exp, scale=0.125),
                             reads=[B_S[sj]], writes=[B_P[pj]])
                        if ci + la < len(chunks):
                            s_mm(chunks[ci + la], (sidx + ci + la) % nsb)
                        seen = set()
                        for a in range(nsub * 2):
                            uu, m = a // 2, a % 2
                            b, o = accs[a]
                            off, w = u["vsl"][m]
                            first_in_bank = (ci == 0) and (b not in seen)
                            seen.add(b)
                            k.op("pe", lambda e, b=b, o=o, w=w, off=off, m=m, uu=uu, pj=pj, c=c, fib=first_in_bank, last=(ci == len(chunks) - 1):
                                 e.matmul(bank(b)[:, o:o + w], lhsT=Ps[pj][:, m, uu * 128:(uu + 1) * 128], rhs=Vv[:, c, off:off + w], start=fib, stop=last, skip_group_check=True),
                                 reads=[B_P[pj], B_V[kj]], writes=[B_O])
                    cn["s"] += len(chunks)
                    sti = cn["st"] % 2; cn["st"] += 1
                    for uu in range(nsub):
                        (b0, o0), (b1, o1_) = accs[2 * uu], accs[2 * uu + 1]
                        w0 = u["vsl"][0][1]; w1 = u["vsl"][1][1]
                        k.op("dve", lambda e, b0=b0, o0=o0, w0=w0, uu=uu: e.reciprocal(out=rc[:, 2 * uu:2 * uu + 1], in_=bank(b0)[:, o0 + w0 - 1:o0 + w0]), reads=[B_O], writes=[B_rc])
                        k.op("dve", lambda e, b1=b1, o1_=o1_, w1=w1, uu=uu: e.reciprocal(out=rc[:, 2 * uu + 1:2 * uu + 2], in_=bank(b1)[:, o1_ + w1 - 1:o1_ + w1]), reads=[B_O], writes=[B_rc])
                        if finish_kind == "gqa":
                            k.op("dve", lambda e, b0=b0, o0=o0, uu=uu, sti=sti: e.tensor_scalar(out=stg[sti][:, uu, 0:64], in0=bank(b0)[:, o0:o0 + 64], scalar1=rc[:, 2 * uu:2 * uu + 1], scalar2=None, op0=ALU.mult),
                                 reads=[B_O, B_rc], writes=[B_stg[sti]])
                            k.op("dve", lambda e, b1=b1, o1_=o1_, uu=uu, sti=sti: e.tensor_scalar(out=stg[sti][:, uu, 64:128], in0=bank(b1)[:, o1_:o1_ + 64], scalar1=rc[:, 2 * uu + 1:2 * uu + 2], scalar2=None, op0=ALU.mult),
                                 reads=[B_O, B_rc], writes=[B_stg[sti]])
                        else:
                            dj = cn["d"] % 2; cn["d"] += 1
                            k.op("dve", lambda e, uu=uu: e.tensor_tensor(out=rc[:, 8 + uu:9 + uu], in0=rc[:, 2 * uu + 1:2 * uu + 2], in1=neglam, op=ALU.mult), reads=[B_rc, B_lam], writes=[B_rc])
                            k.op("dve", lambda e, b0=b0, o0=o0, uu=uu, dj=dj: e.tensor_scalar(out=t1[dj], in0=bank(b0)[:, o0:o0 + 128], scalar1=rc[:, 2 * uu:2 * uu + 1], scalar2=None, op0=ALU.mult),
                                 reads=[B_O, B_rc], writes=[B_t1[dj]])
                            k.op("dve", lambda e, b1=b1, o1_=o1_, uu=uu, dj=dj: e.scalar_tensor_tensor(out=o1[dj], in0=bank(b1)[:, o1_:o1_ + 128], scalar=rc[:, 8 + uu:9 + uu], in1=t1[dj], op0=ALU.mult, op1=ALU.add),
                                 reads=[B_O, B_rc, B_t1[dj]], writes=[B_o1[dj]])
                            k.op("act", lambda e, dj=dj: e.activation(out=jnk, in_=o1[dj], func=AF.Square, accum_out=ssd[dj][:, 0:1]), reads=[B_o1[dj]], writes=[B_jnk, B_ssd[dj]])
                            rstd_chain(ssd[dj], 1.0 / 128, B_ssd[dj])
                            k.op("dve", lambda e, uu=uu, dj=dj, sti=sti: e.scalar_tensor_tensor(out=stg[sti][:, uu, :], in0=o1[dj], scalar=ssd[dj][:, 2:3], in1=sub_r, op0=ALU.mult, op1=ALU.mult),
                                 reads=[B_o1[dj], B_ssd[dj], B_gains], writes=[B_stg[sti]])
                    k.dma("sp", AO[s0:s0 + nq, u["col"]:u["col"] + 128].rearrange("(u p) c -> p u c", p=128), stg[sti][:, 0:nsub, :], reads=[B_stg[sti]])
            k.barrier()

        def na_phase():
            ar.off = PERS
            KTs = [ar.alloc([NKEY], BF16) for _ in range(2)]; B_KT = [k.dbuf(f"nKT{j}") for j in range(2)]
            Vs = [ar.alloc([NCH, 130], BF16) for _ in range(2)]; B_V = [k.dbuf(f"nV{j}") for j in range(2)]
            QTs = [ar.alloc([T], BF16) for _ in range(2)]; B_QT = [k.dbuf(f"nQT{j}") for j in range(2)]
            bst = ar.alloc([NCLS * 2 * 5 * 128], F32); B_bst = k.dbuf("nbst")
            Em = [ar.alloc([NCLS, 2, 640], BF16) for _ in range(2)]; B_E = [Buf(f"nE{j}") for j in range(2)]
            Ps = [ar.alloc([7 * 128], BF16) for _ in range(3)]; B_P = [Buf(f"nP{j}") for j in range(3)]
            rc = [ar.alloc([2], F32) for _ in range(2)]; B_rc = [Buf(f"nrc{j}") for j in range(2)]
            stg = [ar.alloc([4, 128], BF16) for _ in range(2)]; B_stg = [k.dbuf(f"nStg{j}") for j in range(2)]
            B_S = [Buf(f"nS{j}") for j in range(3)]
            B_O = [Buf(f"nO{j}") for j in range(2)]
            cn = {"s": 0, "p": 0, "o": 0}
            for p in range(NAP):
                kj = p % 2
                k.dma("sp", KTs[kj], NAKT[p], writes=[B_KT[kj]])
                k.dma("sp", Vs[kj].rearrange("p a b -> p (a b)"), NAV[p], writes=[B_V[kj]])
                k.dma("sp", QTs[kj], NAQT[p, :, CTX:NKEY], writes=[B_QT[kj]])
                k.dma("sp", bst, nabias[p], writes=[B_bst])
                k.op("act", lambda e, kj=kj: e.activation(out=Em[kj].rearrange("p a b c -> p (a b c)"), in_=bst, func=AF.Exp), reads=[B_bst], writes=[B_E[kj]])
                def unit_info(u):
                    r0 = 2 * u
                    csr = min(min(max(r0 - 4, 0), ROWS - 8), ROWS - 10)
                    cls = {0: 1, 2: 2, ROWS - 4: 3, ROWS - 2: 4}.get(r0, 0)
                    cidx = [CT + csr // 2 + j for j in range(5)] + list(range(CT))
                    return cls, cidx

                units_ = [(u, m) for u in range(NT) for m in range(2)]
                sbase = cn["s"]

                def s_stage(ix):
                    u, m = units_[ix]
                    cls, cidx = unit_info(u)
                    sj = (sbase + ix) % 3
                    Sb = bank(2 * sj, 2)
                    for j, c in enumerate(cidx):
                        k.op("pe", lambda e, j=j, c=c: e.matmul(Sb[:, j * 128:(j + 1) * 128], lhsT=KTs[kj][m * 64:(m + 1) * 64, c * 128:(c + 1) * 128],
                                                             rhs=QTs[kj][m * 64:(m + 1) * 64, u * 128:(u + 1) * 128], start=True, stop=True),
                             reads=[B_KT[kj], B_QT[kj]], writes=[B_S[sj]])

                s_stage(0)
                for ix, (u, m) in enumerate(units_):
                    cls, cidx = unit_info(u)
                    sti = (u // 4) % 2
                    sj = (sbase + ix) % 3
                    Sb = bank(2 * sj, 2)
                    if ix + 1 < len(units_):
                        s_stage(ix + 1)
                    pj = cn["p"] % 3; cn["p"] += 1
                    oj = cn["o"] % 2; cn["o"] += 1
                    k.op("act", lambda e: e.activation(out=Ps[pj], in_=Sb[:, 0:7 * 128], func=AF.Exp, scale=0.125), reads=[B_S[sj]], writes=[B_P[pj]])
                    k.op("dve", lambda e: e.tensor_tensor(out=Ps[pj][:, 0:640], in0=Ps[pj][:, 0:640], in1=Em[kj][:, cls, m, :], op=ALU.mult),
                         reads=[B_P[pj], B_E[kj]], writes=[B_P[pj]])
                    Ob = bank(6 + oj)
                    for j, c in enumerate(cidx):
                        k.op("pe", lambda e, j=j, c=c: e.matmul(Ob[:, 0:65], lhsT=Ps[pj][:, j * 128:(j + 1) * 128], rhs=Vs[kj][:, c, m * 65:(m + 1) * 65], start=(j == 0), stop=(j == 6)),
                             reads=[B_P[pj], B_V[kj]], writes=[B_O[oj]])
                    k.op("dve", lambda e: e.reciprocal(out=rc[oj][:, 0:1], in_=Ob[:, 64:65]), reads=[B_O[oj]], writes=[B_rc[oj]])
                    k.op("dve", lambda e: e.tensor_scalar(out=stg[sti][:, u % 4, m * 64:(m + 1) * 64], in0=Ob[:, 0:64], scalar1=rc[oj][:, 0:1], scalar2=None, op0=ALU.mult),
                         reads=[B_O[oj], B_rc[oj]], writes=[B_stg[sti]])
                    if m == 1 and u % 4 == 3:
                        s0 = CTX + (u - 3) * 128
                        k.dma("sp", AO[s0:s0 + 512, p * 128:(p + 1) * 128].rearrange("(u p) c -> p u c", p=128), stg[sti], reads=[B_stg[sti]])
                cn["s"] = sbase + len(units_)
            k.barrier()

        def wout_phase(l, tiles, dst, xblend=None):
            ar.off = PERS
            w_src = w_out0 if l == 0 else w_out1
            wsb = ar.alloc([KC, D], BF16); B_w = Buf("w_out")
            mark = ar.off
            stg = [ar.alloc([D], F32) for _ in range(2)]; B_stg = [k.dbuf(f"wostg{j}") for j in range(2)]
            load_w(wsb, w_src, KC, D, stg, B_stg, B_w)
            k.barrier()
            ar.off = mark
            ao = [ar.alloc([D], BF16) for _ in range(2)]; B_ao = [k.dbuf(f"ao{j}") for j in range(2)]
            aT = [ar.alloc([KC, 128], BF16) for _ in range(2)]; B_aT = [Buf(f"aT{j}") for j in range(2)]
            xt = [ar.alloc([D], F32) for _ in range(2)]; B_x = [k.dbuf(f"wx{j}") for j in range(2)]
            tmp = [ar.alloc([D], F32) for _ in range(2)]; B_tmp = [Buf(f"wtmp{j}") for j in range(2)]
            xb = [ar.alloc([D], F32) for _ in range(2)]; B_xb = [k.dbuf(f"wxb{j}") for j in range(2)]
            NB = (D + 511) // 512
            def stageA(it):
                i = tiles[it]
                j = it % 2
                k.dma("sp", ao[j], AO[i * 128:(i + 1) * 128, :], writes=[B_ao[j]])
                k.dma("sp", xt[j], src_tile(l, i), writes=[B_x[j]])
                if xblend is not None:
                    k.dma("sp", xb[j], src_tile(l, i + xblend), writes=[B_xb[j]])
                    k.op("dve", lambda e: e.tensor_scalar(out=xt[j], in0=xt[j], scalar1=sel[:, 0:1], scalar2=None, op0=ALU.mult), reads=[B_x[j], B_gains], writes=[B_x[j]])
                    k.op("dve", lambda e: e.scalar_tensor_tensor(out=xt[j], in0=xb[j], scalar=sel[:, 1:2], in1=xt[j], op0=ALU.mult, op1=ALU.add), reads=[B_xb[j], B_x[j], B_gains], writes=[B_x[j]])
                tb = j
                ptb = bank(tb).bitcast(BF16)
                for kc in range(KC):
                    k.op("pe", lambda e, kc=kc: e.transpose(out=ptb[:, kc * 128:(kc + 1) * 128], in_=ao[j][:, kc * 128:(kc + 1) * 128], identity=ident),
                         reads=[B_ao[j], B_ident], writes=[PB[tb]])
                k.op("act", lambda e: e.activation(out=aT[j], in_=ptb[:, 0:KC * 128].rearrange("p (a b) -> p a b", a=KC), func=AF.Copy), reads=[PB[tb]], writes=[B_aT[j]])

            stageA(0)
            for it, i in enumerate(tiles):
                j = it % 2
                if it + 1 < len(tiles):
                    stageA(it + 1)
                yb = 2 + 2 * j
                for nb in range(NB):
                    cw = min(512, D - nb * 512)
                    for kc in range(KC):
                        k.op("pe", lambda e, kc=kc, j=j, nb=nb, cw=cw, yb=yb: e.matmul(bank(yb + nb)[:, 0:cw], lhsT=aT[j][:, kc, :], rhs=wsb[:, kc, nb * 512:nb * 512 + cw], start=(kc == 0), stop=(kc == KC - 1)),
                             reads=[B_aT[j], B_w], writes=[PB[yb + nb]])
                    k.op("dve", lambda e, j=j, nb=nb, cw=cw, yb=yb: e.tensor_tensor(out=tmp[j][:, nb * 512:nb * 512 + cw], in0=bank(yb + nb)[:, 0:cw], in1=modrows[:, 2 * D + nb * 512:2 * D + nb * 512 + cw], op=ALU.mult),
                         reads=[PB[yb + nb], B_mod], writes=[B_tmp[j]])
                k.op("pool", lambda e, j=j: e.tensor_tensor(out=xt[j], in0=xt[j], in1=tmp[j], op=ALU.add), reads=[B_tmp[j], B_x[j]], writes=[B_x[j]])
                k.dma("sp", dst[i * 128:(i + 1) * 128, :], xt[j], reads=[B_x[j]])
            k.barrier()

        def ffn_phase(l, tiles, src, dst, final):
            ar.off = PERS
            wg = ar.alloc([KC, FF], BF16); wu = ar.alloc([KC, FF], BF16); wd = ar.alloc([FC, D], BF16)
            B_wg = Buf("wg"); B_wu = Buf("wu"); B_wd = Buf("wd")
            mark = ar.off
            stg = [ar.alloc([max(FF, D)], F32) for _ in range(2)]; B_stg = [k.dbuf(f"fstg{j}") for j in range(2)]
            load_w(wg, w_gate[l], KC, FF, stg, B_stg, B_wg)
            load_w(wu, w_up[l], KC, FF, stg, B_stg, B_wu)
            load_w(wd, w_down[l], FC, D, stg, B_stg, B_wd)
            k.barrier()
            ar.off = mark
            G = 2
            xg = ar.alloc([G, D], F32); B_xg = [k.dbuf(f"fx{j}") for j in range(G)]
            junk = ar.alloc([D], BF16); B_junk = Buf("fjunk")
            tmp = ar.alloc([D], F32); B_tmp = Buf("ftmp")
            hb = ar.alloc([D], BF16); B_hb = Buf("fhb")
            hT = ar.alloc([KC, G * 128], BF16); B_hT = Buf("fhT")
            AT = ar.alloc([FC, G * 128], BF16); B_AT = Buf("fAT")
            sg = [ar.alloc([G * 128], F32) for _ in range(2)]; B_sg = [Buf(f"sg{j}") for j in range(2)]
            ss = [ar.alloc([3], F32) for _ in range(G)]; B_ss = [Buf(f"fss{j}") for j in range(G)]
            fs = [ar.alloc([3], F32) for _ in range(G)]; B_fs = [Buf(f"ffs{j}") for j in range(G)]
            NB = (D + 511) // 512
            groups = [tiles[a:a + G] for a in range(0, len(tiles), G)]
            fi = 0
            for grp in groups:
                ng = len(grp)
                NQ = ng * 128
                for gi, i in enumerate(grp):
                    k.dma("sp", xg[:, gi, :], src[i * 128:(i + 1) * 128, :], writes=[B_xg[gi]])
                    norm_mod_T(xg[:, gi, :], B_xg[gi], 4 * D, 3 * D, junk, B_junk, ss[gi], B_ss[gi], tmp, B_tmp, hb, B_hb, hT[:, :, gi * 128:(gi + 1) * 128], B_hT, 0)
                for f in range(FC):
                    gb = 1 + fi % 2; ub = 3 + fi % 2; sj = fi % 2; fi += 1
                    for kc in range(KC):
                        k.op("pe", lambda e, kc=kc, f=f, gb=gb: e.matmul(bank(gb)[:, 0:NQ], lhsT=wg[:, kc, f * 128:(f + 1) * 128], rhs=hT[:, kc, 0:NQ], start=(kc == 0), stop=(kc == KC - 1)),
                             reads=[B_wg, B_hT], writes=[PB[gb]])
                    for kc in range(KC):
                        k.op("pe", lambda e, kc=kc, f=f, ub=ub: e.matmul(bank(ub)[:, 0:NQ], lhsT=wu[:, kc, f * 128:(f + 1) * 128], rhs=hT[:, kc, 0:NQ], start=(kc == 0), stop=(kc == KC - 1)),
                             reads=[B_wu, B_hT], writes=[PB[ub]])
                    k.op("act", lambda e, gb=gb, sj=sj: e.activation(out=sg[sj][:, 0:NQ], in_=bank(gb)[:, 0:NQ], func=AF.Silu), reads=[PB[gb]], writes=[B_sg[sj]])
                    k.op("dve", lambda e, ub=ub, sj=sj, f=f: e.tensor_tensor(out=AT[:, f, 0:NQ], in0=bank(ub)[:, 0:NQ], in1=sg[sj][:, 0:NQ], op=ALU.mult), reads=[PB[ub], B_sg[sj]], writes=[B_AT])
                for gi, i in enumerate(grp):
                    for nb in range(NB):
                        cw = min(512, D - nb * 512)
                        yb = 5 + nb
                        for f in range(FC):
                            k.op("pe", lambda e, f=f, gi=gi, nb=nb, cw=cw, yb=yb: e.matmul(bank(yb)[:, 0:cw], lhsT=AT[:, f, gi * 128:(gi + 1) * 128], rhs=wd[:, f, nb * 512:nb * 512 + cw], start=(f == 0), stop=(f == FC - 1)),
                                 reads=[B_AT, B_wd], writes=[PB[yb]])
                        k.op("dve", lambda e, nb=nb, cw=cw, yb=yb: e.tensor_tensor(out=tmp[:, nb * 512:nb * 512 + cw], in0=bank(yb)[:, 0:cw], in1=modrows[:, 5 * D + nb * 512:5 * D + nb * 512 + cw], op=ALU.mult),
                             reads=[PB[yb], B_mod], writes=[B_tmp])
                    xv = xg[:, gi, :]
                    k.op("pool", lambda e, xv=xv: e.tensor_tensor(out=xv, in0=xv, in1=tmp, op=ALU.add), reads=[B_tmp, B_xg[gi]], writes=[B_xg[gi]])
                    if final:
                        k.op("act", lambda e, xv=xv, gi=gi: e.activation(out=junk, in_=xv, func=AF.Square, accum_out=fs[gi][:, 0:1]), reads=[B_xg[gi]], writes=[B_junk, B_fs[gi]])
                        rstd_chain(fs[gi], 1.0 / D, B_fs[gi])
                        k.op("dve", lambda e, xv=xv, gi=gi: e.scalar_tensor_tensor(out=xv, in0=xv, scalar=fs[gi][:, 2:3], in1=fg_r, op0=ALU.mult, op1=ALU.mult),
                             reads=[B_xg[gi], B_fs[gi], B_gains], writes=[B_xg[gi]])
                        k.dma("sp", dst[(i - CT) * 128:(i - CT + 1) * 128, :], xv, reads=[B_xg[gi]])
                    else:
                        k.dma("sp", dst[i * 128:(i + 1) * 128, :], xv, reads=[B_xg[gi]])
            k.barrier()

        ctx_tiles = list(range(CT)); x_tiles = list(range(CT, NCH))
        qt_x = [(CTX + a * 512, 512) for a in range(T // 512)]
        all_chunks = list(range(NCH))
        ada_phase(0, 1, 6 * D)
        proj_phase(0, ctx_tiles)
        na_units = [dict(KT=NAKT[p], V=NAV[p], vtot=130, vsl=[(0, 65), (65, 65)], QT=NAQT[p], col=p * 128) for p in range(NAP)]
        g_units = [dict(KT=GKT[(2 * p) // (GQ // GKV)], V=GV[(2 * p) // (GQ // GKV)], vtot=65, vsl=[(0, 65), (0, 65)], QT=GQT[p], col=(NAP + p) * 128) for p in range(GQP)]
        attn_phase(na_units + g_units, [(0, CTX)], list(range(CT)), "gqa")
        wout_phase(0, ctx_tiles, XM)
        ffn_phase(0, ctx_tiles, XM, X1, False)
        ada_phase(0, 0, 6 * D)
        proj_phase(0, x_tiles)
        na_phase()
        attn_phase(g_units, qt_x, all_chunks, "gqa")
        wout_phase(0, x_tiles, XM)
        ffn_phase(0, x_tiles, XM, X1, False)
        ada_phase(1, 1, 2 * D)
        proj_phase(1, ctx_tiles)
        ada_phase(1, 0, 6 * D)
        proj_phase(1, x_tiles)
        d_units = [dict(KT=DKT[h], V=DV[h], vtot=129, vsl=[(0, 129), (0, 129)], QT=DQT[h], col=h * 128) for h in range(DH)]
        if split:
            own_tiles = list(range(CT, CT + NT // 2))
            qt_own = [(CTX + a * 512, 512) for a in range(T // 2 // 512)]
            attn_phase(d_units, qt_own, all_chunks, "diff", qblend=T // 2)
            wout_phase(1, own_tiles, XM, xblend=NT // 2)
            ffn_phase(1, own_tiles, XM, out_d, True)
        else:
            attn_phase(d_units, qt_x, all_chunks, "diff")
            wout_phase(1, x_tiles, XM)
            ffn_phase(1, x_tiles, XM, out_d, True)
        k.emit()
        print("instr counts", {e: len(k.ops[e]) for e in ENGS}, "signalled", k.ncounts, "sems", k.nsem, flush=True)
        print("max dma sem", sorted([(d.count, n) for n, d in k.dpool.items()])[-6:], flush=True)
    return nc


def _cossin_table(cfg):
    T = cfg["ROWS"] * GRID_W; CTX = cfg["CTX"]
    t = np.arange(T)
    row = (t // GRID_W).astype(np.float32); col = (t % GRID_W).astype(np.float32)
    nf = 16
    inv = (np.float32(10000.0) ** (-np.arange(nf, dtype=np.float32) / np.float32(nf))).astype(np.float32)
    ang = np.concatenate([row[:, None] * inv, col[:, None] * inv], axis=-1).astype(np.float32)
    tab = np.zeros((CTX + T, 64), np.float32)
    tab[:CTX, 0:32] = 1.0
    tab[CTX:, 0:32] = np.cos(ang)
    tab[CTX:, 32:64] = np.sin(ang)
    return tab


def _na_bias_table(cfg, rpb):
    ROWS = cfg["ROWS"]; NA = cfg["NA"]
    out = np.full((NA // 2, 128, 5, 2, 5, 128), NEG, np.float32)
    kk = np.arange(128); qq = np.arange(128)
    for cls, r0 in enumerate((4, 0, 2, ROWS - 4, ROWS - 2)):
        csr = min(min(max(r0 - 4, 0), ROWS - 8), ROWS - 10)
        qr = r0 + qq // 64; qc = qq % 64
        rs = np.clip(qr - 4, 0, ROWS - 8); cs = np.clip(qc - 8, 0, GRID_W - 16)
        for j in range(5):
            kr = csr + 2 * j + kk // 64; kc = kk % 64
            valid = ((kr[:, None] >= rs[None, :]) & (kr[:, None] < rs[None, :] + 8) &
                     (kc[:, None] >= cs[None, :]) & (kc[:, None] < cs[None, :] + 16))
            ri = np.clip(kr[:, None] - qr[None, :] + 7, 0, 14); ci = np.clip(kc[:, None] - qc[None, :] + 15, 0, 30)
            for h in range(NA):
                g = rpb[h][ri, ci]
                out[h // 2, :, cls, h % 2, j, :] = np.where(valid, g, np.float32(NEG))
    return out.reshape(NA // 2, 128, 5 * 2 * 5 * 128)


def make_core_inputs(cfg, inp, b):
    D = cfg["D"]; KC = D // 128
    f = lambda a: np.ascontiguousarray(np.asarray(a, dtype=np.float32))
    cvec = np.concatenate([f(inp["c"][b]).reshape(KC, 128).T, f(inp["c_ctx"]).reshape(KC, 128).T], axis=1)
    lamv = np.concatenate([f(inp["diff_lambda_q1"][0]), f(inp["diff_lambda_k1"][0]), f(inp["diff_lambda_q2"][0]), f(inp["diff_lambda_k2"][0])])[None]
    return {
        "x": f(inp["x"][b]), "ctx": f(inp["ctx"][b]), "cvec": f(cvec),
        "ada_w": f(inp["ada_w"]), "ada_b": f(inp["ada_b"]),
        "ffn_w_gate": f(inp["ffn_w_gate"]), "ffn_w_up": f(inp["ffn_w_up"]), "ffn_w_down": f(inp["ffn_w_down"]),
        "par_w_in": f(inp["par_w_in"][0]), "par_w_out": f(inp["par_w_out"][0]),
        "diff_w_in": f(inp["diff_w_in"][0]), "diff_w_out": f(inp["diff_w_out"][0]),
        "gqa_q_gain": f(inp["gqa_q_gain"]).reshape(1, 64), "gqa_k_gain": f(inp["gqa_k_gain"]).reshape(1, 64),
        "lamv": f(lamv), "diff_subln_gain": f(inp["diff_subln_gain"]).reshape(1, 128),
        "final_norm_gain": f(inp["final_norm_gain"]).reshape(1, D),
        "cossin": _cossin_table(cfg), "nabias": _na_bias_table(cfg, f(inp["na_rpb"][0])),
        "ident": np.eye(128, dtype=np.float32).astype(ml_dtypes.bfloat16),
        "sel": np.tile(np.array([[1.0, 0.0]], np.float32), (128, 1)),
    }


def kernel(**inputs):
    cfg = FULL_CFG
    B = inputs["x"].shape[0]
    nc = build_program(cfg)
    shared = None
    in_maps = []
    T = inputs["x"].shape[1]
    for core in range(8):
        b = core % B
        half = core // B
        m = make_core_inputs(cfg, inputs, b) if shared is None else dict(shared)
        if shared is None:
            shared = m
        else:
            f = lambda a: np.ascontiguousarray(np.asarray(a, dtype=np.float32))
            D = cfg["D"]; KC = D // 128
            m["x"] = f(inputs["x"][b]); m["ctx"] = f(inputs["ctx"][b])
            m["cvec"] = f(np.concatenate([f(inputs["c"][b]).reshape(KC, 128).T, f(inputs["c_ctx"]).reshape(KC, 128).T], axis=1))
        m["sel"] = np.tile(np.array([[1.0, 0.0]] if half == 0 else [[0.0, 1.0]], np.float32), (128, 1))
        in_maps.append(m)
    res = run_bass_kernel_spmd(nc, in_maps, core_ids=list(range(8)))
    out = np.empty((B, T, cfg["D"]), np.float32)
    for core in range(8):
        b = core % B
        half = core // B
        out[b, half * (T // 2):(half + 1) * (T // 2)] = np.asarray(res.results[core]["out"], dtype=np.float32)
    return out
```
